# Optimizing a Trainium2 kernel written in Bass

```python
import math
import jax, jax.numpy as jnp
from jax import lax
import numpy as np

D_MODEL = 1024
BATCH = 16
SEQ = 2048
DEPTH = 1
DEC_BATCH = 16
DEC_SEQ = 32
PAST_LEN = 2048

CHUNK = 64
N_RET_HEADS = 4
RET_QK_DIM = D_MODEL // N_RET_HEADS
RET_V_DIM = 2 * RET_QK_DIM
RET_QK_WIDTH = N_RET_HEADS * RET_QK_DIM
RET_V_WIDTH = N_RET_HEADS * RET_V_DIM
CONV_WIDTH = D_MODEL
CONV_GROUPS = 8
CONV_K = 3
FFN_CONV_K = 3
D_FF = 2816
ROPE_BASE = 10000.0
LN_EPS = 1e-5
ALPHA = (2.0 * DEPTH) ** 0.25
BETA = (8.0 * DEPTH) ** -0.25
SPLIT_SIZES = (RET_QK_WIDTH, RET_QK_WIDTH, RET_V_WIDTH, RET_V_WIDTH,
               CONV_WIDTH, CONV_WIDTH, CONV_WIDTH, D_MODEL, D_MODEL)
SPLIT_POINTS = tuple(int(s) for s in np.cumsum(SPLIT_SIZES)[:-1])
N_IN = int(sum(SPLIT_SIZES))

kernel_name = "hybrid_retention_shortconv_streaming_step"


def _layer_norm(x, g, b):
    xf = x.astype(jnp.float32)
    mu = jnp.mean(xf, axis=-1, keepdims=True)
    var = jnp.mean(jnp.square(xf - mu), axis=-1, keepdims=True)
    return ((xf - mu) * lax.rsqrt(var + LN_EPS)).astype(x.dtype) * g + b


def _head_norm(o):
    mu = jnp.mean(o, axis=-1, keepdims=True)
    var = jnp.mean(jnp.square(o - mu), axis=-1, keepdims=True)
    return (o - mu) * lax.rsqrt(var + LN_EPS)


def _rotary(x, pos):
    half = x.shape[-1] // 2
    inv_freq = ROPE_BASE ** (-jnp.arange(half, dtype=jnp.float32) / half)
    ang = pos.astype(jnp.float32)[:, None] * inv_freq[None, :]
    cos = jnp.cos(ang)[None, :, None, :]
    sin = jnp.sin(ang)[None, :, None, :]
    x1, x2 = x[..., 0::2], x[..., 1::2]
    return jnp.stack([x1 * cos - x2 * sin, x1 * sin + x2 * cos], axis=-1).reshape(x.shape)


def _ret_log_decay():
    return jnp.log1p(-jnp.exp2(-5.0 - jnp.arange(N_RET_HEADS, dtype=jnp.float32)))


def _retention_block(q, k, v, state, log_g):
    L = q.shape[1]
    idx = jnp.arange(L, dtype=jnp.float32)
    diff = idx[:, None] - idx[None, :]
    decay = jnp.where(diff >= 0.0, jnp.exp(log_g[:, None, None] * jnp.maximum(diff, 0.0)), 0.0)
    scores = jnp.einsum("bqhd,bkhd->bhqk", q, k) * decay[None]
    inner = jnp.einsum("bhqk,bkhe->bqhe", scores, v)
    q_dec = jnp.exp(log_g[None, :] * (idx[:, None] + 1.0))
    cross = jnp.einsum("bqhd,bhde->bqhe", q, state) * q_dec[None, :, :, None]
    k_dec = jnp.exp(log_g[None, :] * (L - 1.0 - idx[:, None]))
    new_state = (jnp.exp(log_g * L)[None, :, None, None] * state
                 + jnp.einsum("bkhd,bkhe->bhde", k * k_dec[None, :, :, None], v))
    return inner + cross, new_state


def _retention_prompt(q, k, v, log_g):
    B, L, H, dk = q.shape
    dv = v.shape[-1]
    nc = L // CHUNK

    def to_chunks(t):
        return jnp.swapaxes(t.reshape(B, nc, CHUNK, H, t.shape[-1]), 0, 1)

    def step(S, qkv):
        qc, kc, vc = qkv
        o, S = _retention_block(qc, kc, vc, S, log_g)
        return S, o

    s0 = jnp.zeros((B, H, dk, dv), jnp.float32)
    s_fin, o = lax.scan(step, s0, (to_chunks(q), to_chunks(k), to_chunks(v)))
    o = jnp.swapaxes(o, 0, 1).reshape(B, L, H, dv)
    return o, s_fin


def _causal_dwconv(u, hist, w, b):
    K = w.shape[0]
    L = u.shape[1]
    full = jnp.concatenate([hist.astype(u.dtype), u], axis=1)
    y = b + sum(full[:, j:j + L] * w[j] for j in range(K))
    return y, full[:, -(K - 1):]


def _token_mixer(h, pos, s_ret, conv_hist, w_in, w_o_ret, conv_w, conv_b, w_o_conv, w_out, b_out):
    B, L, _ = h.shape
    proj = h @ w_in
    q, k, v, g, b_gate, c_gate, x_in, gate_ret, gate_conv = jnp.split(proj, SPLIT_POINTS, axis=-1)
    qf = _rotary(q.reshape(B, L, N_RET_HEADS, RET_QK_DIM).astype(jnp.float32), pos)
    kf = _rotary(k.reshape(B, L, N_RET_HEADS, RET_QK_DIM).astype(jnp.float32), pos) * (RET_QK_DIM ** -0.5)
    vf = v.reshape(B, L, N_RET_HEADS, RET_V_DIM).astype(jnp.float32)
    log_g = _ret_log_decay()
    if s_ret is None:
        o, s_new = _retention_prompt(qf, kf, vf, log_g)
    else:
        o, s_new = _retention_block(qf, kf, vf, s_ret.astype(jnp.float32), log_g)
    o = _head_norm(o).reshape(B, L, RET_V_WIDTH).astype(h.dtype)
    y_ret = (jax.nn.silu(g) * o) @ w_o_ret
    u = c_gate * x_in
    if conv_hist is None:
        conv_hist = jnp.zeros((B, CONV_K - 1, CONV_WIDTH), u.dtype)
    conv_out, hist_new = _causal_dwconv(u, conv_hist, conv_w, conv_b)
    y_conv = (b_gate * conv_out) @ w_o_conv
    merged = jax.nn.sigmoid(gate_ret) * y_ret + jax.nn.sigmoid(gate_conv) * y_conv
    return merged @ w_out + b_out, s_new, hist_new


def _conv_ffn(h, hist, w_up, conv_w, conv_b, w_down, b_down):
    a, gb = jnp.split(h @ w_up, 2, axis=-1)
    if hist is None:
        hist = jnp.zeros((h.shape[0], FFN_CONV_K - 1, D_FF), a.dtype)
    a_conv, hist_new = _causal_dwconv(a, hist, conv_w, conv_b)
    return (jax.nn.gelu(a_conv, approximate=False) * gb) @ w_down + b_down, hist_new


def _trunk(x, c, pos, ret_state, conv_state, ffn_state, p):
    (ln_in_g, ln_in_b, w_mod, b_mod, w_in, w_o_ret, conv_w, conv_b, w_o_conv,
     w_out, b_out, ln1_g, ln1_b, w_up, ffn_conv_w, ffn_conv_b, w_down, b_down,
     ln2_g, ln2_b) = p
    x = _layer_norm(x, ln_in_g, ln_in_b)
    rets, convs, ffns = [], [], []
    for l in range(DEPTH):
        mod = (c @ w_mod[l] + b_mod[l])[:, None, :]
        sh_t, sc_t, g_t, sh_c, sc_c, g_c = jnp.split(mod, 6, axis=-1)
        h = x * (1.0 + sc_t) + sh_t
        mix, s_new, ch_new = _token_mixer(
            h, pos,
            None if ret_state is None else ret_state[l],
            None if conv_state is None else conv_state[l],
            w_in[l], w_o_ret[l], conv_w[l], conv_b[l], w_o_conv[l], w_out[l], b_out[l])
        x = _layer_norm(ALPHA * x + g_t * mix, ln1_g[l], ln1_b[l])
        h2 = x * (1.0 + sc_c) + sh_c
        ffn, fh_new = _conv_ffn(h2, None if ffn_state is None else ffn_state[l],
                                w_up[l], ffn_conv_w[l], ffn_conv_b[l], w_down[l], b_down[l])
        x = _layer_norm(ALPHA * x + g_c * ffn, ln2_g[l], ln2_b[l])
        rets.append(s_new)
        convs.append(ch_new)
        ffns.append(fh_new)
    return x, jnp.stack(rets), jnp.stack(convs), jnp.stack(ffns)


def setup_inputs(seed: int = 0) -> dict:
    key = jax.random.key(seed)
    ks = jax.random.split(key, 32)

    def nrm(k, shape, s):
        return jax.random.normal(k, shape, jnp.float32) * s

    D = D_MODEL
    col_scale = jnp.concatenate([
        jnp.ones((2 * RET_QK_WIDTH,), jnp.float32),
        jnp.full((RET_V_WIDTH,), BETA, jnp.float32),
        jnp.ones((N_IN - 2 * RET_QK_WIDTH - RET_V_WIDTH,), jnp.float32)])
    return {
        "x_prompt": nrm(ks[0], (BATCH, SEQ, D), 1.0),
        "x_sample": nrm(ks[1], (DEC_BATCH, DEC_SEQ, D), 1.0),
        "c_prompt": nrm(ks[2], (BATCH, D), 1.0),
        "c_sample": nrm(ks[3], (DEC_BATCH, D), 1.0),
        "state_retention": nrm(ks[4], (DEPTH, DEC_BATCH, N_RET_HEADS, RET_QK_DIM, RET_V_DIM), 0.1),
        "state_shortconv": nrm(ks[5], (DEPTH, DEC_BATCH, CONV_K - 1, CONV_WIDTH), 1.0),
        "state_ffn_conv": nrm(ks[6], (DEPTH, DEC_BATCH, FFN_CONV_K - 1, D_FF), 1.0),
        "ln_in_g": 1.0 + nrm(ks[7], (D,), 0.01),
        "ln_in_b": nrm(ks[8], (D,), 0.01),
        "w_mod": nrm(ks[9], (DEPTH, D, 6 * D), 0.5 * D ** -0.5),
        "b_mod": nrm(ks[10], (DEPTH, 6 * D), 0.01),
        "w_in": nrm(ks[11], (DEPTH, D, N_IN), D ** -0.5) * col_scale,
        "w_o_ret": nrm(ks[12], (DEPTH, RET_V_WIDTH, D), BETA * RET_V_WIDTH ** -0.5),
        "conv_w": nrm(ks[13], (DEPTH, CONV_K, CONV_WIDTH), CONV_K ** -0.5),
        "conv_b": nrm(ks[14], (DEPTH, CONV_WIDTH), 0.01),
        "w_o_conv": nrm(ks[15], (DEPTH, CONV_WIDTH, D), BETA * CONV_WIDTH ** -0.5),
        "w_out": nrm(ks[16], (DEPTH, D, D), BETA * D ** -0.5),
        "b_out": nrm(ks[17], (DEPTH, D), 0.01),
        "ln1_g": 1.0 + nrm(ks[18], (DEPTH, D), 0.01),
        "ln1_b": nrm(ks[19], (DEPTH, D), 0.01),
        "w_up": nrm(ks[20], (DEPTH, D, 2 * D_FF), D ** -0.5),
        "ffn_conv_w": nrm(ks[21], (DEPTH, FFN_CONV_K, D_FF), FFN_CONV_K ** -0.5),
        "ffn_conv_b": nrm(ks[22], (DEPTH, D_FF), 0.01),
        "w_down": nrm(ks[23], (DEPTH, D_FF, D), BETA * D_FF ** -0.5),
        "b_down": nrm(ks[24], (DEPTH, D), 0.01),
        "ln2_g": 1.0 + nrm(ks[25], (DEPTH, D), 0.01),
        "ln2_b": nrm(ks[26], (DEPTH, D), 0.01),
    }


def reference(x_prompt, x_sample, c_prompt, c_sample, state_retention, state_shortconv, state_ffn_conv,
              ln_in_g, ln_in_b, w_mod, b_mod, w_in, w_o_ret, conv_w, conv_b, w_o_conv, w_out, b_out,
              ln1_g, ln1_b, w_up, ffn_conv_w, ffn_conv_b, w_down, b_down, ln2_g, ln2_b):
    p = (ln_in_g, ln_in_b, w_mod, b_mod, w_in, w_o_ret, conv_w, conv_b, w_o_conv,
         w_out, b_out, ln1_g, ln1_b, w_up, ffn_conv_w, ffn_conv_b, w_down, b_down,
         ln2_g, ln2_b)
    pos_prompt = jnp.arange(x_prompt.shape[1], dtype=jnp.int32)
    pos_sample = PAST_LEN + jnp.arange(x_sample.shape[1], dtype=jnp.int32)
    y_prompt, ret_p, conv_p, ffn_p = _trunk(x_prompt, c_prompt, pos_prompt, None, None, None, p)
    y_sample, ret_s, conv_s, ffn_s = _trunk(x_sample, c_sample, pos_sample,
                                            state_retention, state_shortconv, state_ffn_conv, p)
    return (y_prompt, y_sample, ret_p, conv_p, ffn_p, ret_s, conv_s, ffn_s)
```

```python
import numpy as np
import ml_dtypes
from contextlib import ExitStack
import concourse.bass as bass
import concourse.mybir as mybir
from concourse.bass_utils import run_bass_kernel_spmd

F32 = mybir.dt.float32
BF16 = mybir.dt.bfloat16
I32 = mybir.dt.int32
AF = mybir.ActivationFunctionType
ALU = mybir.AluOpType

D = 1024
H = 4
DK = 256
DV = 512
DFF = 2816
NFF = 22
SEQ = 2048
DEC_SEQ = 32
PAST = 2048
LN_EPS = 1e-5
ALPHA = 2.0 ** 0.25
NPAN = 94
NS = 7
NCAST = 24
GAMMA = [1.0 - 2.0 ** (-5.0 - h) for h in range(H)]


class Res:
    __slots__ = ("name", "writer", "readers")

    def __init__(self, name):
        self.name = name
        self.writer = None
        self.readers = {}

    def absorb(self, others):
        for o in others:
            if o.writer is not None:
                self.readers[o.writer[0]] = max(self.readers.get(o.writer[0], 0), o.writer[1])
            for k, v in o.readers.items():
                self.readers[k] = max(self.readers.get(k, 0), v)


class Eng:
    def __init__(self, name):
        self.name = name
        self.count = 0
        self.pending = False
        self.waited = {}
        self.prog = []


class Tracker:
    ENGS = ("pe", "act", "dve", "pool", "sp")

    def __init__(self, nc):
        self.nc = nc
        self.engs = {n: Eng(n) for n in self.ENGS}
        self.sems = {}
        self.dma_vals = {}
        self.ninst = 0

    def _need(self, e, key, val):
        if e.waited.get(key, 0) >= val:
            return
        if key in self.engs:
            src = self.engs[key]
            if val > src.count:
                raise RuntimeError(f"dep on future inc: {e.name} waits {key}>={val} but count={src.count}")
        e.waited[key] = val
        e.prog.append(("wait", key, val))

    def _deps(self, e, reads, writes, same_engine_ok):
        for r in reads:
            if r.writer is not None:
                if not (same_engine_ok and r.writer[0] == e.name):
                    self._need(e, *r.writer)
        for w in writes:
            if w.writer is not None:
                if not (same_engine_ok and w.writer[0] == e.name):
                    self._need(e, *w.writer)
            for k, v in w.readers.items():
                if k == e.name:
                    continue
                self._need(e, k, v)

    def emit(self, eng, fn, reads=(), writes=(), inc=True, same_engine_ok=False):
        e = self.engs[eng]
        self._deps(e, reads, writes, same_engine_ok)
        val = e.count + 1
        e.prog.append(("inst", fn, inc, getattr(self, "tag", "")))
        if inc:
            e.count = val
            e.pending = False
        else:
            e.pending = True
        for w in writes:
            w.writer = (eng, val)
            w.readers = {}
        for r in reads:
            r.readers[eng] = max(r.readers.get(eng, 0), val)
        self.ninst += 1

    def dma(self, eng, semkey, fn, reads=(), writes=()):
        e = self.engs[eng]
        self._deps(e, reads, writes, False)
        val = self.dma_vals.get(semkey, 0) + 16
        self.dma_vals[semkey] = val
        e.prog.append(("dma", fn, semkey))
        for w in writes:
            w.writer = (semkey, val)
            w.readers = {}
        for r in reads:
            r.readers[semkey] = max(r.readers.get(semkey, 0), val)
        self.ninst += 1

    def wait_all(self, eng, resources):
        e = self.engs[eng]
        for r in resources:
            if r.writer is not None:
                self._need(e, *r.writer)
            for k, v in r.readers.items():
                self._need(e, k, v)

    def run(self, es):
        nc = self.nc
        keys = list(self.ENGS) + sorted(self.dma_vals.keys())
        for k in keys:
            self.sems[k] = es.enter_context(nc.semaphore("s_" + k))
        for e in self.engs.values():
            if e.pending:
                raise RuntimeError(f"engine {e.name} has trailing non-inc instruction")
        block = es.enter_context(nc.Block())
        sems = self.sems

        def runner(e):
            def body(h):
                mysem = sems[e.name]
                for item in e.prog:
                    if item[0] == "wait":
                        h.wait_ge(sems[item[1]], item[2])
                    elif item[0] == "inst":
                        ins = item[1](h)
                        if item[2]:
                            ins.then_inc(mysem, 1)
                    else:
                        ins = item[1](h)
                        ins.then_inc(sems[item[2]], 16)
            return body

        block.tensor(runner(self.engs["pe"]))
        block.scalar(runner(self.engs["act"]))
        block.vector(runner(self.engs["dve"]))
        block.gpsimd(runner(self.engs["pool"]))
        block.sync(runner(self.engs["sp"]))


def build_program(debug=()):
    nc = bass.Bass("TRN2", target_bir_lowering=False)
    tr = Tracker(nc)

    def din(name, shape, dt=F32):
        return nc.dram_tensor(name, list(shape), dt, kind="ExternalInput").ap()

    def dout(name, shape, dt=F32):
        return nc.dram_tensor(name, list(shape), dt, kind="ExternalOutput").ap()

    xp = din("xp", [2, SEQ, D])
    xs = din("xs", [64, D])
    cT = din("cT", [128, 8, 4])
    sret = din("sret", [2, H, 2, 128, DV])
    sconv = din("sconv", [2, 128, 8, 2])
    sffn = din("sffn", [2, 128, NFF, 2])
    wpan = din("wpan", [NPAN, 128, 2048])
    wmod = din("wmod", [48, 128, 1024])
    bmod = din("bmod", [128, 48])
    vtm = din("vtm", [128, 6, D])
    vfm = din("vfm", [128, 4, 8])
    cw = din("cw", [128, 8, 4])
    fw = din("fw", [128, NFF, 4])
    brow = din("brow", [1, 2 * D])
    rope_p = din("rope_p", [128, 2, SEQ])
    rope_s = din("rope_s", [128, 2, 64])
    maskT = din("maskT", [128, H, 128], BF16)
    epsv = din("epsv", [128, 4, H])
    kdec = din("kdec", [128, 2, 4, H])
    offs = din("offs", [128, H, 4])
    identb = din("identb", [128, 128], BF16)
    identf = din("identf", [128, 128])

    wbf = nc.dram_tensor("wbf", [NPAN, 128, 2048], BF16, kind="Internal").ap()

    yp = dout("yp", [2, SEQ, D])
    ys = dout("ys", [64, D])
    rs_o = dout("rs_o", [4, H, 2, 128, DV])
    cs_o = dout("cs_o", [4, 128, 8, 2])
    fs_o = dout("fs_o", [4, 128, NFF, 2])
    out_res = []
    dbg_outs = {}

    es = ExitStack()
    with es:
        def sb(name, shape, dt=F32):
            return es.enter_context(nc.sbuf_tensor(name, list(shape), dt))

        ring = sb("ring", [128, NS, 2048], BF16)
        r_ring = [Res(f"ring{i}") for i in range(NS)]
        xa = sb("xa", [128, 2, 4, D])
        r_xa = [[Res(f"xa{p}{b}") for b in range(4)] for p in range(2)]
        ptm = sb("ptm", [128, 6, D])
        r_ptm = Res("ptm")
        gtc = sb("gtc", [128, 2, D])
        r_gtc = Res("gtc")
        S_p = sb("S_p", [128, H, 2, DV])
        r_S = [Res(f"S{h}") for h in range(H)]
        Sbf = sb("Sbf", [128, 2, 2, DV], BF16)
        r_Sbf = [Res("Sbf0"), Res("Sbf1")]
        rope = sb("rope", [128, 2, 512])
        r_rope = Res("rope")
        hT = sb("hT", [128, 8, 512], BF16)
        r_hT = Res("hT")
        qk = sb("qk", [128, 2, 2, 2, 512], BF16)
        r_qk = [Res("qk0"), Res("qk1")]
        mvbuf = sb("mvbuf", [128, 4096], BF16)
        vsb = mvbuf[:].rearrange("p (b r d) -> p b r d", b=2, r=4)
        r_v = [Res("v0"), Res("v1")]
        sgb = sb("sgb", [128, 2, 4, 512], BF16)
        r_sg = [Res("sg0"), Res("sg1")]
        ktok = sb("ktok", [128, 4, 256], BF16)
        r_ktok = Res("ktok")
        smT = sb("smT", [128, 10, 128], BF16)
        r_smT = Res("smT")
        onb = sb("onb", [128, 4, DV], BF16)
        r_onb = [Res(f"onb{i}") for i in range(4)]
        gbuf = sb("gbuf", [128, 12288], BF16)
        gatedT = gbuf[:, 0:8192].rearrange("p (c t) -> p c t", t=512)
        r_gated = Res("gatedT")
        bgc = gbuf[:, 8192:12288].rearrange("p (c t) -> p c t", t=512)
        r_bgc = Res("bgc")
        mergedT = mvbuf[:].rearrange("p (c t) -> p c t", t=512)
        r_merged = Res("mergedT")
        actT = gbuf[:, 0:NFF * 512].rearrange("p (c t) -> p c t", t=512)
        r_actT = [Res(f"actT{f}") for f in range(NFF)]
        xnb = sb("xnb", [128, D], BF16)
        r_xnb = Res("xnb")
        tmpA = sb("tmpA", [128, 4, 512])
        r_tmpA = [Res(f"tmpA{i}") for i in range(4)]
        ut = sb("ut", [128, 2, 520])
        r_ut = [Res("ut0"), Res("ut1")]
        uh = sb("uh", [128, 4, 8, 2])
        r_uh = Res("uh")
        ah = sb("ah", [128, 4, NFF, 2])
        r_ah = Res("ah")
        modfm = sb("modfm", [128, 48, 4])
        r_modfm = Res("modfm")
        mA = sb("mA", [128, 4, 8, 4])
        r_mA = Res("mA")
        cTt = sb("cTt", [128, 8, 4])
        vfmt = sb("vfmt", [128, 4, 8])
        cwt = sb("cwt", [128, 8, 4])
        fwt = sb("fwt", [128, NFF, 4])
        bmt = sb("bmt", [128, 48])
        browb = sb("browb", [1, 2 * D], BF16)
        ones = sb("ones", [1, 128], BF16)
        maskt = sb("maskt", [128, H, 128], BF16)
        epst = sb("epst", [128, 4, H])
        kdt = sb("kdt", [128, 2, 4, H])
        offt = sb("offt", [128, H, 4])
        idb = sb("idb", [128, 128], BF16)
        idf = sb("idf", [128, 128])
        bcl = sb("bcl", [128, 128])
        r_bcl = Res("bcl")
        r_const = Res("const")
        r_browb = Res("browb")
        r_ones = Res("ones")
        st6 = sb("st6", [128, 2, 6])
        mvt = sb("mvt", [128, 2])
        rsq = sb("rsq", [128, 8])
        rsqi = sb("rsqi", [128, 2], I32)
        rsq4 = sb("rsq4", [128, 3, 4])
        rsq4i = sb("rsq4i", [128, 4], I32)
        mv4 = sb("mv4", [128, 4, 2])
        st64 = sb("st64", [128, 4, 6])
        rs4 = sb("rs4", [128, 2, 4, 2])
        r_rs4 = [Res("rs4a"), Res("rs4b")]
        st6a = sb("st6a", [128, 4, 2, 6])
        mv4a = sb("mv4a", [128, 4, 2])
        rsA = sb("rsA", [128, 4, 2])
        r_rsA = Res("rsA")
        r_statA = Res("statA")
        st6b = sb("st6b", [128, 4, 2, 6])
        mv4b = sb("mv4b", [128, 4, 2])
        rsB = sb("rsB", [128, 4, 2])
        r_rsB = Res("rsB")
        r_statB = Res("statB")
        r_stat = Res("stat")
        rstd_t = sb("rstd_t", [128, 4, 2])
        r_rstd = [Res(f"rstd{i}") for i in range(4)]
        wm32 = tmpA[:].rearrange("p a b -> p (a b)").rearrange("p (b f) -> p b f", f=1024)
        r_wm = [Res("wm0"), Res("wm1")]

        psb = [es.enter_context(nc.psum_tensor(f"ps{i}", [128, 512], F32)) for i in range(8)]
        r_ps = [Res(f"ps{i}") for i in range(8)]
        ps_ctr = [0]

        ps_held = set()

        def psum(hold=False):
            while True:
                i = ps_ctr[0] % 8
                ps_ctr[0] += 1
                if i not in ps_held:
                    break
            if hold:
                ps_held.add(i)
            return psb[i], r_ps[i]

        def psum_unhold(r):
            ps_held.discard(r_ps.index(r))

        rot = {"tmp": 0, "rstd": 0}

        def tmp32():
            i = rot["tmp"] % 4
            rot["tmp"] += 1
            return tmpA[:, i, :], r_tmpA[i]

        def rstd_slot():
            i = rot["rstd"] % 4
            rot["rstd"] += 1
            return rstd_t[:, i, :], r_rstd[i]

        def mm(out, lhsT, rhs, start, stop, reads, writes, inc):
            tr.emit("pe", lambda t: t.matmul(out, lhsT=lhsT, rhs=rhs, start=start, stop=stop),
                    reads, writes, inc=inc, same_engine_ok=True)

        def transp(out, in_, ident, reads, writes, inc):
            tr.emit("pe", lambda t: t.transpose(out=out, in_=in_, identity=ident),
                    reads, writes, inc=inc, same_engine_ok=True)

        def act(out, in_, func, reads, writes, bias=None, scale=None):
            kw = {}
            if bias is not None:
                kw["bias"] = bias
            if scale is not None:
                kw["scale"] = scale
            tr.emit("act", lambda a: a.activation(out=out, in_=in_, func=func, **kw), reads, writes)

        def tt(eng, out, in0, in1, op, reads, writes):
            tr.emit(eng, lambda v: v.tensor_tensor(out=out, in0=in0, in1=in1, op=op), reads, writes)

        def ts(eng, out, in0, s1, s2, op0, op1, reads, writes):
            if op1 is None:
                tr.emit(eng, lambda v: v.tensor_scalar(out=out, in0=in0, scalar1=s1, scalar2=None, op0=op0), reads, writes)
            else:
                tr.emit(eng, lambda v: v.tensor_scalar(out=out, in0=in0, scalar1=s1, scalar2=s2, op0=op0, op1=op1), reads, writes)

        def stt(out, in0, scalar, in1, op0, op1, reads, writes):
            tr.emit("dve", lambda v: v.scalar_tensor_tensor(out=out, in0=in0, scalar=scalar, in1=in1, op0=op0, op1=op1), reads, writes)

        def cp(eng, out, in_, reads, writes):
            if eng == "act":
                tr.emit("act", lambda a: a.copy(out=out, in_=in_), reads, writes)
            else:
                tr.emit(eng, lambda v: v.tensor_copy(out=out, in_=in_), reads, writes)

        def dma(eng, key, out, in_, reads, writes):
            tr.dma(eng, key, lambda g: g.dma_start(out=out, in_=in_), reads, writes)

        def dbg(name, ap, shape, res, dt=F32):
            if name not in debug:
                return
            o = dout("dbg_" + name, shape, dt)
            r = Res("dbg_" + name)
            dma("sp", "dbg", o, ap, [res], [r])
            out_res.append(r)
            dbg_outs[name] = shape

        def rsqrt(n, mean_ap, var_ap, eps, rs_ap, reads, r_rs):
            xe = rsq[:n, 0:1]
            yy = rsq[:n, 1:2]
            t_ = rsq[:n, 2:3]
            ti = rsqi[:n, 0:1]
            if isinstance(eps, float):
                ts("dve", xe, var_ap, eps, None, ALU.add, None, reads + [r_stat], [r_stat])
            else:
                tt("dve", xe, var_ap, eps, ALU.add, reads + [r_stat, r_const], [r_stat])
            ts("dve", ti, xe.bitcast(I32), 1, None, ALU.arith_shift_right, None, [r_stat], [r_stat])
            ts("dve", yy.bitcast(I32), ti, -1.0, 1597463007.0, ALU.mult, ALU.add, [r_stat], [r_stat])
            for it in range(3):
                stt(t_, xe, yy, yy, ALU.mult, ALU.mult, [r_stat], [r_stat])
                ts("dve", t_, t_, -0.5, 1.5, ALU.mult, ALU.add, [r_stat], [r_stat])
                if it < 2:
                    tt("dve", yy, yy, t_, ALU.mult, [r_stat], [r_stat])
                else:
                    tt("dve", rs_ap[:n, 0:1], yy, t_, ALU.mult, [r_stat], [r_stat, r_rs])
            stt(rs_ap[:n, 1:2], mean_ap, -1.0, rs_ap[:n, 0:1], ALU.mult, ALU.mult, reads + [r_stat, r_rs], [r_rs])

        def rsqrt_multi(n, k, mean_ap, var_ap, eps_ap, rs3, reads, r_rs):
            xe = rsq4[:n, 0, 0:k]
            yy = rsq4[:n, 1, 0:k]
            t_ = rsq4[:n, 2, 0:k]
            ti = rsq4i[:n, 0:k]
            if isinstance(eps_ap, float):
                ts("dve", xe, var_ap, eps_ap, None, ALU.add, None, reads + [r_stat], [r_stat])
            else:
                tt("dve", xe, var_ap, eps_ap, ALU.add, reads + [r_stat, r_const], [r_stat])
            ts("dve", ti, xe.bitcast(I32), 1, None, ALU.arith_shift_right, None, [r_stat], [r_stat])
            ts("dve", yy.bitcast(I32), ti, -1.0, 1597463007.0, ALU.mult, ALU.add, [r_stat], [r_stat])
            for it in range(3):
                tt("dve", t_, xe, yy, ALU.mult, [r_stat], [r_stat])
                tt("dve", t_, t_, yy, ALU.mult, [r_stat], [r_stat])
                ts("dve", t_, t_, -0.5, 1.5, ALU.mult, ALU.add, [r_stat], [r_stat])
                if it < 2:
                    tt("dve", yy, yy, t_, ALU.mult, [r_stat], [r_stat])
                else:
                    tt("dve", rs3[:n, 0:k, 0], yy, t_, ALU.mult, [r_stat], [r_stat, r_rs])
            stt(rs3[:n, 0:k, 1], mean_ap, -1.0, rs3[:n, 0:k, 0], ALU.mult, ALU.mult, reads + [r_stat, r_rs], [r_rs])

        r_cast = [Res(f"cast{g}") for g in range(NCAST)]
        wst = {"next_load": 0, "next_use": 0}
        NPASS = 9
        TOTAL = NPASS * NPAN

        r_wbf = [Res(f"wbf{i}") for i in range(NPAN)]
        wunits = [(xa[:, 1, b, :], r_xa[1][b]) for b in range(4)] + [(xa[:, 0, b, :], r_xa[0][b]) for b in range(1, 4)]
        pend_store = []

        def w_flush_store(keep):
            while len(pend_store) > keep:
                pidx, slot = pend_store.pop(0)
                dma("sp", f"wst{slot}", wbf[pidx], ring[:, slot, :], [r_ring[slot]], [r_wbf[pidx]])

        pend_cast = []

        def emit_cast(g):
            slot = g % NS
            pidx = g % NPAN
            for hf in range(2):
                u = (2 * g + hf) % len(wunits)
                uap, ures = wunits[u]
                cp("act" if hf == 0 else "dve", ring[:, slot, hf * 1024:(hf + 1) * 1024], uap, [ures], [r_ring[slot]])
            pend_store.append((pidx, slot))

        def w_load_one():
            g = wst["next_load"]
            if g >= TOTAL:
                return
            wst["next_load"] += 1
            slot = g % NS
            pidx = g % NPAN
            if g < NPAN:
                for hf in range(2):
                    u = (2 * g + hf) % len(wunits)
                    uap, ures = wunits[u]
                    dma("sp", f"wld{u}", uap, wpan[pidx][:, hf * 1024:(hf + 1) * 1024], [], [ures])
                pend_cast.append(g)
                while len(pend_cast) > 2:
                    emit_cast(pend_cast.pop(0))
                w_flush_store(2)
            else:
                while pend_cast:
                    emit_cast(pend_cast.pop(0))
                w_flush_store(0)
                dma("sp", f"ring{slot}", ring[:, slot, :], wbf[pidx], [r_wbf[pidx]], [r_ring[slot]])

        pan_order = []

        def w_next(key):
            g = wst["next_use"]
            wst["next_use"] += 1
            if g < NPAN:
                pan_order.append(key)
            else:
                assert pan_order[g % NPAN] == key, (g, key, pan_order[g % NPAN])
            slot = g % NS
            return ring[:, slot, :], r_ring[slot]

        def w_release(k=1):
            for _ in range(k):
                w_load_one()

        cl = [(cTt[:], cT), (vfmt[:], vfm), (cwt[:], cw), (fwt[:], fw), (bmt[:], bmod),
              (maskt[:], maskT), (epst[:], epsv), (kdt[:], kdec), (offt[:], offs),
              (idb[:], identb), (idf[:], identf)]
        for o, i in cl:
            dma("sp", "const", o, i, [], [r_const])
        dma("sp", "ptm", ptm[:], vtm, [], [r_ptm])
        dma("sp", "hist_u", uh[:, 2:4, :, :], sconv.rearrange("s p c r -> p s c r"), [], [r_uh])
        dma("sp", "hist_a", ah[:, 2:4, :, :], sffn.rearrange("s p c r -> p s c r"), [], [r_ah])
        tr.emit("pool", lambda g: g.memset(uh[:, 0:2, :, :], 0.0), [], [r_uh])
        tr.emit("pool", lambda g: g.memset(ah[:, 0:2, :, :], 0.0), [], [r_ah])
        tr.emit("pool", lambda g: g.memset(ones[:], 1.0), [], [r_ones])
        tr.dma("pool", "browc", lambda q: q.dma_start(out=browb[:], in_=brow), [], [r_browb])
        for i_ in range(4):
            tr.emit("act", (lambda i_: (lambda a: a.activation(out=ptm[:, i_, :], in_=ptm[:, i_, :], func=AF.Copy, scale=float(ALPHA))))(i_),
                    [r_ptm], [r_ptm])
        pm, r_pm = psum()
        wmb = gbuf[:, 0:4096].rearrange("p (b f) -> p b f", f=1024)
        r_wmb = [Res(f"wmb{i}") for i in range(4)]
        wm32 = xa[:, 1, :, :]
        r_wm = [Res(f"wm{i}") for i in range(4)]
        cTb = gbuf[:, 4096:4128].rearrange("p (k s) -> p k s", s=4)
        r_cTb = Res("cTb")
        cp("dve", cTb, cTt[:], [r_const], [r_cTb])
        for oc in range(48):
            b_ = oc % 4
            dma("sp", f"wm{b_}", wm32[:, b_, :], wmod[oc], [], [r_wm[b_]])
            cp("act" if oc % 2 == 0 else "dve", wmb[:, b_, :], wm32[:, b_, :], [r_wm[b_]], [r_wmb[b_]])
            for kc in range(8):
                mm(pm[:, oc * 4:(oc + 1) * 4], wmb[:, b_, kc * 128:(kc + 1) * 128], cTb[:, kc, :],
                   kc == 0, kc == 7, [r_wmb[b_], r_cTb], [r_pm], kc == 7)
        tt("dve", modfm[:], pm[:, 0:192].rearrange("p (a b) -> p a b", b=4),
           bmt[:].unsqueeze(2).broadcast_to([128, 48, 4]), ALU.add, [r_pm, r_const], [r_modfm])
        for which, (gi, sci, shi) in enumerate([(0, 1, 0), (2, 4, 3)]):
            one_sc = tmpA[:, 0, 0:32].rearrange("p (c s) -> p c s", s=4)
            ts("dve", one_sc, modfm[:, sci * 8:(sci + 1) * 8, :], 1.0, None, ALU.add, None, [r_modfm], [r_tmpA[0]])
            tt("dve", mA[:, 2 * which, :, :], one_sc, vfmt[:, gi, :].unsqueeze(2).broadcast_to([128, 8, 4]), ALU.mult,
               [r_tmpA[0], r_const], [r_mA])
            tt("dve", mA[:, 2 * which + 1, :, :], one_sc, vfmt[:, gi + 1, :].unsqueeze(2).broadcast_to([128, 8, 4]), ALU.mult,
               [r_tmpA[0], r_const], [r_mA])
            tt("dve", mA[:, 2 * which + 1, :, :], mA[:, 2 * which + 1, :, :], modfm[:, shi * 8:(shi + 1) * 8, :], ALU.add,
               [r_mA, r_modfm], [r_mA])

        def build_gtc(part_slots):
            for which, mi in enumerate((2, 5)):
                for half in range(2):
                    pg, r_pg = psum()
                    for cc in range(4):
                        c = half * 4 + cc
                        for (p0, p1, slot) in part_slots:
                            cp("dve", bcl[:, p0:p1], modfm[:, mi * 8 + c, slot:slot + 1].broadcast_to([128, p1 - p0]),
                               [r_modfm], [r_bcl])
                        npart = part_slots[-1][1]
                        mm(pg[:npart, cc * 128:(cc + 1) * 128], bcl[:, 0:npart], idf[:], True, True,
                           [r_bcl, r_const], [r_pg], True)
                    npart = part_slots[-1][1]
                    cp("act", gtc[:npart, which, half * 512:(half + 1) * 512], pg[:npart, :], [r_pg], [r_gtc])

        for b_ in range(4):
            r_xa[1][b_].absorb([r_wm[b_]])
        for _ in range(NS):
            w_load_one()

        class Pass:
            pass

        def make_pass(idx, kind, seq, st):
            P = Pass()
            P.idx, P.kind, P.seq, P.st = idx, kind, seq, st
            P.par = idx % 2
            if kind == "prompt":
                P.T = 512
                P.blocks = [(b * 128, 128) for b in range(4)]
                P.segs = [(0, 512, seq)]
                P.n = 128
                P.kd = 0
                P.rope_src = rope_p[:, :, st * 512:(st + 1) * 512]
                P.last = (st == 3)
                P.Lr = 512
            else:
                P.T = 64
                P.blocks = [(0, 64)]
                P.segs = [(0, 32, 2), (32, 32, 3)]
                P.n = 32
                P.kd = 1
                P.rope_src = rope_s
                P.last = True
                P.Lr = 32
            P.gL = [GAMMA[h] ** P.Lr for h in range(H)]
            return P

        def seg_of_block(P, tok0, nb):
            return [(max(o, tok0), min(o + L, tok0 + nb), slot) for (o, L, slot) in P.segs
                    if max(o, tok0) < min(o + L, tok0 + nb)]

        def ln_block(P, bi, tok0, nb, load_from, whichA, do_T, affine_idx):
            ln_part1(P, bi, tok0, nb, load_from, whichA, do_T, affine_idx)
            if do_T:
                ln_part2(P, bi, tok0, nb, whichA)

        def ln_part1(P, bi, tok0, nb, load_from, whichA, do_T, affine_idx):
            par = P.par
            xab = xa[:nb, par, bi, :]
            rx = r_xa[par][bi]
            if load_from is not None:
                dma("sp", f"xld{par}{bi}", xab, load_from, [], [rx])
            for i in range(2):
                tr.emit("dve", (lambda i: (lambda v: v.bn_stats(out=st6[:nb, i, :], in_=xa[:nb, par, bi, i * 512:(i + 1) * 512])))(i),
                        [rx, r_stat], [r_stat])
            tr.emit("dve", lambda v: v.bn_aggr(out=mvt[:nb, :], in_=st6[:nb, :, :].rearrange("p a b -> p (a b)")),
                    [r_stat], [r_stat])
            rs_ap, r_rs = rstd_slot()
            rsqrt(nb, mvt[:nb, 0:1], mvt[:nb, 1:2], LN_EPS, rs_ap, [], r_rs)
            if do_T:
                act(xnb[:nb, :], xab, AF.Identity, [rx, r_rs], [r_xnb], bias=rs_ap[:nb, 1:2], scale=rs_ap[:nb, 0:1])
            act(xab, xab, AF.Identity, [rx, r_rs], [rx], bias=rs_ap[:nb, 1:2], scale=rs_ap[:nb, 0:1])
            tt("pool", xab, xab, ptm[:nb, affine_idx, :], ALU.mult, [rx, r_ptm], [rx])
            tt("pool", xab, xab, ptm[:nb, affine_idx + 1, :], ALU.add, [rx, r_ptm], [rx])

        def ln_part2(P, bi, tok0, nb, whichA):
            pt, r_pt = psum()
            ptb = pt[:].bitcast(BF16)
            for c in range(8):
                transp(ptb[:, c * 128:c * 128 + nb], xnb[:nb, c * 128:(c + 1) * 128], idb[:nb, :nb],
                       [r_xnb, r_const], [r_pt], c == 7)
            for c in range(8):
                for (a0, a1, slot) in seg_of_block(P, tok0, nb):
                    if c % 2 == 0:
                        ts("dve", hT[:, c, a0:a1], ptb[:, c * 128 + (a0 - tok0):c * 128 + (a1 - tok0)],
                           mA[:, 2 * whichA, c, slot:slot + 1], mA[:, 2 * whichA + 1, c, slot:slot + 1],
                           ALU.mult, ALU.add, [r_pt, r_mA], [r_hT])
                    else:
                        act(hT[:, c, a0:a1], ptb[:, c * 128 + (a0 - tok0):c * 128 + (a1 - tok0)], AF.Identity,
                            [r_pt, r_mA], [r_hT], bias=mA[:, 2 * whichA + 1, c, slot:slot + 1], scale=mA[:, 2 * whichA, c, slot:slot + 1])

        def stage_A(P):
            tr.tag = f"{P.kind}{P.seq}{P.st}.A"
            dma("sp", "rope", rope[:, :, 0:P.T], P.rope_src, [], [r_rope])
            for bi, (tok0, nb) in enumerate(P.blocks):
                src = xp[P.seq, P.st * 512 + tok0: P.st * 512 + tok0 + nb, :] if P.kind == "prompt" else xs
                ln_block(P, bi, tok0, nb, src, 0, True, 0)

        def stage_pre(P):
            if P.kind == "sample":
                build_gtc([(0, 32, 2), (32, 64, 3)])
            elif P.st == 0:
                build_gtc([(0, 128, P.seq)])
                for h in range(H):
                    tr.emit("pool", (lambda h: (lambda g: g.memset(S_p[:, h, :, :], 0.0)))(h), [], [r_S[h]])

        def task_P1(P, h):
            tr.tag = f"{P.kind}{P.seq}{P.st}.P1{h}"
            T = P.T
            hb = h % 2
            for which in range(2):
                wp_, r_wp = w_next(("q" if which == 0 else "k", h))
                pss = []
                for e in range(2):
                    p_, r_p = psum()
                    for kc in range(8):
                        mm(p_[:, 0:T], wp_[:, kc * 256 + e * 128: kc * 256 + (e + 1) * 128], hT[:, kc, 0:T],
                           kc == 0, kc == 7, [r_wp, r_hT], [r_p], kc == 7)
                    pss.append((p_, r_p))
                w_release()
                (x1, r1), (x2, r2) = pss
                cos = rope[:, 0, 0:T]
                sin = rope[:, 1, 0:T]
                t1, rt1 = tmp32()
                t2, rt2 = tmp32()
                tt("dve", t1[:, 0:T], x1[:, 0:T], cos, ALU.mult, [r1, r_rope], [rt1])
                tt("dve", t2[:, 0:T], x2[:, 0:T], sin, ALU.mult, [r2, r_rope], [rt2])
                tt("dve", qk[:, hb, which, 0, 0:T], t1[:, 0:T], t2[:, 0:T], ALU.subtract, [rt1, rt2], [r_qk[hb]])
                t3, rt3 = tmp32()
                t4, rt4 = tmp32()
                tt("dve", t3[:, 0:T], x1[:, 0:T], sin, ALU.mult, [r1, r_rope], [rt3])
                tt("dve", t4[:, 0:T], x2[:, 0:T], cos, ALU.mult, [r2, r_rope], [rt4])
                tt("dve", qk[:, hb, which, 1, 0:T], t3[:, 0:T], t4[:, 0:T], ALU.add, [rt3, rt4], [r_qk[hb]])

        def rb_list(P):
            out = []
            for si, (o, L, slot) in enumerate(P.segs):
                for ib in range(L // P.n):
                    out.append((si, ib, o + ib * P.n, slot))
            return out

        def task_P2(P, h):
            tr.tag = f"{P.kind}{P.seq}{P.st}.P2{h}"
            T = P.T
            n = P.n
            hb = h % 2
            wv0, r_wv0 = w_next(("v", h, 0))
            wv1, r_wv1 = w_next(("v", h, 1))
            for rb, (si, ib, t0, slot) in enumerate(rb_list(P)):
                p_, r_p = psum()
                for kc in range(8):
                    wv = wv0 if kc < 4 else wv1
                    rw = r_wv0 if kc < 4 else r_wv1
                    mm(p_[:n, :], hT[:, kc, t0:t0 + n], wv[:, (kc % 4) * 512:(kc % 4 + 1) * 512],
                       kc == 0, kc == 7, [rw, r_hT], [r_p], kc == 7)
                cp("act", vsb[:n, hb, rb, :], p_[:n, :], [r_p], [r_v[hb], r_merged])
            w_release(2)
            for j2 in range(2):
                wg, r_wg = w_next(("g", h, j2))
                for jj in range(2):
                    j = j2 * 2 + jj
                    p_, r_p = psum()
                    for kc in range(8):
                        mm(p_[:, 0:T], wg[:, kc * 256 + jj * 128: kc * 256 + (jj + 1) * 128], hT[:, kc, 0:T],
                           kc == 0, kc == 7, [r_wg, r_hT], [r_p], kc == 7)
                    act(sgb[:, hb, j, 0:T], p_[:, 0:T], AF.Silu, [r_p], [r_sg[hb]])
                w_release()

        def S_of(P, h, slot):
            if P.kind == "sample" and slot == 3:
                hh = (h + 2) % 4
            else:
                hh = h
            return S_p[:, hh, :, :], r_S[hh]

        def task_R1(P, h):
            tr.tag = f"{P.kind}{P.seq}{P.st}.R1{h}"
            n = P.n
            hb = h % 2
            qT = qk[:, hb, 0, :, :]
            kT = qk[:, hb, 1, :, :]
            rbl = rb_list(P)
            for si, (o, L, slot) in enumerate(P.segs):
                S_ap, r_Sx = S_of(P, h, slot)
                if P.kind == "sample":
                    dma("sp", f"sld{slot}{h}", S_ap, sret[slot - 2, h].rearrange("e p v -> p e v"), [], [r_Sx])
                sbi = hb if P.kind == "prompt" else si
                cp("act", Sbf[:, sbi, :, :], S_ap, [r_Sx], [r_Sbf[sbi]])
            blk = 0
            for rb, (si, ib, t0, slot) in enumerate(rbl):
                seg_t0 = P.segs[si][0]
                for jb in range(ib + 1):
                    tj = seg_t0 + jb * n
                    psc, r_psc = psum()
                    for e in range(2):
                        mm(psc[:n, 0:n], kT[:, e, tj:tj + n], qT[:, e, t0:t0 + n], e == 0, e == 1, [r_qk[hb]], [r_psc], e == 1)
                    if jb == ib:
                        stt(smT[:n, blk, 0:n], psc[:n, 0:n], float(GAMMA[h] ** (-128.0 * jb)), maskt[:n, h, 0:n], ALU.mult, ALU.mult,
                            [r_psc, r_const], [r_smT])
                    else:
                        ts("dve", smT[:n, blk, 0:n], psc[:n, 0:n], offt[:n, h, jb:jb + 1], None, ALU.mult, None,
                           [r_psc, r_const], [r_smT])
                    blk += 1
            for rb, (si, ib, t0, slot) in enumerate(rbl):
                pkt, r_pkt = psum()
                pktb = pkt[:].bitcast(BF16)
                for e in range(2):
                    transp(pktb[:n, e * 128:(e + 1) * 128], kT[:, e, t0:t0 + n], idb[:, :], [r_qk[hb], r_const], [r_pkt], e == 1)
                ts("dve", ktok[:n, rb, :], pktb[:n, 0:256], kdt[:n, P.kd, ib, h:h + 1], None, ALU.mult, None,
                   [r_pkt, r_const], [r_ktok])

        def task_R2a(P, h):
            tr.tag = f"{P.kind}{P.seq}{P.st}.R2a{h}"
            n = P.n
            hb = h % 2
            qT = qk[:, hb, 0, :, :]
            rbl = rb_list(P)
            nrb = len(rbl)
            blk = 0
            pos = []
            for rb, (si, ib, t0, slot) in enumerate(rbl):
                po, r_po = psum()
                for jb in range(ib + 1):
                    rbj = rb - ib + jb
                    mm(po[:n, :], smT[:n, blk, 0:n], vsb[:n, hb, rbj, :], jb == 0, False, [r_smT, r_v[hb]], [r_po], False)
                    blk += 1
                sbi = hb if P.kind == "prompt" else si
                for e in range(2):
                    mm(po[:n, :], qT[:, e, t0:t0 + n], Sbf[:, sbi, e, :], False, e == 1, [r_qk[hb], r_Sbf[sbi]], [r_po], e == 1)
                pos.append((po, r_po))
                tr.emit("dve", (lambda n, po, rb: (lambda v: v.bn_stats(out=st64[:n, rb, :], in_=po[:n, :])))(n, po, rb), [r_po, r_stat], [r_stat])
                tr.emit("dve", (lambda n, rb: (lambda v: v.bn_aggr(out=mv4[:n, rb, :], in_=st64[:n, rb, :])))(n, rb), [r_stat], [r_stat])
            if P.kind == "prompt":
                eps_ap = epst[:n, 0:nrb, h]
            else:
                eps_ap = epst[:n, 0:1, h].broadcast_to([n, nrb])
            rs3 = rs4[:, hb, :, :]
            rsqrt_multi(n, nrb, mv4[:n, 0:nrb, 0], mv4[:n, 0:nrb, 1], eps_ap, rs3, [], r_rs4[hb])
            for rb, (po, r_po) in enumerate(pos):
                act(onb[:n, rb, :], po[:n, :], AF.Identity, [r_po, r_rs4[hb]], [r_onb[rb]], bias=rs3[:n, rb, 1:2], scale=rs3[:n, rb, 0:1])
            for si, (o, L, slot) in enumerate(P.segs):
                S_ap, r_Sx = S_of(P, h, slot)
                rbs = [rb for rb, x in enumerate(rbl) if x[0] == si]
                for e in range(2):
                    pkv, r_pkv = psum()
                    for k_, rb in enumerate(rbs):
                        mm(pkv[:, :], ktok[:n, rb, e * 128:(e + 1) * 128], vsb[:n, hb, rb, :], k_ == 0, k_ == len(rbs) - 1,
                           [r_ktok, r_v[hb]], [r_pkv], k_ == len(rbs) - 1)
                    stt(S_ap[:, e, :], S_ap[:, e, :], float(P.gL[h]), pkv[:, :], ALU.mult, ALU.add, [r_Sx, r_pkv], [r_Sx])
                if P.last:
                    r_o = Res("rs_o")
                    dma("sp", f"strs{slot}{h}", rs_o[slot, h].rearrange("e p v -> p e v"), S_ap, [r_Sx], [r_o])
                    out_res.append(r_o)

        def task_R2b(P, h):
            tr.tag = f"{P.kind}{P.seq}{P.st}.R2b{h}"
            n = P.n
            hb = h % 2
            for rb, (si, ib, t0, slot) in enumerate(rb_list(P)):
                pot, r_pot = psum()
                potb = pot[:].bitcast(BF16)
                for d in range(4):
                    transp(potb[:, d * 128:d * 128 + n], onb[:n, rb, d * 128:(d + 1) * 128], idb[:n, :n],
                           [r_onb[rb], r_const], [r_pot], d == 3)
                tt("dve", gatedT[:, h * 4:(h + 1) * 4, t0:t0 + n],
                   potb[:, 0:512].rearrange("p (d t) -> p d t", t=128)[:, :, 0:n],
                   sgb[:, hb, :, t0:t0 + n], ALU.mult, [r_pot, r_sg[hb]], [r_gated] + r_actT)

        def task_C(P, cpair):
            tr.tag = f"{P.kind}{P.seq}{P.st}.C{cpair}"
            T = P.T
            wcx = [w_next(("cx", cpair * 2)), w_next(("cx", cpair * 2 + 1))]
            wbg, r_wbg = w_next(("bg", cpair))
            for ci in range(2):
                c = cpair * 2 + ci
                wc, r_wc = wcx[ci]
                pcg, r_pcg = psum()
                pxi, r_pxi = psum()
                pbg, r_pbg = psum()
                for kc in range(8):
                    mm(pcg[:, 0:T], wc[:, kc * 256: kc * 256 + 128], hT[:, kc, 0:T], kc == 0, kc == 7, [r_wc, r_hT], [r_pcg], kc == 7)
                for kc in range(8):
                    mm(pxi[:, 0:T], wc[:, kc * 256 + 128: kc * 256 + 256], hT[:, kc, 0:T], kc == 0, kc == 7, [r_wc, r_hT], [r_pxi], kc == 7)
                for kc in range(8):
                    mm(pbg[:, 0:T], wbg[:, kc * 256 + ci * 128: kc * 256 + (ci + 1) * 128], hT[:, kc, 0:T], kc == 0, kc == 7,
                       [r_wbg, r_hT], [r_pbg], kc == 7)
                cgs, r_cgs = tmp32()
                cp("act", cgs[:, 0:T], pcg[:, 0:T], [r_pcg], [r_cgs])
                ui = c % 2
                y1, r_y1 = tmp32()
                y2, r_y2 = tmp32()
                for si, (o, L, slot) in enumerate(P.segs):
                    u0 = si * (L + 2)
                    cp("pool", ut[:, ui, u0:u0 + 2], uh[:, slot, c, :], [r_uh], [r_ut[ui]])
                    tt("dve", ut[:, ui, u0 + 2:u0 + 2 + L], pxi[:, o:o + L], cgs[:, o:o + L], ALU.mult, [r_pxi, r_cgs], [r_ut[ui]])
                    cp("pool", uh[:, slot, c, :], ut[:, ui, u0 + L:u0 + L + 2], [r_ut[ui]], [r_uh])
                    act(y1[:, o:o + L], ut[:, ui, u0 + 2:u0 + 2 + L], AF.Identity, [r_ut[ui], r_const], [r_y1],
                        bias=cwt[:, c, 3:4], scale=cwt[:, c, 2:3])
                    stt(y2[:, o:o + L], ut[:, ui, u0 + 1:u0 + 1 + L], cwt[:, c, 1:2], y1[:, o:o + L], ALU.mult, ALU.add,
                        [r_ut[ui], r_y1, r_const], [r_y2])
                    stt(y1[:, o:o + L], ut[:, ui, u0:u0 + L], cwt[:, c, 0:1], y2[:, o:o + L], ALU.mult, ALU.add,
                        [r_ut[ui], r_y2, r_const], [r_y1])
                tt("dve", bgc[:, c, 0:T], y1[:, 0:T], pbg[:, 0:T], ALU.mult, [r_y1, r_pbg], [r_bgc] + r_actT)
            w_release(3)

        def stage_D(P):
            tr.tag = f"{P.kind}{P.seq}{P.st}.D"
            T = P.T
            woc = None
            for m in range(8):
                wor, r_wor = w_next(("wor", m))
                if m % 2 == 0:
                    woc, r_woc = w_next(("woc", m // 2))
                wgt, r_wgt = w_next(("gate", m))
                pgr, r_pgr = psum()
                pgc, r_pgc = psum()
                for kc in range(8):
                    mm(pgr[:, 0:T], wgt[:, kc * 256: kc * 256 + 128], hT[:, kc, 0:T], kc == 0, kc == 7, [r_wgt, r_hT], [r_pgr], kc == 7)
                for kc in range(8):
                    mm(pgc[:, 0:T], wgt[:, kc * 256 + 128: kc * 256 + 256], hT[:, kc, 0:T], kc == 0, kc == 7, [r_wgt, r_hT], [r_pgc], kc == 7)
                sr, r_sr = tmp32()
                sc, r_sc = tmp32()
                act(sr[:, 0:T], pgr[:, 0:T], AF.Sigmoid, [r_pgr], [r_sr])
                act(sc[:, 0:T], pgc[:, 0:T], AF.Sigmoid, [r_pgc], [r_sc])
                pyc, r_pyc = psum()
                mi = m % 2
                for kc in range(8):
                    mm(pyc[:, 0:T], woc[:, kc * 256 + mi * 128: kc * 256 + (mi + 1) * 128], bgc[:, kc, 0:T], kc == 0, kc == 7,
                       [r_woc, r_bgc], [r_pyc], kc == 7)
                pyr, r_pyr = psum()
                for kc in range(16):
                    mm(pyr[:, 0:T], wor[:, kc * 128:(kc + 1) * 128], gatedT[:, kc, 0:T], kc == 0, kc == 15, [r_wor, r_gated], [r_pyr], kc == 15)
                tt("dve", sc[:, 0:T], sc[:, 0:T], pyc[:, 0:T], ALU.mult, [r_sc, r_pyc], [r_sc])
                tt("dve", sr[:, 0:T], sr[:, 0:T], pyr[:, 0:T], ALU.mult, [r_sr, r_pyr], [r_sr])
                tt("pool", mergedT[:, m, 0:T], sr[:, 0:T], sc[:, 0:T], ALU.add, [r_sr, r_sc], [r_merged, r_v[0], r_v[1]])
                w_release(4 if m % 2 == 1 else 1)

        def stage_E(P):
            tr.tag = f"{P.kind}{P.seq}{P.st}.E"
            par = P.par
            pans = [[w_next(("wout", ch, half)) for half in range(2)] for ch in range(2)]
            for bi, (tok0, nb) in enumerate(P.blocks):
                rx = r_xa[par][bi]
                for ch in range(2):
                    pa, r_pa = psum()
                    for kc in range(8):
                        wp_, r_wp = pans[ch][kc // 4]
                        mm(pa[:nb, :], mergedT[:, kc, tok0:tok0 + nb], wp_[:, (kc % 4) * 512:(kc % 4 + 1) * 512],
                           kc == 0, False, [r_wp, r_merged], [r_pa], False)
                    mm(pa[:nb, :], ones[0:1, 0:nb], browb[0:1, ch * 512:(ch + 1) * 512], False, True, [r_ones, r_browb], [r_pa], True)
                    t_, r_t = tmp32()
                    tt("dve", t_[:nb, :], pa[:nb, :], gtc[:nb, 0, ch * 512:(ch + 1) * 512], ALU.mult, [r_pa, r_gtc], [r_t])
                    tt("dve", xa[:nb, par, bi, ch * 512:(ch + 1) * 512], xa[:nb, par, bi, ch * 512:(ch + 1) * 512], t_[:nb, :], ALU.add,
                       [rx, r_t], [rx])
                for i in range(2):
                    tr.emit("dve", (lambda i, bi, nb: (lambda v: v.bn_stats(out=st6b[:nb, bi, i, :], in_=xa[:nb, par, bi, i * 512:(i + 1) * 512])))(i, bi, nb),
                            [rx, r_statB], [r_statB])
                tr.emit("dve", (lambda bi, nb: (lambda v: v.bn_aggr(out=mv4b[:nb, bi, :], in_=st6b[:nb, bi, :, :].rearrange("p a b -> p (a b)"))))(bi, nb),
                        [r_statB], [r_statB])
            w_release(4)
            k = len(P.blocks)
            nbm = P.blocks[0][1]
            rsqrt_multi(nbm, k, mv4b[:nbm, 0:k, 0], mv4b[:nbm, 0:k, 1], LN_EPS, rsB, [r_statB], r_rsB)
            for bi, (tok0, nb) in enumerate(P.blocks):
                xab = xa[:nb, par, bi, :]
                rx = r_xa[par][bi]
                act(xnb[:nb, :], xab, AF.Identity, [rx, r_rsB], [r_xnb], bias=rsB[:nb, bi, 1:2], scale=rsB[:nb, bi, 0:1])
                ln_part2(P, bi, tok0, nb, 1)
                act(xab, xab, AF.Identity, [rx, r_rsB], [rx], bias=rsB[:nb, bi, 1:2], scale=rsB[:nb, bi, 0:1])
                tt("pool", xab, xab, ptm[:nb, 2, :], ALU.mult, [rx, r_ptm], [rx])
                tt("pool", xab, xab, ptm[:nb, 3, :], ALU.add, [rx, r_ptm], [rx])

        def ln_stats_multi(P, blocks, st6x, mv4x, r_statX, load_src):
            par = P.par
            for bi, (tok0, nb) in enumerate(blocks):
                rx = r_xa[par][bi]
                if load_src is not None:
                    dma("sp", f"xld{par}{bi}", xa[:nb, par, bi, :], load_src(tok0, nb), [], [rx])
                for i in range(2):
                    tr.emit("dve", (lambda i, bi, nb: (lambda v: v.bn_stats(out=st6x[:nb, bi, i, :], in_=xa[:nb, par, bi, i * 512:(i + 1) * 512])))(i, bi, nb),
                            [rx, r_statX], [r_statX])
                tr.emit("dve", (lambda bi, nb: (lambda v: v.bn_aggr(out=mv4x[:nb, bi, :], in_=st6x[:nb, bi, :, :].rearrange("p a b -> p (a b)"))))(bi, nb),
                        [r_statX], [r_statX])

        def make_astep(Pn):
            ablocks = list(enumerate(Pn.blocks)) if Pn is not None else []
            stA = {"i": 0, "phase": 0}

            def a_step():
                if Pn is None:
                    return
                tr.tag = f"{Pn.kind}{Pn.seq}{Pn.st}.A"
                par = Pn.par
                if stA["phase"] == 0:
                    stA["phase"] = 1
                    dma("sp", "rope", rope[:, :, 0:Pn.T], Pn.rope_src, [], [r_rope])
                    ln_stats_multi(Pn, Pn.blocks, st6a, mv4a, r_statA,
                                   lambda tok0, nb: xp[Pn.seq, Pn.st * 512 + tok0: Pn.st * 512 + tok0 + nb, :])
                    k = len(Pn.blocks)
                    rsqrt_multi(128, k, mv4a[:, 0:k, 0], mv4a[:, 0:k, 1], LN_EPS, rsA, [r_statA], r_rsA)
                    return
                if stA["i"] < len(ablocks):
                    bi, (tok0, nb) = ablocks[stA["i"]]
                    stA["i"] += 1
                    xab = xa[:nb, par, bi, :]
                    rx = r_xa[par][bi]
                    act(xnb[:nb, :], xab, AF.Identity, [rx, r_rsA], [r_xnb], bias=rsA[:nb, bi, 1:2], scale=rsA[:nb, bi, 0:1])
                    act(xab, xab, AF.Identity, [rx, r_rsA], [rx], bias=rsA[:nb, bi, 1:2], scale=rsA[:nb, bi, 0:1])
                    tt("pool", xab, xab, ptm[:nb, 0, :], ALU.mult, [rx, r_ptm], [rx])
                    tt("pool", xab, xab, ptm[:nb, 1, :], ALU.add, [rx, r_ptm], [rx])
                    ln_part2(Pn, bi, tok0, nb, 0)

            def a_done():
                return Pn is None or stA["i"] >= len(ablocks)
            return a_step, a_done

        def stage_FG(P, Pn):
            a_step, a_done = make_astep(Pn)
            par = P.par
            T = P.T
            accs = {}

            def g_panel(ch, pi):
                tr.tag = f"{P.kind}{P.seq}{P.st}.G"
                if ch not in accs:
                    accs[ch] = [psum(hold=True) for _ in P.blocks]
                wp_, r_wp = w_next(("down", ch, pi))
                kcs = list(range(pi * 4, min(pi * 4 + 4, NFF)))
                for bi, (tok0, nb) in enumerate(P.blocks):
                    pa, r_pa = accs[ch][bi]
                    for kl, kc in enumerate(kcs):
                        mm(pa[:nb, :], actT[:, kc, tok0:tok0 + nb], wp_[:, kl * 512:(kl + 1) * 512],
                           kc == 0, False, [r_wp, r_actT[kc]], [r_pa],
                           (pi < 5 and bi == len(P.blocks) - 1 and kl == len(kcs) - 1))
                    if pi == 5:
                        mm(pa[:nb, :], ones[0:1, 0:nb], browb[0:1, D + ch * 512: D + (ch + 1) * 512],
                           False, True, [r_ones, r_browb], [r_pa], True)
                w_release()

            def g_evac(ch):
                for bi, (tok0, nb) in enumerate(P.blocks):
                    pa, r_pa = accs[ch][bi]
                    rx = r_xa[par][bi]
                    t_, r_t = tmp32()
                    tt("dve", t_[:nb, :], pa[:nb, :], gtc[:nb, 1, ch * 512:(ch + 1) * 512], ALU.mult, [r_pa, r_gtc], [r_t])
                    tt("pool", xa[:nb, par, bi, ch * 512:(ch + 1) * 512], xa[:nb, par, bi, ch * 512:(ch + 1) * 512], t_[:nb, :], ALU.add,
                       [rx, r_t], [rx])
                    psum_unhold(r_pa)

            a_step_real = a_step
            if P.idx == 0:
                a_step = lambda: None
            for f in range(NFF):
                if f == 16:
                    a_step()
                tr.tag = f"{P.kind}{P.seq}{P.st}.F"
                wf, r_wf = w_next(("up", f))
                pa_, r_pa = psum()
                pg_, r_pg = psum()
                for kc in range(8):
                    mm(pa_[:, 0:T], wf[:, kc * 256: kc * 256 + 128], hT[:, kc, 0:T], kc == 0, kc == 7, [r_wf, r_hT], [r_pa], kc == 7)
                for kc in range(8):
                    mm(pg_[:, 0:T], wf[:, kc * 256 + 128: kc * 256 + 256], hT[:, kc, 0:T], kc == 0, kc == 7, [r_wf, r_hT], [r_pg], kc == 7)
                w_release()
                ui = f % 2
                y1, r_y1 = tmp32()
                y2, r_y2 = tmp32()
                for si, (o, L, slot) in enumerate(P.segs):
                    u0 = si * (L + 2)
                    cp("pool", ut[:, ui, u0:u0 + 2], ah[:, slot, f, :], [r_ah], [r_ut[ui]])
                    cp("act", ut[:, ui, u0 + 2:u0 + 2 + L], pa_[:, o:o + L], [r_pa], [r_ut[ui]])
                    cp("pool", ah[:, slot, f, :], ut[:, ui, u0 + L:u0 + L + 2], [r_ut[ui]], [r_ah])
                    act(y1[:, o:o + L], pa_[:, o:o + L], AF.Identity, [r_pa, r_const], [r_y1], bias=fwt[:, f, 3:4], scale=fwt[:, f, 2:3])
                    stt(y2[:, o:o + L], ut[:, ui, u0 + 1:u0 + 1 + L], fwt[:, f, 1:2], y1[:, o:o + L], ALU.mult, ALU.add,
                        [r_ut[ui], r_y1, r_const], [r_y2])
                    stt(y1[:, o:o + L], ut[:, ui, u0:u0 + L], fwt[:, f, 0:1], y2[:, o:o + L], ALU.mult, ALU.add,
                        [r_ut[ui], r_y2, r_const], [r_y1])
                act(y2[:, 0:T], y1[:, 0:T], AF.Gelu, [r_y1], [r_y2])
                tt("dve", actT[:, f, 0:T], y2[:, 0:T], pg_[:, 0:T], ALU.mult, [r_y2, r_pg], [r_actT[f], r_gated, r_bgc])
                if f >= 18:
                    g_panel(0, f - 18)
            g_panel(0, 4)
            a_step()
            g_panel(0, 5)
            g_evac(0)
            g_panel(1, 0)
            a_step()
            g_panel(1, 1)
            g_panel(1, 2)
            a_step()
            g_panel(1, 3)
            g_panel(1, 4)
            a_step()
            g_panel(1, 5)
            g_evac(1)
            while not a_done():
                a_step_real()
            deferred.append(lambda: ln2_tail(P))

        def ln2_tail(P):
            par = P.par
            tr.tag = f"{P.kind}{P.seq}{P.st}.G"
            ln_stats_multi(P, P.blocks, st6b, mv4b, r_statB, None)
            k = len(P.blocks)
            nbm = P.blocks[0][1]
            rsqrt_multi(nbm, k, mv4b[:nbm, 0:k, 0], mv4b[:nbm, 0:k, 1], LN_EPS, rsB, [r_statB], r_rsB)
            for bi, (tok0, nb) in enumerate(P.blocks):
                xab = xa[:nb, par, bi, :]
                rx = r_xa[par][bi]
                act(xab, xab, AF.Identity, [rx, r_rsB], [rx], bias=rsB[:nb, bi, 1:2], scale=rsB[:nb, bi, 0:1])
                tt("dve", xab, xab, ptm[:nb, 4, :], ALU.mult, [rx, r_ptm], [rx])
                tt("pool", xab, xab, ptm[:nb, 5, :], ALU.add, [rx, r_ptm], [rx])
                r_o = Res("y_o")
                if P.kind == "prompt":
                    dst = yp[P.seq, P.st * 512 + tok0: P.st * 512 + tok0 + nb, :]
                else:
                    dst = ys
                dma("sp", f"yst{par}{bi}", dst, xab, [rx], [r_o])
                out_res.append(r_o)
            if P.last:
                for (o, L, slot) in P.segs:
                    r_o = Res("cs_o")
                    dma("sp", f"stcs{slot}", cs_o[slot], uh[:, slot, :, :], [r_uh], [r_o])
                    out_res.append(r_o)
                    r_o = Res("fs_o")
                    dma("sp", f"stfs{slot}", fs_o[slot], ah[:, slot, :, :], [r_ah], [r_o])
                    out_res.append(r_o)

        deferred = []

        def run_deferred():
            while deferred:
                deferred.pop(0)()

        def stage_BC(P):
            order = [("P1", 0), ("P2", 0), ("P1", 1), ("R1", 0), ("P2", 1), ("R2a", 0), ("P1", 2), ("R2b", 0), ("R1", 1), ("P2", 2),
                     ("R2a", 1), ("P1", 3), ("R2b", 1), ("R1", 2), ("P2", 3), ("R2a", 2), ("C", 0), ("R2b", 2), ("R1", 3), ("C", 1),
                     ("R2a", 3), ("C", 2), ("R2b", 3), ("C", 3)]
            fn = {"P1": task_P1, "P2": task_P2, "R1": task_R1, "R2a": task_R2a, "R2b": task_R2b, "C": task_C}
            for (k, a) in order:
                fn[k](P, a)
                if (k, a) == ("P2", 0):
                    run_deferred()

        passes = [make_pass(0, "sample", 0, 0)]
        for seq in range(2):
            for st in range(4):
                passes.append(make_pass(len(passes), "prompt", seq, st))
        stage_A(passes[0])
        for i, P in enumerate(passes):
            stage_pre(P)
            stage_BC(P)
            stage_D(P)
            stage_E(P)
            stage_FG(P, passes[i + 1] if i + 1 < len(passes) else None)
        run_deferred()

        assert wst["next_use"] == TOTAL, (wst["next_use"], TOTAL)
        tr.wait_all("sp", out_res)
        tr.run(es)
    global _LAST_TRACKER
    _LAST_TRACKER = tr
    return nc, dbg_outs, pan_order


def _lhs_panel(W, cols):
    sub = W[:, cols]
    return np.ascontiguousarray(sub.reshape(8, 128, 256).transpose(1, 0, 2).reshape(128, 2048))


def _rhs_panel(W, kcs, cols):
    out = np.zeros((128, 4, 512), np.float32)
    for i, kc in enumerate(kcs):
        out[:, i, :] = W[kc * 128:(kc + 1) * 128, cols]
    return out.reshape(128, 2048)


def _build_wpan(order, w_in, w_o_ret, w_o_conv, w_out, w_up, w_down):
    a128 = np.arange(128)
    a256 = np.arange(256)
    a512 = np.arange(512)
    pans = []
    for key in order:
        k = key[0]
        if k in ("q", "k"):
            h = key[1]
            perm = np.concatenate([h * 256 + 2 * a128, h * 256 + 2 * a128 + 1])
            pans.append(_lhs_panel(w_in, perm + (0 if k == "q" else 1024)))
        elif k == "v":
            h, half = key[1], key[2]
            pans.append(_rhs_panel(w_in, list(range(4 * half, 4 * half + 4)), 2048 + h * 512 + a512))
        elif k == "g":
            h, j2 = key[1], key[2]
            pans.append(_lhs_panel(w_in, 4096 + h * 512 + j2 * 256 + a256))
        elif k == "cx":
            c = key[1]
            pans.append(_lhs_panel(w_in, np.concatenate([7168 + c * 128 + a128, 8192 + c * 128 + a128])))
        elif k == "bg":
            pans.append(_lhs_panel(w_in, 6144 + key[1] * 256 + a256))
        elif k == "wor":
            m = key[1]
            wor = w_o_ret[:, m * 128:(m + 1) * 128].reshape(16, 128, 128).transpose(1, 0, 2).reshape(128, 2048)
            pans.append(np.ascontiguousarray(wor))
        elif k == "woc":
            pans.append(_lhs_panel(w_o_conv, key[1] * 256 + a256))
        elif k == "gate":
            m = key[1]
            pans.append(_lhs_panel(w_in, np.concatenate([9216 + m * 128 + a128, 10240 + m * 128 + a128])))
        elif k == "wout":
            ch, half = key[1], key[2]
            pans.append(_rhs_panel(w_out, list(range(4 * half, 4 * half + 4)), ch * 512 + a512))
        elif k == "up":
            f = key[1]
            pans.append(_lhs_panel(w_up, np.concatenate([f * 128 + a128, DFF + f * 128 + a128])))
        elif k == "down":
            ch, i = key[1], key[2]
            pans.append(_rhs_panel(w_down, list(range(4 * i, min(4 * i + 4, NFF))), ch * 512 + a512))
        else:
            raise KeyError(key)
    assert len(pans) == NPAN
    return np.stack(pans).astype(np.float32)


def _consts():
    half = 128
    inv_freq = (np.float32(10000.0) ** (-(np.arange(half, dtype=np.float32) / np.float32(half)))).astype(np.float32)

    def tab(pos):
        ang = (pos.astype(np.float32)[None, :] * inv_freq[:, None]).astype(np.float32)
        a64 = ang.astype(np.float64)
        return np.stack([np.cos(a64), np.sin(a64)], axis=1).astype(np.float32)

    rope_p = tab(np.arange(SEQ))
    ps = PAST + np.arange(DEC_SEQ)
    rope_s = tab(np.concatenate([ps, ps]))
    g = np.array(GAMMA, np.float64)
    j = np.arange(128)
    maskT = np.zeros((128, H, 128), np.float64)
    offs = np.zeros((128, H, 4), np.float64)
    epsv = np.zeros((128, 4, H), np.float64)
    kdec = np.zeros((128, 2, 4, H), np.float64)
    for h in range(H):
        for jb in range(4):
            col = g[h] ** (-(128.0 * jb + j + 1.0)) / 16.0
            if jb == 0:
                maskT[:, h, :] = col[:, None] * (j[None, :] >= j[:, None])
            offs[:, h, jb] = col
            epsv[:, jb, h] = LN_EPS * g[h] ** (-2.0 * (128.0 * jb + j + 1.0))
            kdec[:, 0, jb, h] = g[h] ** (511.0 - 128.0 * jb - j) / 16.0
        kdec[:32, 1, 0, h] = g[h] ** (31.0 - j[:32]) / 16.0
    return dict(rope_p=rope_p, rope_s=rope_s, maskT=maskT.astype(ml_dtypes.bfloat16),
                epsv=epsv.astype(np.float32), kdec=kdec.astype(np.float32), offs=offs.astype(np.float32),
                identb=np.eye(128).astype(ml_dtypes.bfloat16), identf=np.eye(128, dtype=np.float32))


_CACHE = {}


def kernel(x_prompt, x_sample, c_prompt, c_sample, state_retention, state_shortconv, state_ffn_conv,
           ln_in_g, ln_in_b, w_mod, b_mod, w_in, w_o_ret, conv_w, conv_b, w_o_conv, w_out, b_out,
           ln1_g, ln1_b, w_up, ffn_conv_w, ffn_conv_b, w_down, b_down, ln2_g, ln2_b, _debug=()):
    f = lambda a: np.asarray(a, dtype=np.float32)
    x_prompt, x_sample, c_prompt, c_sample = f(x_prompt), f(x_sample), f(c_prompt), f(c_sample)
    state_retention, state_shortconv, state_ffn_conv = f(state_retention), f(state_shortconv), f(state_ffn_conv)
    key = tuple(_debug)
    if key not in _CACHE:
        _CACHE[key] = build_program(debug=_debug)
    nc, dbg_outs, pan_order = _CACHE[key]

    shared = _consts()
    shared["wpan"] = _build_wpan(pan_order, f(w_in)[0], f(w_o_ret)[0], f(w_o_conv)[0], f(w_out)[0], f(w_up)[0], f(w_down)[0])
    wm = f(w_mod)[0]
    shared["wmod"] = np.ascontiguousarray(wm.reshape(8, 128, 48, 128).transpose(2, 1, 0, 3).reshape(48, 128, 1024))
    shared["bmod"] = np.ascontiguousarray(f(b_mod)[0].reshape(48, 128).T)
    vecs = np.stack([f(ln_in_g), f(ln_in_b), f(ln1_g)[0], f(ln1_b)[0], f(ln2_g)[0], f(ln2_b)[0]])
    shared["vtm"] = np.ascontiguousarray(np.broadcast_to(vecs[None], (128, 6, D)))
    shared["vfm"] = np.ascontiguousarray(vecs[:4].reshape(4, 8, 128).transpose(2, 0, 1))
    cwa = np.concatenate([f(conv_w)[0], f(conv_b)], axis=0)
    shared["cw"] = np.ascontiguousarray(cwa.reshape(4, 8, 128).transpose(2, 1, 0))
    fwa = np.concatenate([f(ffn_conv_w)[0], f(ffn_conv_b)], axis=0)
    shared["fw"] = np.ascontiguousarray(fwa.reshape(4, NFF, 128).transpose(2, 1, 0))
    shared["brow"] = np.concatenate([f(b_out)[0], f(b_down)[0]])[None, :].copy()

    in_maps = []
    for i in range(8):
        m = dict(shared)
        m["xp"] = np.ascontiguousarray(x_prompt[2 * i:2 * i + 2])
        m["xs"] = np.ascontiguousarray(x_sample[2 * i:2 * i + 2].reshape(64, D))
        c4 = np.concatenate([c_prompt[2 * i:2 * i + 2], c_sample[2 * i:2 * i + 2]], axis=0)
        m["cT"] = np.ascontiguousarray(c4.reshape(4, 8, 128).transpose(2, 1, 0))
        sr = state_retention[0, 2 * i:2 * i + 2]
        m["sret"] = np.ascontiguousarray(sr.reshape(2, H, 128, 2, DV).transpose(0, 1, 3, 2, 4))
        sc = state_shortconv[0, 2 * i:2 * i + 2]
        m["sconv"] = np.ascontiguousarray(sc.reshape(2, 2, 8, 128).transpose(0, 3, 2, 1))
        sf = state_ffn_conv[0, 2 * i:2 * i + 2]
        m["sffn"] = np.ascontiguousarray(sf.reshape(2, 2, NFF, 128).transpose(0, 3, 2, 1))
        in_maps.append(m)

    res = run_bass_kernel_spmd(nc, in_maps, core_ids=list(range(8)))
    R = res.results
    y_prompt = np.concatenate([r["yp"] for r in R], axis=0)
    y_sample = np.concatenate([r["ys"].reshape(2, DEC_SEQ, D) for r in R], axis=0)

    def ret_state(lo):
        outs = []
        for r in R:
            a = r["rs_o"][lo:lo + 2]
            outs.append(a.transpose(0, 1, 3, 2, 4).reshape(2, H, DK, DV))
        return np.concatenate(outs, axis=0)[None]

    def hist_state(name, lo, nch):
        outs = []
        for r in R:
            a = r[name][lo:lo + 2]
            outs.append(a.transpose(0, 3, 2, 1).reshape(2, 2, nch * 128))
        return np.concatenate(outs, axis=0)[None]

    outs = (y_prompt, y_sample,
            ret_state(0), hist_state("cs_o", 0, 8), hist_state("fs_o", 0, NFF),
            ret_state(2), hist_state("cs_o", 2, 8), hist_state("fs_o", 2, NFF))
    outs = tuple(np.ascontiguousarray(o, dtype=np.float32) for o in outs)
    if _debug:
        return outs, [{k: r["dbg_" + k] for k in dbg_outs} for r in R]
    return outs
```

```python
import numpy as np
import ml_dtypes
from contextlib import ExitStack
import concourse.bass as bass
import concourse.mybir as mybir
from concourse.bass_utils import run_bass_kernel_spmd

F32 = mybir.dt.float32
BF16 = mybir.dt.bfloat16
I32 = mybir.dt.int32
AF = mybir.ActivationFunctionType
ALU = mybir.AluOpType

D = 1024
H = 4
DK = 256
DV = 512
DFF = 2816
NFF = 22
SEQ = 2048
DEC_SEQ = 32
PAST = 2048
LN_EPS = 1e-5
ALPHA = 2.0 ** 0.25
NPAN = 94
NS = 7
NCAST = 24
GAMMA = [1.0 - 2.0 ** (-5.0 - h) for h in range(H)]


class Res:
    __slots__ = ("name", "writer", "readers")

    def __init__(self, name):
        self.name = name
        self.writer = None
        self.readers = {}

    def absorb(self, others):
        for o in others:
            if o.writer is not None:
                self.readers[o.writer[0]] = max(self.readers.get(o.writer[0], 0), o.writer[1])
            for k, v in o.readers.items():
                self.readers[k] = max(self.readers.get(k, 0), v)


class Eng:
    def __init__(self, name):
        self.name = name
        self.count = 0
        self.pending = False
        self.waited = {}
        self.prog = []


class Tracker:
    ENGS = ("pe", "act", "dve", "pool", "sp")

    def __init__(self, nc):
        self.nc = nc
        self.engs = {n: Eng(n) for n in self.ENGS}
        self.sems = {}
        self.dma_vals = {}
        self.ninst = 0

    def _need(self, e, key, val):
        if e.waited.get(key, 0) >= val:
            return
        if key in self.engs:
            src = self.engs[key]
            if val > src.count:
                raise RuntimeError(f"dep on future inc: {e.name} waits {key}>={val} but count={src.count}")
        e.waited[key] = val
        e.prog.append(("wait", key, val))

    def _deps(self, e, reads, writes, same_engine_ok):
        for r in reads:
            if r.writer is not None:
                if not (same_engine_ok and r.writer[0] == e.name):
                    self._need(e, *r.writer)
        for w in writes:
            if w.writer is not None:
                if not (same_engine_ok and w.writer[0] == e.name):
                    self._need(e, *w.writer)
            for k, v in w.readers.items():
                if k == e.name:
                    continue
                self._need(e, k, v)

    def emit(self, eng, fn, reads=(), writes=(), inc=True, same_engine_ok=False):
        e = self.engs[eng]
        self._deps(e, reads, writes, same_engine_ok)
        val = e.count + 1
        e.prog.append(("inst", fn, inc, getattr(self, "tag", "")))
        if inc:
            e.count = val
            e.pending = False
        else:
            e.pending = True
        for w in writes:
            w.writer = (eng, val)
            w.readers = {}
        for r in reads:
            r.readers[eng] = max(r.readers.get(eng, 0), val)
        self.ninst += 1

    def dma(self, eng, semkey, fn, reads=(), writes=()):
        e = self.engs[eng]
        self._deps(e, reads, writes, False)
        val = self.dma_vals.get(semkey, 0) + 16
        self.dma_vals[semkey] = val
        e.prog.append(("dma", fn, semkey))
        for w in writes:
            w.writer = (semkey, val)
            w.readers = {}
        for r in reads:
            r.readers[semkey] = max(r.readers.get(semkey, 0), val)
        self.ninst += 1

    def wait_all(self, eng, resources):
        e = self.engs[eng]
        for r in resources:
            if r.writer is not None:
                self._need(e, *r.writer)
            for k, v in r.readers.items():
                self._need(e, k, v)

    def run(self, es):
        nc = self.nc
        keys = list(self.ENGS) + sorted(self.dma_vals.keys())
        for k in keys:
            self.sems[k] = es.enter_context(nc.semaphore("s_" + k))
        for e in self.engs.values():
            if e.pending:
                raise RuntimeError(f"engine {e.name} has trailing non-inc instruction")
        block = es.enter_context(nc.Block())
        sems = self.sems

        def runner(e):
            def body(h):
                mysem = sems[e.name]
                for item in e.prog:
                    if item[0] == "wait":
                        h.wait_ge(sems[item[1]], item[2])
                    elif item[0] == "inst":
                        ins = item[1](h)
                        if item[2]:
                            ins.then_inc(mysem, 1)
                    else:
                        ins = item[1](h)
                        ins.then_inc(sems[item[2]], 16)
            return body

        block.tensor(runner(self.engs["pe"]))
        block.scalar(runner(self.engs["act"]))
        block.vector(runner(self.engs["dve"]))
        block.gpsimd(runner(self.engs["pool"]))
        block.sync(runner(self.engs["sp"]))


def build_program(debug=()):
    nc = bass.Bass("TRN2", target_bir_lowering=False)
    tr = Tracker(nc)

    def din(name, shape, dt=F32):
        return nc.dram_tensor(name, list(shape), dt, kind="ExternalInput").ap()

    def dout(name, shape, dt=F32):
        return nc.dram_tensor(name, list(shape), dt, kind="ExternalOutput").ap()

    xp = din("xp", [2, SEQ, D])
    xs = din("xs", [64, D])
    cT = din("cT", [128, 8, 4])
    sret = din("sret", [2, H, 2, 128, DV])
    sconv = din("sconv", [2, 128, 8, 2])
    sffn = din("sffn", [2, 128, NFF, 2])
    wpan = din("wpan", [NPAN, 128, 2048])
    wmod = din("wmod", [48, 128, 1024])
    bmod = din("bmod", [128, 48])
    vtm = din("vtm", [128, 6, D])
    vfm = din("vfm", [128, 4, 8])
    cw = din("cw", [128, 8, 4])
    fw = din("fw", [128, NFF, 4])
    brow = din("brow", [1, 2 * D])
    rope_p = din("rope_p", [128, 2, SEQ])
    rope_s = din("rope_s", [128, 2, 64])
    maskT = din("maskT", [128, H, 128], BF16)
    epsv = din("epsv", [128, 4, H])
    kdec = din("kdec", [128, 2, 4, H])
    offs = din("offs", [128, H, 4])
    identb = din("identb", [128, 128], BF16)
    identf = din("identf", [128, 128])

    wbf = nc.dram_tensor("wbf", [NPAN, 128, 2048], BF16, kind="Internal").ap()

    yp = dout("yp", [2, SEQ, D])
    ys = dout("ys", [64, D])
    rs_o = dout("rs_o", [4, H, 2, 128, DV])
    cs_o = dout("cs_o", [4, 128, 8, 2])
    fs_o = dout("fs_o", [4, 128, NFF, 2])
    out_res = []
    dbg_outs = {}

    es = ExitStack()
    with es:
        def sb(name, shape, dt=F32):
            return es.enter_context(nc.sbuf_tensor(name, list(shape), dt))

        ring = sb("ring", [128, NS, 2048], BF16)
        r_ring = [Res(f"ring{i}") for i in range(NS)]
        xa = sb("xa", [128, 2, 4, D])
        r_xa = [[Res(f"xa{p}{b}") for b in range(4)] for p in range(2)]
        ptm = sb("ptm", [128, 6, D])
        r_ptm = Res("ptm")
        gtc = sb("gtc", [128, 2, D])
        r_gtc = Res("gtc")
        S_p = sb("S_p", [128, H, 2, DV])
        r_S = [Res(f"S{h}") for h in range(H)]
        Sbf = sb("Sbf", [128, 2, 2, DV], BF16)
        r_Sbf = [Res("Sbf0"), Res("Sbf1")]
        rope = sb("rope", [128, 2, 512])
        r_rope = Res("rope")
        hT = sb("hT", [128, 8, 512], BF16)
        r_hT = Res("hT")
        qk = sb("qk", [128, 2, 2, 2, 512], BF16)
        r_qk = [Res("qk0"), Res("qk1")]
        mvbuf = sb("mvbuf", [128, 4096], BF16)
        vsb = mvbuf[:].rearrange("p (b r d) -> p b r d", b=2, r=4)
        r_v = [Res("v0"), Res("v1")]
        sgb = sb("sgb", [128, 2, 4, 512], BF16)
        r_sg = [Res("sg0"), Res("sg1")]
        ktok = sb("ktok", [128, 4, 256], BF16)
        r_ktok = Res("ktok")
        smT = sb("smT", [128, 10, 128], BF16)
        r_smT = Res("smT")
        onb = sb("onb", [128, 4, DV], BF16)
        r_onb = [Res(f"onb{i}") for i in range(4)]
        gbuf = sb("gbuf", [128, 12288], BF16)
        gatedT = gbuf[:, 0:8192].rearrange("p (c t) -> p c t", t=512)
        r_gated = Res("gatedT")
        bgc = gbuf[:, 8192:12288].rearrange("p (c t) -> p c t", t=512)
        r_bgc = Res("bgc")
        mergedT = mvbuf[:].rearrange("p (c t) -> p c t", t=512)
        r_merged = Res("mergedT")
        actT = gbuf[:, 0:NFF * 512].rearrange("p (c t) -> p c t", t=512)
        r_actT = [Res(f"actT{f}") for f in range(NFF)]
        xnb2 = sb("xnb2", [128, 2, D], BF16)
        xnb = xnb2[:, 0, :]
        r_xnb = Res("xnb")
        r_xnb1 = Res("xnb1")
        tmpA = sb("tmpA", [128, 4, 512])
        r_tmpA = [Res(f"tmpA{i}") for i in range(4)]
        ut = sb("ut", [128, 2, 520])
        r_ut = [Res("ut0"), Res("ut1")]
        uh = sb("uh", [128, 4, 8, 2])
        r_uh = Res("uh")
        ah = sb("ah", [128, 4, NFF, 2])
        r_ah = Res("ah")
        modfm = sb("modfm", [128, 48, 4])
        r_modfm = Res("modfm")
        mA = sb("mA", [128, 4, 8, 4])
        r_mA = Res("mA")
        cTt = sb("cTt", [128, 8, 4])
        vfmt = sb("vfmt", [128, 4, 8])
        cwt = sb("cwt", [128, 8, 4])
        fwt = sb("fwt", [128, NFF, 4])
        bmt = sb("bmt", [128, 48])
        browb = sb("browb", [1, 2 * D], BF16)
        ones = sb("ones", [1, 128], BF16)
        maskt = sb("maskt", [128, H, 128], BF16)
        epst = sb("epst", [128, 4, H])
        kdt = sb("kdt", [128, 2, 4, H])
        offt = sb("offt", [128, H, 4])
        idb = sb("idb", [128, 128], BF16)
        idf = sb("idf", [128, 128])
        bcl = sb("bcl", [128, 128])
        r_bcl = Res("bcl")
        r_const = Res("const")
        r_browb = Res("browb")
        r_ones = Res("ones")
        st6 = sb("st6", [128, 2, 6])
        mvt = sb("mvt", [128, 2])
        rsq = sb("rsq", [128, 8])
        rsqi = sb("rsqi", [128, 2], I32)
        rsq4 = sb("rsq4", [128, 3, 4])
        rsq4i = sb("rsq4i", [128, 4], I32)
        mv4 = sb("mv4", [128, 4, 2])
        st64 = sb("st64", [128, 4, 6])
        rs4 = sb("rs4", [128, 2, 4, 2])
        r_rs4 = [Res("rs4a"), Res("rs4b")]
        st6a = sb("st6a", [128, 4, 2, 6])
        mv4a = sb("mv4a", [128, 4, 2])
        rsA = sb("rsA", [128, 4, 2])
        r_rsA = Res("rsA")
        r_statA = Res("statA")
        st6b = sb("st6b", [128, 4, 2, 6])
        mv4b = sb("mv4b", [128, 4, 2])
        rsB = sb("rsB", [128, 4, 2])
        r_rsB = Res("rsB")
        r_statB = Res("statB")
        st6c = sb("st6c", [128, 4, 2, 6])
        mv4c = sb("mv4c", [128, 4, 2])
        rsC = sb("rsC", [128, 4, 2])
        r_rsC = Res("rsC")
        r_statC = Res("statC")
        r_stat = Res("stat")
        rstd_t = sb("rstd_t", [128, 4, 2])
        r_rstd = [Res(f"rstd{i}") for i in range(4)]
        wm32 = tmpA[:].rearrange("p a b -> p (a b)").rearrange("p (b f) -> p b f", f=1024)
        r_wm = [Res("wm0"), Res("wm1")]

        psb = [es.enter_context(nc.psum_tensor(f"ps{i}", [128, 512], F32)) for i in range(8)]
        r_ps = [Res(f"ps{i}") for i in range(8)]
        ps_ctr = [0]

        ps_held = set()

        def psum(hold=False):
            while True:
                i = ps_ctr[0] % 8
                ps_ctr[0] += 1
                if i not in ps_held:
                    break
            if hold:
                ps_held.add(i)
            return psb[i], r_ps[i]

        def psum_unhold(r):
            ps_held.discard(r_ps.index(r))

        rot = {"tmp": 0, "rstd": 0}

        def tmp32():
            i = rot["tmp"] % 4
            rot["tmp"] += 1
            return tmpA[:, i, :], r_tmpA[i]

        def rstd_slot():
            i = rot["rstd"] % 4
            rot["rstd"] += 1
            return rstd_t[:, i, :], r_rstd[i]

        def mm(out, lhsT, rhs, start, stop, reads, writes, inc):
            tr.emit("pe", lambda t: t.matmul(out, lhsT=lhsT, rhs=rhs, start=start, stop=stop),
                    reads, writes, inc=inc, same_engine_ok=True)

        def transp(out, in_, ident, reads, writes, inc):
            tr.emit("pe", lambda t: t.transpose(out=out, in_=in_, identity=ident),
                    reads, writes, inc=inc, same_engine_ok=True)

        def act(out, in_, func, reads, writes, bias=None, scale=None):
            kw = {}
            if bias is not None:
                kw["bias"] = bias
            if scale is not None:
                kw["scale"] = scale
            tr.emit("act", lambda a: a.activation(out=out, in_=in_, func=func, **kw), reads, writes)

        def tt(eng, out, in0, in1, op, reads, writes):
            tr.emit(eng, lambda v: v.tensor_tensor(out=out, in0=in0, in1=in1, op=op), reads, writes)

        def ts(eng, out, in0, s1, s2, op0, op1, reads, writes):
            if op1 is None:
                tr.emit(eng, lambda v: v.tensor_scalar(out=out, in0=in0, scalar1=s1, scalar2=None, op0=op0), reads, writes)
            else:
                tr.emit(eng, lambda v: v.tensor_scalar(out=out, in0=in0, scalar1=s1, scalar2=s2, op0=op0, op1=op1), reads, writes)

        def stt(out, in0, scalar, in1, op0, op1, reads, writes):
            tr.emit("dve", lambda v: v.scalar_tensor_tensor(out=out, in0=in0, scalar=scalar, in1=in1, op0=op0, op1=op1), reads, writes)

        def cp(eng, out, in_, reads, writes):
            if eng == "act":
                tr.emit("act", lambda a: a.copy(out=out, in_=in_), reads, writes)
            else:
                tr.emit(eng, lambda v: v.tensor_copy(out=out, in_=in_), reads, writes)

        def dma(eng, key, out, in_, reads, writes):
            tr.dma(eng, key, lambda g: g.dma_start(out=out, in_=in_), reads, writes)

        def dbg(name, ap, shape, res, dt=F32):
            if name not in debug:
                return
            o = dout("dbg_" + name, shape, dt)
            r = Res("dbg_" + name)
            dma("sp", "dbg", o, ap, [res], [r])
            out_res.append(r)
            dbg_outs[name] = shape

        def rsqrt(n, mean_ap, var_ap, eps, rs_ap, reads, r_rs):
            xe = rsq[:n, 0:1]
            yy = rsq[:n, 1:2]
            t_ = rsq[:n, 2:3]
            ti = rsqi[:n, 0:1]
            if isinstance(eps, float):
                ts("dve", xe, var_ap, eps, None, ALU.add, None, reads + [r_stat], [r_stat])
            else:
                tt("dve", xe, var_ap, eps, ALU.add, reads + [r_stat, r_const], [r_stat])
            ts("dve", ti, xe.bitcast(I32), 1, None, ALU.arith_shift_right, None, [r_stat], [r_stat])
            ts("dve", yy.bitcast(I32), ti, -1.0, 1597463007.0, ALU.mult, ALU.add, [r_stat], [r_stat])
            for it in range(3):
                stt(t_, xe, yy, yy, ALU.mult, ALU.mult, [r_stat], [r_stat])
                ts("dve", t_, t_, -0.5, 1.5, ALU.mult, ALU.add, [r_stat], [r_stat])
                if it < 2:
                    tt("dve", yy, yy, t_, ALU.mult, [r_stat], [r_stat])
                else:
                    tt("dve", rs_ap[:n, 0:1], yy, t_, ALU.mult, [r_stat], [r_stat, r_rs])
            stt(rs_ap[:n, 1:2], mean_ap, -1.0, rs_ap[:n, 0:1], ALU.mult, ALU.mult, reads + [r_stat, r_rs], [r_rs])

        def rsqrt_multi(n, k, mean_ap, var_ap, eps_ap, rs3, reads, r_rs):
            xe = rsq4[:n, 0, 0:k]
            yy = rsq4[:n, 1, 0:k]
            t_ = rsq4[:n, 2, 0:k]
            ti = rsq4i[:n, 0:k]
            if isinstance(eps_ap, float):
                ts("dve", xe, var_ap, eps_ap, None, ALU.add, None, reads + [r_stat], [r_stat])
            else:
                tt("dve", xe, var_ap, eps_ap, ALU.add, reads + [r_stat, r_const], [r_stat])
            ts("dve", ti, xe.bitcast(I32), 1, None, ALU.arith_shift_right, None, [r_stat], [r_stat])
            ts("dve", yy.bitcast(I32), ti, -1.0, 1597463007.0, ALU.mult, ALU.add, [r_stat], [r_stat])
            for it in range(3):
                tt("dve", t_, xe, yy, ALU.mult, [r_stat], [r_stat])
                tt("dve", t_, t_, yy, ALU.mult, [r_stat], [r_stat])
                ts("dve", t_, t_, -0.5, 1.5, ALU.mult, ALU.add, [r_stat], [r_stat])
                if it < 2:
                    tt("dve", yy, yy, t_, ALU.mult, [r_stat], [r_stat])
                else:
                    tt("dve", rs3[:n, 0:k, 0], yy, t_, ALU.mult, [r_stat], [r_stat, r_rs])
            stt(rs3[:n, 0:k, 1], mean_ap, -1.0, rs3[:n, 0:k, 0], ALU.mult, ALU.mult, reads + [r_stat, r_rs], [r_rs])

        r_cast = [Res(f"cast{g}") for g in range(NCAST)]
        wst = {"next_load": 0, "next_use": 0}
        NPASS = 9
        TOTAL = NPASS * NPAN

        r_wbf = [Res(f"wbf{i}") for i in range(NPAN)]
        wunits = [(xa[:, 1, b, :], r_xa[1][b]) for b in range(4)] + [(xa[:, 0, b, :], r_xa[0][b]) for b in range(1, 4)]
        pend_store = []

        def w_flush_store(keep):
            while len(pend_store) > keep:
                pidx, slot = pend_store.pop(0)
                dma("sp", f"wst{slot}", wbf[pidx], ring[:, slot, :], [r_ring[slot]], [r_wbf[pidx]])

        pend_cast = []

        def emit_cast(g):
            slot = g % NS
            pidx = g % NPAN
            for hf in range(2):
                u = (2 * g + hf) % len(wunits)
                uap, ures = wunits[u]
                cp("act" if hf == 0 else "dve", ring[:, slot, hf * 1024:(hf + 1) * 1024], uap, [ures], [r_ring[slot]])
            pend_store.append((pidx, slot))

        def w_load_one():
            g = wst["next_load"]
            if g >= TOTAL:
                return
            wst["next_load"] += 1
            slot = g % NS
            pidx = g % NPAN
            if g < NPAN:
                for hf in range(2):
                    u = (2 * g + hf) % len(wunits)
                    uap, ures = wunits[u]
                    dma("sp", f"wld{u}", uap, wpan[pidx][:, hf * 1024:(hf + 1) * 1024], [], [ures])
                pend_cast.append(g)
                while len(pend_cast) > 2:
                    emit_cast(pend_cast.pop(0))
                w_flush_store(2)
            else:
                while pend_cast:
                    emit_cast(pend_cast.pop(0))
                w_flush_store(0)
                dma("sp", f"ring{slot}", ring[:, slot, :], wbf[pidx], [r_wbf[pidx]], [r_ring[slot]])

        pan_order = []

        def w_next(key):
            g = wst["next_use"]
            wst["next_use"] += 1
            if g < NPAN:
                pan_order.append(key)
            else:
                assert pan_order[g % NPAN] == key, (g, key, pan_order[g % NPAN])
            slot = g % NS
            return ring[:, slot, :], r_ring[slot]

        def w_release(k=1):
            for _ in range(k):
                w_load_one()

        cl = [(cTt[:], cT), (vfmt[:], vfm), (cwt[:], cw), (fwt[:], fw), (bmt[:], bmod),
              (maskt[:], maskT), (epst[:], epsv), (kdt[:], kdec), (offt[:], offs),
              (idb[:], identb), (idf[:], identf)]
        for o, i in cl:
            dma("sp", "const", o, i, [], [r_const])
        dma("sp", "ptm", ptm[:], vtm, [], [r_ptm])
        dma("sp", "hist_u", uh[:, 2:4, :, :], sconv.rearrange("s p c r -> p s c r"), [], [r_uh])
        dma("sp", "hist_a", ah[:, 2:4, :, :], sffn.rearrange("s p c r -> p s c r"), [], [r_ah])
        tr.emit("pool", lambda g: g.memset(uh[:, 0:2, :, :], 0.0), [], [r_uh])
        tr.emit("pool", lambda g: g.memset(ah[:, 0:2, :, :], 0.0), [], [r_ah])
        tr.emit("pool", lambda g: g.memset(ones[:], 1.0), [], [r_ones])
        tr.dma("pool", "browc", lambda q: q.dma_start(out=browb[:], in_=brow), [], [r_browb])
        for i_ in range(4):
            tr.emit("act", (lambda i_: (lambda a: a.activation(out=ptm[:, i_, :], in_=ptm[:, i_, :], func=AF.Copy, scale=float(ALPHA))))(i_),
                    [r_ptm], [r_ptm])
        pm, r_pm = psum()
        wmb = gbuf[:, 0:4096].rearrange("p (b f) -> p b f", f=1024)
        r_wmb = [Res(f"wmb{i}") for i in range(4)]
        wm32 = xa[:, 1, :, :]
        r_wm = [Res(f"wm{i}") for i in range(4)]
        cTb = gbuf[:, 4096:4128].rearrange("p (k s) -> p k s", s=4)
        r_cTb = Res("cTb")
        cp("dve", cTb, cTt[:], [r_const], [r_cTb])
        for oc in range(48):
            b_ = oc % 4
            dma("sp", f"wm{b_}", wm32[:, b_, :], wmod[oc], [], [r_wm[b_]])
            cp("act" if oc % 2 == 0 else "dve", wmb[:, b_, :], wm32[:, b_, :], [r_wm[b_]], [r_wmb[b_]])
            for kc in range(8):
                mm(pm[:, oc * 4:(oc + 1) * 4], wmb[:, b_, kc * 128:(kc + 1) * 128], cTb[:, kc, :],
                   kc == 0, kc == 7, [r_wmb[b_], r_cTb], [r_pm], kc == 7)
        tt("dve", modfm[:], pm[:, 0:192].rearrange("p (a b) -> p a b", b=4),
           bmt[:].unsqueeze(2).broadcast_to([128, 48, 4]), ALU.add, [r_pm, r_const], [r_modfm])
        for which, (gi, sci, shi) in enumerate([(0, 1, 0), (2, 4, 3)]):
            one_sc = tmpA[:, 0, 0:32].rearrange("p (c s) -> p c s", s=4)
            ts("dve", one_sc, modfm[:, sci * 8:(sci + 1) * 8, :], 1.0, None, ALU.add, None, [r_modfm], [r_tmpA[0]])
            tt("dve", mA[:, 2 * which, :, :], one_sc, vfmt[:, gi, :].unsqueeze(2).broadcast_to([128, 8, 4]), ALU.mult,
               [r_tmpA[0], r_const], [r_mA])
            tt("dve", mA[:, 2 * which + 1, :, :], one_sc, vfmt[:, gi + 1, :].unsqueeze(2).broadcast_to([128, 8, 4]), ALU.mult,
               [r_tmpA[0], r_const], [r_mA])
            tt("dve", mA[:, 2 * which + 1, :, :], mA[:, 2 * which + 1, :, :], modfm[:, shi * 8:(shi + 1) * 8, :], ALU.add,
               [r_mA, r_modfm], [r_mA])

        def build_gtc(part_slots):
            for which, mi in enumerate((2, 5)):
                for half in range(2):
                    pg, r_pg = psum()
                    for cc in range(4):
                        c = half * 4 + cc
                        for (p0, p1, slot) in part_slots:
                            cp("dve", bcl[:, p0:p1], modfm[:, mi * 8 + c, slot:slot + 1].broadcast_to([128, p1 - p0]),
                               [r_modfm], [r_bcl])
                        npart = part_slots[-1][1]
                        mm(pg[:npart, cc * 128:(cc + 1) * 128], bcl[:, 0:npart], idf[:], True, True,
                           [r_bcl, r_const], [r_pg], True)
                    npart = part_slots[-1][1]
                    cp("act", gtc[:npart, which, half * 512:(half + 1) * 512], pg[:npart, :], [r_pg], [r_gtc])

        for b_ in range(4):
            r_xa[1][b_].absorb([r_wm[b_]])
        for _ in range(NS):
            w_load_one()

        class Pass:
            pass

        def make_pass(idx, kind, seq, st):
            P = Pass()
            P.idx, P.kind, P.seq, P.st = idx, kind, seq, st
            P.par = idx % 2
            if kind == "prompt":
                P.T = 512
                P.blocks = [(b * 128, 128) for b in range(4)]
                P.segs = [(0, 512, seq)]
                P.n = 128
                P.kd = 0
                P.rope_src = rope_p[:, :, st * 512:(st + 1) * 512]
                P.last = (st == 3)
                P.Lr = 512
            else:
                P.T = 64
                P.blocks = [(0, 64)]
                P.segs = [(0, 32, 2), (32, 32, 3)]
                P.n = 32
                P.kd = 1
                P.rope_src = rope_s
                P.last = True
                P.Lr = 32
            P.gL = [GAMMA[h] ** P.Lr for h in range(H)]
            return P

        def seg_of_block(P, tok0, nb):
            return [(max(o, tok0), min(o + L, tok0 + nb), slot) for (o, L, slot) in P.segs
                    if max(o, tok0) < min(o + L, tok0 + nb)]

        def ln_block(P, bi, tok0, nb, load_from, whichA, do_T, affine_idx):
            ln_part1(P, bi, tok0, nb, load_from, whichA, do_T, affine_idx)
            if do_T:
                ln_part2(P, bi, tok0, nb, whichA)

        def ln_part1(P, bi, tok0, nb, load_from, whichA, do_T, affine_idx):
            par = P.par
            xab = xa[:nb, par, bi, :]
            rx = r_xa[par][bi]
            if load_from is not None:
                dma("sp", f"xld{par}{bi}", xab, load_from, [], [rx])
            for i in range(2):
                tr.emit("dve", (lambda i: (lambda v: v.bn_stats(out=st6[:nb, i, :], in_=xa[:nb, par, bi, i * 512:(i + 1) * 512])))(i),
                        [rx, r_stat], [r_stat])
            tr.emit("dve", lambda v: v.bn_aggr(out=mvt[:nb, :], in_=st6[:nb, :, :].rearrange("p a b -> p (a b)")),
                    [r_stat], [r_stat])
            rs_ap, r_rs = rstd_slot()
            rsqrt(nb, mvt[:nb, 0:1], mvt[:nb, 1:2], LN_EPS, rs_ap, [], r_rs)
            if do_T:
                act(xnb[:nb, :], xab, AF.Identity, [rx, r_rs], [r_xnb], bias=rs_ap[:nb, 1:2], scale=rs_ap[:nb, 0:1])
            act(xab, xab, AF.Identity, [rx, r_rs], [rx], bias=rs_ap[:nb, 1:2], scale=rs_ap[:nb, 0:1])
            tt("pool", xab, xab, ptm[:nb, affine_idx, :], ALU.mult, [rx, r_ptm], [rx])
            tt("pool", xab, xab, ptm[:nb, affine_idx + 1, :], ALU.add, [rx, r_ptm], [rx])

        def ln_part2(P, bi, tok0, nb, whichA, xb=0):
            pt, r_pt = psum()
            ptb = pt[:].bitcast(BF16)
            rxn = r_xnb if xb == 0 else r_xnb1
            for c in range(8):
                transp(ptb[:, c * 128:c * 128 + nb], xnb2[:nb, xb, c * 128:(c + 1) * 128], idb[:nb, :nb],
                       [rxn, r_const], [r_pt], c == 7)
            for c in range(8):
                for (a0, a1, slot) in seg_of_block(P, tok0, nb):
                    if c % 2 == 0:
                        ts("dve", hT[:, c, a0:a1], ptb[:, c * 128 + (a0 - tok0):c * 128 + (a1 - tok0)],
                           mA[:, 2 * whichA, c, slot:slot + 1], mA[:, 2 * whichA + 1, c, slot:slot + 1],
                           ALU.mult, ALU.add, [r_pt, r_mA], [r_hT])
                    else:
                        act(hT[:, c, a0:a1], ptb[:, c * 128 + (a0 - tok0):c * 128 + (a1 - tok0)], AF.Identity,
                            [r_pt, r_mA], [r_hT], bias=mA[:, 2 * whichA + 1, c, slot:slot + 1], scale=mA[:, 2 * whichA, c, slot:slot + 1])

        def stage_A(P):
            tr.tag = f"{P.kind}{P.seq}{P.st}.A"
            dma("sp", "rope", rope[:, :, 0:P.T], P.rope_src, [], [r_rope])
            for bi, (tok0, nb) in enumerate(P.blocks):
                src = xp[P.seq, P.st * 512 + tok0: P.st * 512 + tok0 + nb, :] if P.kind == "prompt" else xs
                ln_block(P, bi, tok0, nb, src, 0, True, 0)

        def stage_pre(P):
            if P.kind == "sample":
                build_gtc([(0, 32, 2), (32, 64, 3)])
            elif P.st == 0:
                build_gtc([(0, 128, P.seq)])
                for h in range(H):
                    tr.emit("pool", (lambda h: (lambda g: g.memset(S_p[:, h, :, :], 0.0)))(h), [], [r_S[h]])

        def task_P1(P, h):
            tr.tag = f"{P.kind}{P.seq}{P.st}.P1{h}"
            T = P.T
            hb = h % 2
            for which in range(2):
                wp_, r_wp = w_next(("q" if which == 0 else "k", h))
                pss = []
                for e in range(2):
                    p_, r_p = psum()
                    for kc in range(8):
                        mm(p_[:, 0:T], wp_[:, kc * 256 + e * 128: kc * 256 + (e + 1) * 128], hT[:, kc, 0:T],
                           kc == 0, kc == 7, [r_wp, r_hT], [r_p], kc == 7)
                    pss.append((p_, r_p))
                w_release()
                (x1, r1), (x2, r2) = pss
                cos = rope[:, 0, 0:T]
                sin = rope[:, 1, 0:T]
                t1, rt1 = tmp32()
                t2, rt2 = tmp32()
                tt("dve", t1[:, 0:T], x1[:, 0:T], cos, ALU.mult, [r1, r_rope], [rt1])
                tt("dve", t2[:, 0:T], x2[:, 0:T], sin, ALU.mult, [r2, r_rope], [rt2])
                tt("dve", qk[:, hb, which, 0, 0:T], t1[:, 0:T], t2[:, 0:T], ALU.subtract, [rt1, rt2], [r_qk[hb]])
                t3, rt3 = tmp32()
                t4, rt4 = tmp32()
                tt("dve", t3[:, 0:T], x1[:, 0:T], sin, ALU.mult, [r1, r_rope], [rt3])
                tt("dve", t4[:, 0:T], x2[:, 0:T], cos, ALU.mult, [r2, r_rope], [rt4])
                tt("dve", qk[:, hb, which, 1, 0:T], t3[:, 0:T], t4[:, 0:T], ALU.add, [rt3, rt4], [r_qk[hb]])

        def rb_list(P):
            out = []
            for si, (o, L, slot) in enumerate(P.segs):
                for ib in range(L // P.n):
                    out.append((si, ib, o + ib * P.n, slot))
            return out

        def task_P2(P, h):
            tr.tag = f"{P.kind}{P.seq}{P.st}.P2{h}"
            T = P.T
            n = P.n
            hb = h % 2
            wv0, r_wv0 = w_next(("v", h, 0))
            wv1, r_wv1 = w_next(("v", h, 1))
            for rb, (si, ib, t0, slot) in enumerate(rb_list(P)):
                p_, r_p = psum()
                for kc in range(8):
                    wv = wv0 if kc < 4 else wv1
                    rw = r_wv0 if kc < 4 else r_wv1
                    mm(p_[:n, :], hT[:, kc, t0:t0 + n], wv[:, (kc % 4) * 512:(kc % 4 + 1) * 512],
                       kc == 0, kc == 7, [rw, r_hT], [r_p], kc == 7)
                cp("act", vsb[:n, hb, rb, :], p_[:n, :], [r_p], [r_v[hb], r_merged])
            w_release(2)
            for j2 in range(2):
                wg, r_wg = w_next(("g", h, j2))
                for jj in range(2):
                    j = j2 * 2 + jj
                    p_, r_p = psum()
                    for kc in range(8):
                        mm(p_[:, 0:T], wg[:, kc * 256 + jj * 128: kc * 256 + (jj + 1) * 128], hT[:, kc, 0:T],
                           kc == 0, kc == 7, [r_wg, r_hT], [r_p], kc == 7)
                    act(sgb[:, hb, j, 0:T], p_[:, 0:T], AF.Silu, [r_p], [r_sg[hb]])
                w_release()

        def S_of(P, h, slot):
            if P.kind == "sample" and slot == 3:
                hh = (h + 2) % 4
            else:
                hh = h
            return S_p[:, hh, :, :], r_S[hh]

        def task_R1(P, h):
            tr.tag = f"{P.kind}{P.seq}{P.st}.R1{h}"
            n = P.n
            hb = h % 2
            qT = qk[:, hb, 0, :, :]
            kT = qk[:, hb, 1, :, :]
            rbl = rb_list(P)
            for si, (o, L, slot) in enumerate(P.segs):
                S_ap, r_Sx = S_of(P, h, slot)
                if P.kind == "sample":
                    dma("sp", f"sld{slot}{h}", S_ap, sret[slot - 2, h].rearrange("e p v -> p e v"), [], [r_Sx])
                sbi = hb if P.kind == "prompt" else si
                cp("act", Sbf[:, sbi, :, :], S_ap, [r_Sx], [r_Sbf[sbi]])
            blk = 0
            for rb, (si, ib, t0, slot) in enumerate(rbl):
                seg_t0 = P.segs[si][0]
                for jb in range(ib + 1):
                    tj = seg_t0 + jb * n
                    psc, r_psc = psum()
                    for e in range(2):
                        mm(psc[:n, 0:n], kT[:, e, tj:tj + n], qT[:, e, t0:t0 + n], e == 0, e == 1, [r_qk[hb]], [r_psc], e == 1)
                    if jb == ib:
                        stt(smT[:n, blk, 0:n], psc[:n, 0:n], float(GAMMA[h] ** (-128.0 * jb)), maskt[:n, h, 0:n], ALU.mult, ALU.mult,
                            [r_psc, r_const], [r_smT])
                    else:
                        ts("dve", smT[:n, blk, 0:n], psc[:n, 0:n], offt[:n, h, jb:jb + 1], None, ALU.mult, None,
                           [r_psc, r_const], [r_smT])
                    blk += 1
            for rb, (si, ib, t0, slot) in enumerate(rbl):
                pkt, r_pkt = psum()
                pktb = pkt[:].bitcast(BF16)
                for e in range(2):
                    transp(pktb[:n, e * 128:(e + 1) * 128], kT[:, e, t0:t0 + n], idb[:, :], [r_qk[hb], r_const], [r_pkt], e == 1)
                ts("dve", ktok[:n, rb, :], pktb[:n, 0:256], kdt[:n, P.kd, ib, h:h + 1], None, ALU.mult, None,
                   [r_pkt, r_const], [r_ktok])

        def task_R2a(P, h):
            tr.tag = f"{P.kind}{P.seq}{P.st}.R2a{h}"
            n = P.n
            hb = h % 2
            qT = qk[:, hb, 0, :, :]
            rbl = rb_list(P)
            nrb = len(rbl)
            blk = 0
            pos = []
            for rb, (si, ib, t0, slot) in enumerate(rbl):
                po, r_po = psum()
                for jb in range(ib + 1):
                    rbj = rb - ib + jb
                    mm(po[:n, :], smT[:n, blk, 0:n], vsb[:n, hb, rbj, :], jb == 0, False, [r_smT, r_v[hb]], [r_po], False)
                    blk += 1
                sbi = hb if P.kind == "prompt" else si
                for e in range(2):
                    mm(po[:n, :], qT[:, e, t0:t0 + n], Sbf[:, sbi, e, :], False, e == 1, [r_qk[hb], r_Sbf[sbi]], [r_po], e == 1)
                pos.append((po, r_po))
                tr.emit("dve", (lambda n, po, rb: (lambda v: v.bn_stats(out=st64[:n, rb, :], in_=po[:n, :])))(n, po, rb), [r_po, r_stat], [r_stat])
                tr.emit("dve", (lambda n, rb: (lambda v: v.bn_aggr(out=mv4[:n, rb, :], in_=st64[:n, rb, :])))(n, rb), [r_stat], [r_stat])
            if P.kind == "prompt":
                eps_ap = epst[:n, 0:nrb, h]
            else:
                eps_ap = epst[:n, 0:1, h].broadcast_to([n, nrb])
            rs3 = rs4[:, hb, :, :]
            rsqrt_multi(n, nrb, mv4[:n, 0:nrb, 0], mv4[:n, 0:nrb, 1], eps_ap, rs3, [], r_rs4[hb])
            for rb, (po, r_po) in enumerate(pos):
                act(onb[:n, rb, :], po[:n, :], AF.Identity, [r_po, r_rs4[hb]], [r_onb[rb]], bias=rs3[:n, rb, 1:2], scale=rs3[:n, rb, 0:1])
            for si, (o, L, slot) in enumerate(P.segs):
                S_ap, r_Sx = S_of(P, h, slot)
                rbs = [rb for rb, x in enumerate(rbl) if x[0] == si]
                for e in range(2):
                    pkv, r_pkv = psum()
                    for k_, rb in enumerate(rbs):
                        mm(pkv[:, :], ktok[:n, rb, e * 128:(e + 1) * 128], vsb[:n, hb, rb, :], k_ == 0, k_ == len(rbs) - 1,
                           [r_ktok, r_v[hb]], [r_pkv], k_ == len(rbs) - 1)
                    stt(S_ap[:, e, :], S_ap[:, e, :], float(P.gL[h]), pkv[:, :], ALU.mult, ALU.add, [r_Sx, r_pkv], [r_Sx])
                if P.last:
                    r_o = Res("rs_o")
                    dma("sp", f"strs{slot}{h}", rs_o[slot, h].rearrange("e p v -> p e v"), S_ap, [r_Sx], [r_o])
                    out_res.append(r_o)

        def task_R2b(P, h):
            tr.tag = f"{P.kind}{P.seq}{P.st}.R2b{h}"
            n = P.n
            hb = h % 2
            for rb, (si, ib, t0, slot) in enumerate(rb_list(P)):
                pot, r_pot = psum()
                potb = pot[:].bitcast(BF16)
                for d in range(4):
                    transp(potb[:, d * 128:d * 128 + n], onb[:n, rb, d * 128:(d + 1) * 128], idb[:n, :n],
                           [r_onb[rb], r_const], [r_pot], d == 3)
                tt("dve", gatedT[:, h * 4:(h + 1) * 4, t0:t0 + n],
                   potb[:, 0:512].rearrange("p (d t) -> p d t", t=128)[:, :, 0:n],
                   sgb[:, hb, :, t0:t0 + n], ALU.mult, [r_pot, r_sg[hb]], [r_gated] + r_actT)

        def task_C(P, cpair):
            tr.tag = f"{P.kind}{P.seq}{P.st}.C{cpair}"
            T = P.T
            wcx = [w_next(("cx", cpair * 2)), w_next(("cx", cpair * 2 + 1))]
            wbg, r_wbg = w_next(("bg", cpair))
            for ci in range(2):
                c = cpair * 2 + ci
                wc, r_wc = wcx[ci]
                pcg, r_pcg = psum()
                pxi, r_pxi = psum()
                pbg, r_pbg = psum()
                for kc in range(8):
                    mm(pcg[:, 0:T], wc[:, kc * 256: kc * 256 + 128], hT[:, kc, 0:T], kc == 0, kc == 7, [r_wc, r_hT], [r_pcg], kc == 7)
                for kc in range(8):
                    mm(pxi[:, 0:T], wc[:, kc * 256 + 128: kc * 256 + 256], hT[:, kc, 0:T], kc == 0, kc == 7, [r_wc, r_hT], [r_pxi], kc == 7)
                for kc in range(8):
                    mm(pbg[:, 0:T], wbg[:, kc * 256 + ci * 128: kc * 256 + (ci + 1) * 128], hT[:, kc, 0:T], kc == 0, kc == 7,
                       [r_wbg, r_hT], [r_pbg], kc == 7)
                cgs, r_cgs = tmp32()
                cp("act", cgs[:, 0:T], pcg[:, 0:T], [r_pcg], [r_cgs])
                ui = c % 2
                y1, r_y1 = tmp32()
                y2, r_y2 = tmp32()
                for si, (o, L, slot) in enumerate(P.segs):
                    u0 = si * (L + 2)
                    cp("pool", ut[:, ui, u0:u0 + 2], uh[:, slot, c, :], [r_uh], [r_ut[ui]])
                    tt("dve", ut[:, ui, u0 + 2:u0 + 2 + L], pxi[:, o:o + L], cgs[:, o:o + L], ALU.mult, [r_pxi, r_cgs], [r_ut[ui]])
                    cp("pool", uh[:, slot, c, :], ut[:, ui, u0 + L:u0 + L + 2], [r_ut[ui]], [r_uh])
                    act(y1[:, o:o + L], ut[:, ui, u0 + 2:u0 + 2 + L], AF.Identity, [r_ut[ui], r_const], [r_y1],
                        bias=cwt[:, c, 3:4], scale=cwt[:, c, 2:3])
                    stt(y2[:, o:o + L], ut[:, ui, u0 + 1:u0 + 1 + L], cwt[:, c, 1:2], y1[:, o:o + L], ALU.mult, ALU.add,
                        [r_ut[ui], r_y1, r_const], [r_y2])
                    stt(y1[:, o:o + L], ut[:, ui, u0:u0 + L], cwt[:, c, 0:1], y2[:, o:o + L], ALU.mult, ALU.add,
                        [r_ut[ui], r_y2, r_const], [r_y1])
                tt("dve", bgc[:, c, 0:T], y1[:, 0:T], pbg[:, 0:T], ALU.mult, [r_y1, r_pbg], [r_bgc] + r_actT)
            w_release(3)

        def stage_D(P):
            tr.tag = f"{P.kind}{P.seq}{P.st}.D"
            T = P.T
            woc = None
            for m in range(8):
                wor, r_wor = w_next(("wor", m))
                if m % 2 == 0:
                    woc, r_woc = w_next(("woc", m // 2))
                wgt, r_wgt = w_next(("gate", m))
                pgr, r_pgr = psum()
                pgc, r_pgc = psum()
                for kc in range(8):
                    mm(pgr[:, 0:T], wgt[:, kc * 256: kc * 256 + 128], hT[:, kc, 0:T], kc == 0, kc == 7, [r_wgt, r_hT], [r_pgr], kc == 7)
                for kc in range(8):
                    mm(pgc[:, 0:T], wgt[:, kc * 256 + 128: kc * 256 + 256], hT[:, kc, 0:T], kc == 0, kc == 7, [r_wgt, r_hT], [r_pgc], kc == 7)
                sr, r_sr = tmp32()
                sc, r_sc = tmp32()
                act(sr[:, 0:T], pgr[:, 0:T], AF.Sigmoid, [r_pgr], [r_sr])
                act(sc[:, 0:T], pgc[:, 0:T], AF.Sigmoid, [r_pgc], [r_sc])
                pyc, r_pyc = psum()
                mi = m % 2
                for kc in range(8):
                    mm(pyc[:, 0:T], woc[:, kc * 256 + mi * 128: kc * 256 + (mi + 1) * 128], bgc[:, kc, 0:T], kc == 0, kc == 7,
                       [r_woc, r_bgc], [r_pyc], kc == 7)
                pyr, r_pyr = psum()
                for kc in range(16):
                    mm(pyr[:, 0:T], wor[:, kc * 128:(kc + 1) * 128], gatedT[:, kc, 0:T], kc == 0, kc == 15, [r_wor, r_gated], [r_pyr], kc == 15)
                tt("dve", sc[:, 0:T], sc[:, 0:T], pyc[:, 0:T], ALU.mult, [r_sc, r_pyc], [r_sc])
                tt("dve", sr[:, 0:T], sr[:, 0:T], pyr[:, 0:T], ALU.mult, [r_sr, r_pyr], [r_sr])
                tt("pool", mergedT[:, m, 0:T], sr[:, 0:T], sc[:, 0:T], ALU.add, [r_sr, r_sc], [r_merged, r_v[0], r_v[1]])
                w_release(4 if m % 2 == 1 else 1)

        def stage_E(P):
            tr.tag = f"{P.kind}{P.seq}{P.st}.E"
            par = P.par
            pans = [[w_next(("wout", ch, half)) for half in range(2)] for ch in range(2)]
            for bi, (tok0, nb) in enumerate(P.blocks):
                rx = r_xa[par][bi]
                for ch in range(2):
                    pa, r_pa = psum()
                    for kc in range(8):
                        wp_, r_wp = pans[ch][kc // 4]
                        mm(pa[:nb, :], mergedT[:, kc, tok0:tok0 + nb], wp_[:, (kc % 4) * 512:(kc % 4 + 1) * 512],
                           kc == 0, False, [r_wp, r_merged], [r_pa], False)
                    mm(pa[:nb, :], ones[0:1, 0:nb], browb[0:1, ch * 512:(ch + 1) * 512], False, True, [r_ones, r_browb], [r_pa], True)
                    t_, r_t = tmp32()
                    tt("dve", t_[:nb, :], pa[:nb, :], gtc[:nb, 0, ch * 512:(ch + 1) * 512], ALU.mult, [r_pa, r_gtc], [r_t])
                    tt("dve", xa[:nb, par, bi, ch * 512:(ch + 1) * 512], xa[:nb, par, bi, ch * 512:(ch + 1) * 512], t_[:nb, :], ALU.add,
                       [rx, r_t], [rx])
                for i in range(2):
                    tr.emit("dve", (lambda i, bi, nb: (lambda v: v.bn_stats(out=st6b[:nb, bi, i, :], in_=xa[:nb, par, bi, i * 512:(i + 1) * 512])))(i, bi, nb),
                            [rx, r_statB], [r_statB])
                tr.emit("dve", (lambda bi, nb: (lambda v: v.bn_aggr(out=mv4b[:nb, bi, :], in_=st6b[:nb, bi, :, :].rearrange("p a b -> p (a b)"))))(bi, nb),
                        [r_statB], [r_statB])
            w_release(4)
            k = len(P.blocks)
            nbm = P.blocks[0][1]
            rsqrt_multi(nbm, k, mv4b[:nbm, 0:k, 0], mv4b[:nbm, 0:k, 1], LN_EPS, rsB, [r_statB], r_rsB)
            def e_xnb(bi):
                tok0, nb = P.blocks[bi]
                xb = bi % 2
                act(xnb2[:nb, xb, :], xa[:nb, par, bi, :], AF.Identity, [r_xa[par][bi], r_rsB], [r_xnb if xb == 0 else r_xnb1],
                    bias=rsB[:nb, bi, 1:2], scale=rsB[:nb, bi, 0:1])

            e_xnb(0)
            for bi, (tok0, nb) in enumerate(P.blocks):
                if bi + 1 < len(P.blocks):
                    e_xnb(bi + 1)
                ln_part2(P, bi, tok0, nb, 1, xb=bi % 2)
            for bi, (tok0, nb) in enumerate(P.blocks):
                xab = xa[:nb, par, bi, :]
                rx = r_xa[par][bi]
                act(xab, xab, AF.Identity, [rx, r_rsB], [rx], bias=rsB[:nb, bi, 1:2], scale=rsB[:nb, bi, 0:1])
                tt("pool", xab, xab, ptm[:nb, 2, :], ALU.mult, [rx, r_ptm], [rx])
                tt("pool", xab, xab, ptm[:nb, 3, :], ALU.add, [rx, r_ptm], [rx])

        def ln_stats_multi(P, blocks, st6x, mv4x, r_statX, load_src, only=None):
            par = P.par
            for bi, (tok0, nb) in enumerate(blocks):
                if only is not None and bi != only:
                    continue
                rx = r_xa[par][bi]
                if load_src is not None:
                    dma("sp", f"xld{par}{bi}", xa[:nb, par, bi, :], load_src(tok0, nb), [], [rx])
                for i in range(2):
                    tr.emit("dve", (lambda i, bi, nb: (lambda v: v.bn_stats(out=st6x[:nb, bi, i, :], in_=xa[:nb, par, bi, i * 512:(i + 1) * 512])))(i, bi, nb),
                            [rx, r_statX], [r_statX])
                tr.emit("dve", (lambda bi, nb: (lambda v: v.bn_aggr(out=mv4x[:nb, bi, :], in_=st6x[:nb, bi, :, :].rearrange("p a b -> p (a b)"))))(bi, nb),
                        [r_statX], [r_statX])

        def make_astep(Pn):
            ablocks = list(enumerate(Pn.blocks)) if Pn is not None else []
            stA = {"i": 0, "phase": 0}

            def a_step():
                if Pn is None:
                    return
                tr.tag = f"{Pn.kind}{Pn.seq}{Pn.st}.A"
                par = Pn.par
                if stA["phase"] <= len(Pn.blocks):
                    ph = stA["phase"]
                    stA["phase"] += 1
                    if ph == 0:
                        dma("sp", "rope", rope[:, :, 0:Pn.T], Pn.rope_src, [], [r_rope])
                    if ph < len(Pn.blocks):
                        ln_stats_multi(Pn, Pn.blocks, st6a, mv4a, r_statA,
                                       lambda tok0, nb: xp[Pn.seq, Pn.st * 512 + tok0: Pn.st * 512 + tok0 + nb, :], only=ph)
                    else:
                        k = len(Pn.blocks)
                        rsqrt_multi(128, k, mv4a[:, 0:k, 0], mv4a[:, 0:k, 1], LN_EPS, rsA, [r_statA], r_rsA)
                    return
                if stA["i"] < len(ablocks):
                    bi, (tok0, nb) = ablocks[stA["i"]]
                    stA["i"] += 1
                    xab = xa[:nb, par, bi, :]
                    rx = r_xa[par][bi]
                    act(xnb[:nb, :], xab, AF.Identity, [rx, r_rsA], [r_xnb], bias=rsA[:nb, bi, 1:2], scale=rsA[:nb, bi, 0:1])
                    act(xab, xab, AF.Identity, [rx, r_rsA], [rx], bias=rsA[:nb, bi, 1:2], scale=rsA[:nb, bi, 0:1])
                    tt("pool", xab, xab, ptm[:nb, 0, :], ALU.mult, [rx, r_ptm], [rx])
                    tt("pool", xab, xab, ptm[:nb, 1, :], ALU.add, [rx, r_ptm], [rx])
                    ln_part2(Pn, bi, tok0, nb, 0)

            def a_done():
                return Pn is None or stA["i"] >= len(ablocks)
            return a_step, a_done

        def stage_FG(P, Pn):
            a_step, a_done = make_astep(Pn)
            par = P.par
            T = P.T
            accs = {}

            def g_panel(ch, pi):
                tr.tag = f"{P.kind}{P.seq}{P.st}.G"
                if ch not in accs:
                    accs[ch] = [psum(hold=True) for _ in P.blocks]
                wp_, r_wp = w_next(("down", ch, pi))
                kcs = list(range(pi * 4, min(pi * 4 + 4, NFF)))
                for bi, (tok0, nb) in enumerate(P.blocks):
                    pa, r_pa = accs[ch][bi]
                    for kl, kc in enumerate(kcs):
                        mm(pa[:nb, :], actT[:, kc, tok0:tok0 + nb], wp_[:, kl * 512:(kl + 1) * 512],
                           kc == 0, False, [r_wp, r_actT[kc]], [r_pa],
                           (pi < 5 and bi == len(P.blocks) - 1 and kl == len(kcs) - 1))
                    if pi == 5:
                        mm(pa[:nb, :], ones[0:1, 0:nb], browb[0:1, D + ch * 512: D + (ch + 1) * 512],
                           False, True, [r_ones, r_browb], [r_pa], True)
                w_release()

            def g_evac(ch):
                for bi, (tok0, nb) in enumerate(P.blocks):
                    pa, r_pa = accs[ch][bi]
                    rx = r_xa[par][bi]
                    t_, r_t = tmp32()
                    tt("dve", t_[:nb, :], pa[:nb, :], gtc[:nb, 1, ch * 512:(ch + 1) * 512], ALU.mult, [r_pa, r_gtc], [r_t])
                    tt("pool", xa[:nb, par, bi, ch * 512:(ch + 1) * 512], xa[:nb, par, bi, ch * 512:(ch + 1) * 512], t_[:nb, :], ALU.add,
                       [rx, r_t], [rx])
                    psum_unhold(r_pa)

            a_step_real = a_step
            if P.idx == 0:
                a_step = lambda: None
            for f in range(NFF):
                if f in (6, 8, 10, 12, 15):
                    a_step()
                tr.tag = f"{P.kind}{P.seq}{P.st}.F"
                wf, r_wf = w_next(("up", f))
                pa_, r_pa = psum()
                pg_, r_pg = psum()
                for kc in range(8):
                    mm(pa_[:, 0:T], wf[:, kc * 256: kc * 256 + 128], hT[:, kc, 0:T], kc == 0, kc == 7, [r_wf, r_hT], [r_pa], kc == 7)
                for kc in range(8):
                    mm(pg_[:, 0:T], wf[:, kc * 256 + 128: kc * 256 + 256], hT[:, kc, 0:T], kc == 0, kc == 7, [r_wf, r_hT], [r_pg], kc == 7)
                w_release()
                ui = f % 2
                y1, r_y1 = tmp32()
                y2, r_y2 = tmp32()
                for si, (o, L, slot) in enumerate(P.segs):
                    u0 = si * (L + 2)
                    cp("pool", ut[:, ui, u0:u0 + 2], ah[:, slot, f, :], [r_ah], [r_ut[ui]])
                    cp("act", ut[:, ui, u0 + 2:u0 + 2 + L], pa_[:, o:o + L], [r_pa], [r_ut[ui]])
                    cp("pool", ah[:, slot, f, :], ut[:, ui, u0 + L:u0 + L + 2], [r_ut[ui]], [r_ah])
                    act(y1[:, o:o + L], pa_[:, o:o + L], AF.Identity, [r_pa, r_const], [r_y1], bias=fwt[:, f, 3:4], scale=fwt[:, f, 2:3])
                    stt(y2[:, o:o + L], ut[:, ui, u0 + 1:u0 + 1 + L], fwt[:, f, 1:2], y1[:, o:o + L], ALU.mult, ALU.add,
                        [r_ut[ui], r_y1, r_const], [r_y2])
                    stt(y1[:, o:o + L], ut[:, ui, u0:u0 + L], fwt[:, f, 0:1], y2[:, o:o + L], ALU.mult, ALU.add,
                        [r_ut[ui], r_y2, r_const], [r_y1])
                act(y2[:, 0:T], y1[:, 0:T], AF.Gelu, [r_y1], [r_y2])
                tt("dve", actT[:, f, 0:T], y2[:, 0:T], pg_[:, 0:T], ALU.mult, [r_y2, r_pg], [r_actT[f], r_gated, r_bgc])
                if f >= 18:
                    g_panel(0, f - 18)
            g_panel(0, 4)
            a_step()
            g_panel(0, 5)
            g_evac(0)
            g_panel(1, 0)
            a_step()
            g_panel(1, 1)
            g_panel(1, 2)
            a_step()
            g_panel(1, 3)
            g_panel(1, 4)
            a_step()
            g_panel(1, 5)
            g_evac(1)
            while not a_done():
                a_step_real()
            for bi_ in range(len(P.blocks)):
                deferred.append((lambda bi_: (lambda: ln_stats_multi(P, P.blocks, st6c, mv4c, r_statC, None, only=bi_)))(bi_))
            deferred.append(lambda: ln2_rsqrt(P))
            for bi_ in range(len(P.blocks)):
                deferred.append((lambda bi_: (lambda: ln2_tail(P, bi_)))(bi_))

        def ln2_rsqrt(P):
            k = len(P.blocks)
            nbm = P.blocks[0][1]
            rsqrt_multi(nbm, k, mv4c[:nbm, 0:k, 0], mv4c[:nbm, 0:k, 1], LN_EPS, rsC, [r_statC], r_rsC)

        def ln2_tail(P, only):
            par = P.par
            tr.tag = f"{P.kind}{P.seq}{P.st}.G"
            for bi, (tok0, nb) in enumerate(P.blocks):
                if bi != only:
                    continue
                xab = xa[:nb, par, bi, :]
                rx = r_xa[par][bi]
                act(xab, xab, AF.Identity, [rx, r_rsC], [rx], bias=rsC[:nb, bi, 1:2], scale=rsC[:nb, bi, 0:1])
                tt("dve", xab, xab, ptm[:nb, 4, :], ALU.mult, [rx, r_ptm], [rx])
                tt("pool", xab, xab, ptm[:nb, 5, :], ALU.add, [rx, r_ptm], [rx])
                r_o = Res("y_o")
                if P.kind == "prompt":
                    dst = yp[P.seq, P.st * 512 + tok0: P.st * 512 + tok0 + nb, :]
                else:
                    dst = ys
                dma("sp", f"yst{par}{bi}", dst, xab, [rx], [r_o])
                out_res.append(r_o)
            if P.last and only == len(P.blocks) - 1:
                for (o, L, slot) in P.segs:
                    r_o = Res("cs_o")
                    dma("sp", f"stcs{slot}", cs_o[slot], uh[:, slot, :, :], [r_uh], [r_o])
                    out_res.append(r_o)
                    r_o = Res("fs_o")
                    dma("sp", f"stfs{slot}", fs_o[slot], ah[:, slot, :, :], [r_ah], [r_o])
                    out_res.append(r_o)

        deferred = []

        def run_deferred(n=None):
            while deferred and (n is None or n > 0):
                deferred.pop(0)()
                if n is not None:
                    n -= 1

        def stage_BC(P):
            order = [("P1", 0), ("P2", 0), ("P1", 1), ("R1", 0), ("P2", 1), ("R2a", 0), ("P1", 2), ("R2b", 0), ("R1", 1), ("P2", 2),
                     ("R2a", 1), ("P1", 3), ("R2b", 1), ("R1", 2), ("P2", 3), ("R2a", 2), ("C", 0), ("R2b", 2), ("R1", 3), ("C", 1),
                     ("R2a", 3), ("C", 2), ("R2b", 3), ("C", 3)]
            fn = {"P1": task_P1, "P2": task_P2, "R1": task_R1, "R2a": task_R2a, "R2b": task_R2b, "C": task_C}
            for (k, a) in order:
                fn[k](P, a)
                if (k, a) != ("P1", 0):
                    run_deferred(1)

        passes = [make_pass(0, "sample", 0, 0)]
        for seq in range(2):
            for st in range(4):
                passes.append(make_pass(len(passes), "prompt", seq, st))
        stage_A(passes[0])
        for i, P in enumerate(passes):
            stage_pre(P)
            stage_BC(P)
            stage_D(P)
            stage_E(P)
            stage_FG(P, passes[i + 1] if i + 1 < len(passes) else None)
        run_deferred()

        assert wst["next_use"] == TOTAL, (wst["next_use"], TOTAL)
        tr.wait_all("sp", out_res)
        tr.run(es)
    global _LAST_TRACKER
    _LAST_TRACKER = tr
    return nc, dbg_outs, pan_order


def _lhs_panel(W, cols):
    sub = W[:, cols]
    return np.ascontiguousarray(sub.reshape(8, 128, 256).transpose(1, 0, 2).reshape(128, 2048))


def _rhs_panel(W, kcs, cols):
    out = np.zeros((128, 4, 512), np.float32)
    for i, kc in enumerate(kcs):
        out[:, i, :] = W[kc * 128:(kc + 1) * 128, cols]
    return out.reshape(128, 2048)


def _build_wpan(order, w_in, w_o_ret, w_o_conv, w_out, w_up, w_down):
    a128 = np.arange(128)
    a256 = np.arange(256)
    a512 = np.arange(512)
    pans = []
    for key in order:
        k = key[0]
        if k in ("q", "k"):
            h = key[1]
            perm = np.concatenate([h * 256 + 2 * a128, h * 256 + 2 * a128 + 1])
            pans.append(_lhs_panel(w_in, perm + (0 if k == "q" else 1024)))
        elif k == "v":
            h, half = key[1], key[2]
            pans.append(_rhs_panel(w_in, list(range(4 * half, 4 * half + 4)), 2048 + h * 512 + a512))
        elif k == "g":
            h, j2 = key[1], key[2]
            pans.append(_lhs_panel(w_in, 4096 + h * 512 + j2 * 256 + a256))
        elif k == "cx":
            c = key[1]
            pans.append(_lhs_panel(w_in, np.concatenate([7168 + c * 128 + a128, 8192 + c * 128 + a128])))
        elif k == "bg":
            pans.append(_lhs_panel(w_in, 6144 + key[1] * 256 + a256))
        elif k == "wor":
            m = key[1]
            wor = w_o_ret[:, m * 128:(m + 1) * 128].reshape(16, 128, 128).transpose(1, 0, 2).reshape(128, 2048)
            pans.append(np.ascontiguousarray(wor))
        elif k == "woc":
            pans.append(_lhs_panel(w_o_conv, key[1] * 256 + a256))
        elif k == "gate":
            m = key[1]
            pans.append(_lhs_panel(w_in, np.concatenate([9216 + m * 128 + a128, 10240 + m * 128 + a128])))
        elif k == "wout":
            ch, half = key[1], key[2]
            pans.append(_rhs_panel(w_out, list(range(4 * half, 4 * half + 4)), ch * 512 + a512))
        elif k == "up":
            f = key[1]
            pans.append(_lhs_panel(w_up, np.concatenate([f * 128 + a128, DFF + f * 128 + a128])))
        elif k == "down":
            ch, i = key[1], key[2]
            pans.append(_rhs_panel(w_down, list(range(4 * i, min(4 * i + 4, NFF))), ch * 512 + a512))
        else:
            raise KeyError(key)
    assert len(pans) == NPAN
    return np.stack(pans).astype(np.float32)


def _consts():
    half = 128
    inv_freq = (np.float32(10000.0) ** (-(np.arange(half, dtype=np.float32) / np.float32(half)))).astype(np.float32)

    def tab(pos):
        ang = (pos.astype(np.float32)[None, :] * inv_freq[:, None]).astype(np.float32)
        a64 = ang.astype(np.float64)
        return np.stack([np.cos(a64), np.sin(a64)], axis=1).astype(np.float32)

    rope_p = tab(np.arange(SEQ))
    ps = PAST + np.arange(DEC_SEQ)
    rope_s = tab(np.concatenate([ps, ps]))
    g = np.array(GAMMA, np.float64)
    j = np.arange(128)
    maskT = np.zeros((128, H, 128), np.float64)
    offs = np.zeros((128, H, 4), np.float64)
    epsv = np.zeros((128, 4, H), np.float64)
    kdec = np.zeros((128, 2, 4, H), np.float64)
    for h in range(H):
        for jb in range(4):
            col = g[h] ** (-(128.0 * jb + j + 1.0)) / 16.0
            if jb == 0:
                maskT[:, h, :] = col[:, None] * (j[None, :] >= j[:, None])
            offs[:, h, jb] = col
            epsv[:, jb, h] = LN_EPS * g[h] ** (-2.0 * (128.0 * jb + j + 1.0))
            kdec[:, 0, jb, h] = g[h] ** (511.0 - 128.0 * jb - j) / 16.0
        kdec[:32, 1, 0, h] = g[h] ** (31.0 - j[:32]) / 16.0
    return dict(rope_p=rope_p, rope_s=rope_s, maskT=maskT.astype(ml_dtypes.bfloat16),
                epsv=epsv.astype(np.float32), kdec=kdec.astype(np.float32), offs=offs.astype(np.float32),
                identb=np.eye(128).astype(ml_dtypes.bfloat16), identf=np.eye(128, dtype=np.float32))


_CACHE = {}


def kernel(x_prompt, x_sample, c_prompt, c_sample, state_retention, state_shortconv, state_ffn_conv,
           ln_in_g, ln_in_b, w_mod, b_mod, w_in, w_o_ret, conv_w, conv_b, w_o_conv, w_out, b_out,
           ln1_g, ln1_b, w_up, ffn_conv_w, ffn_conv_b, w_down, b_down, ln2_g, ln2_b, _debug=()):
    f = lambda a: np.asarray(a, dtype=np.float32)
    x_prompt, x_sample, c_prompt, c_sample = f(x_prompt), f(x_sample), f(c_prompt), f(c_sample)
    state_retention, state_shortconv, state_ffn_conv = f(state_retention), f(state_shortconv), f(state_ffn_conv)
    key = tuple(_debug)
    if key not in _CACHE:
        _CACHE[key] = build_program(debug=_debug)
    nc, dbg_outs, pan_order = _CACHE[key]

    shared = _consts()
    shared["wpan"] = _build_wpan(pan_order, f(w_in)[0], f(w_o_ret)[0], f(w_o_conv)[0], f(w_out)[0], f(w_up)[0], f(w_down)[0])
    wm = f(w_mod)[0]
    shared["wmod"] = np.ascontiguousarray(wm.reshape(8, 128, 48, 128).transpose(2, 1, 0, 3).reshape(48, 128, 1024))
    shared["bmod"] = np.ascontiguousarray(f(b_mod)[0].reshape(48, 128).T)
    vecs = np.stack([f(ln_in_g), f(ln_in_b), f(ln1_g)[0], f(ln1_b)[0], f(ln2_g)[0], f(ln2_b)[0]])
    shared["vtm"] = np.ascontiguousarray(np.broadcast_to(vecs[None], (128, 6, D)))
    shared["vfm"] = np.ascontiguousarray(vecs[:4].reshape(4, 8, 128).transpose(2, 0, 1))
    cwa = np.concatenate([f(conv_w)[0], f(conv_b)], axis=0)
    shared["cw"] = np.ascontiguousarray(cwa.reshape(4, 8, 128).transpose(2, 1, 0))
    fwa = np.concatenate([f(ffn_conv_w)[0], f(ffn_conv_b)], axis=0)
    shared["fw"] = np.ascontiguousarray(fwa.reshape(4, NFF, 128).transpose(2, 1, 0))
    shared["brow"] = np.concatenate([f(b_out)[0], f(b_down)[0]])[None, :].copy()

    in_maps = []
    for i in range(8):
        m = dict(shared)
        m["xp"] = np.ascontiguousarray(x_prompt[2 * i:2 * i + 2])
        m["xs"] = np.ascontiguousarray(x_sample[2 * i:2 * i + 2].reshape(64, D))
        c4 = np.concatenate([c_prompt[2 * i:2 * i + 2], c_sample[2 * i:2 * i + 2]], axis=0)
        m["cT"] = np.ascontiguousarray(c4.reshape(4, 8, 128).transpose(2, 1, 0))
        sr = state_retention[0, 2 * i:2 * i + 2]
        m["sret"] = np.ascontiguousarray(sr.reshape(2, H, 128, 2, DV).transpose(0, 1, 3, 2, 4))
        sc = state_shortconv[0, 2 * i:2 * i + 2]
        m["sconv"] = np.ascontiguousarray(sc.reshape(2, 2, 8, 128).transpose(0, 3, 2, 1))
        sf = state_ffn_conv[0, 2 * i:2 * i + 2]
        m["sffn"] = np.ascontiguousarray(sf.reshape(2, 2, NFF, 128).transpose(0, 3, 2, 1))
        in_maps.append(m)

    res = run_bass_kernel_spmd(nc, in_maps, core_ids=list(range(8)))
    R = res.results
    y_prompt = np.concatenate([r["yp"] for r in R], axis=0)
    y_sample = np.concatenate([r["ys"].reshape(2, DEC_SEQ, D) for r in R], axis=0)

    def ret_state(lo):
        outs = []
        for r in R:
            a = r["rs_o"][lo:lo + 2]
            outs.append(a.transpose(0, 1, 3, 2, 4).reshape(2, H, DK, DV))
        return np.concatenate(outs, axis=0)[None]

    def hist_state(name, lo, nch):
        outs = []
        for r in R:
            a = r[name][lo:lo + 2]
            outs.append(a.transpose(0, 3, 2, 1).reshape(2, 2, nch * 128))
        return np.concatenate(outs, axis=0)[None]

    outs = (y_prompt, y_sample,
            ret_state(0), hist_state("cs_o", 0, 8), hist_state("fs_o", 0, NFF),
            ret_state(2), hist_state("cs_o", 2, 8), hist_state("fs_o", 2, NFF))
    outs = tuple(np.ascontiguousarray(o, dtype=np.float32) for o in outs)
    if _debug:
        return outs, [{k: r["dbg_" + k] for k in dbg_outs} for r in R]
    return outs
```

```python
import numpy as np
import ml_dtypes
from contextlib import ExitStack
import concourse.bass as bass
import concourse.mybir as mybir
from concourse.bass_utils import run_bass_kernel_spmd

F32 = mybir.dt.float32
BF16 = mybir.dt.bfloat16
I32 = mybir.dt.int32
AF = mybir.ActivationFunctionType
ALU = mybir.AluOpType

D = 1024
H = 4
DK = 256
DV = 512
DFF = 2816
NFF = 22
SEQ = 2048
DEC_SEQ = 32
PAST = 2048
LN_EPS = 1e-5
ALPHA = 2.0 ** 0.25
NPAN = 94
NS = 7
NCAST = 24
GAMMA = [1.0 - 2.0 ** (-5.0 - h) for h in range(H)]


class Res:
    __slots__ = ("name", "writer", "readers")

    def __init__(self, name):
        self.name = name
        self.writer = None
        self.readers = {}

    def absorb(self, others):
        for o in others:
            if o.writer is not None:
                self.readers[o.writer[0]] = max(self.readers.get(o.writer[0], 0), o.writer[1])
            for k, v in o.readers.items():
                self.readers[k] = max(self.readers.get(k, 0), v)


class Eng:
    def __init__(self, name):
        self.name = name
        self.count = 0
        self.pending = False
        self.waited = {}
        self.prog = []


class Tracker:
    ENGS = ("pe", "act", "dve", "pool", "sp")

    def __init__(self, nc):
        self.nc = nc
        self.engs = {n: Eng(n) for n in self.ENGS}
        self.sems = {}
        self.dma_vals = {}
        self.ninst = 0

    def _need(self, e, key, val):
        if e.waited.get(key, 0) >= val:
            return
        if key in self.engs:
            src = self.engs[key]
            if val > src.count:
                raise RuntimeError(f"dep on future inc: {e.name} waits {key}>={val} but count={src.count}")
        e.waited[key] = val
        e.prog.append(("wait", key, val))

    def _deps(self, e, reads, writes, same_engine_ok):
        for r in reads:
            if r.writer is not None:
                if not (same_engine_ok and r.writer[0] == e.name):
                    self._need(e, *r.writer)
        for w in writes:
            if w.writer is not None:
                if not (same_engine_ok and w.writer[0] == e.name):
                    self._need(e, *w.writer)
            for k, v in w.readers.items():
                if k == e.name:
                    continue
                self._need(e, k, v)

    def emit(self, eng, fn, reads=(), writes=(), inc=True, same_engine_ok=False):
        e = self.engs[eng]
        self._deps(e, reads, writes, same_engine_ok)
        val = e.count + 1
        e.prog.append(("inst", fn, inc, getattr(self, "tag", "")))
        if inc:
            e.count = val
            e.pending = False
        else:
            e.pending = True
        for w in writes:
            w.writer = (eng, val)
            w.readers = {}
        for r in reads:
            r.readers[eng] = max(r.readers.get(eng, 0), val)
        self.ninst += 1

    def dma(self, eng, semkey, fn, reads=(), writes=()):
        e = self.engs[eng]
        self._deps(e, reads, writes, False)
        val = self.dma_vals.get(semkey, 0) + 16
        self.dma_vals[semkey] = val
        e.prog.append(("dma", fn, semkey))
        for w in writes:
            w.writer = (semkey, val)
            w.readers = {}
        for r in reads:
            r.readers[semkey] = max(r.readers.get(semkey, 0), val)
        self.ninst += 1

    def wait_all(self, eng, resources):
        e = self.engs[eng]
        for r in resources:
            if r.writer is not None:
                self._need(e, *r.writer)
            for k, v in r.readers.items():
                self._need(e, k, v)

    def run(self, es):
        nc = self.nc
        keys = list(self.ENGS) + sorted(self.dma_vals.keys())
        for k in keys:
            self.sems[k] = es.enter_context(nc.semaphore("s_" + k))
        for e in self.engs.values():
            if e.pending:
                raise RuntimeError(f"engine {e.name} has trailing non-inc instruction")
        block = es.enter_context(nc.Block())
        sems = self.sems

        def runner(e):
            def body(h):
                mysem = sems[e.name]
                for item in e.prog:
                    if item[0] == "wait":
                        h.wait_ge(sems[item[1]], item[2])
                    elif item[0] == "inst":
                        ins = item[1](h)
                        if item[2]:
                            ins.then_inc(mysem, 1)
                    else:
                        ins = item[1](h)
                        ins.then_inc(sems[item[2]], 16)
            return body

        block.tensor(runner(self.engs["pe"]))
        block.scalar(runner(self.engs["act"]))
        block.vector(runner(self.engs["dve"]))
        block.gpsimd(runner(self.engs["pool"]))
        block.sync(runner(self.engs["sp"]))


def build_program(debug=()):
    nc = bass.Bass("TRN2", target_bir_lowering=False)
    tr = Tracker(nc)

    def din(name, shape, dt=F32):
        return nc.dram_tensor(name, list(shape), dt, kind="ExternalInput").ap()

    def dout(name, shape, dt=F32):
        return nc.dram_tensor(name, list(shape), dt, kind="ExternalOutput").ap()

    xp = din("xp", [2, SEQ, D])
    xs = din("xs", [64, D])
    cT = din("cT", [128, 8, 4])
    sret = din("sret", [2, H, 2, 128, DV])
    sconv = din("sconv", [2, 128, 8, 2])
    sffn = din("sffn", [2, 128, NFF, 2])
    wpan = din("wpan", [NPAN, 128, 2048])
    wmod = din("wmod", [48, 128, 1024])
    bmod = din("bmod", [128, 48])
    vtm = din("vtm", [128, 6, D])
    vfm = din("vfm", [128, 4, 8])
    cw = din("cw", [128, 8, 4])
    fw = din("fw", [128, NFF, 4])
    brow = din("brow", [1, 2 * D])
    rope_p = din("rope_p", [128, 2, SEQ])
    rope_s = din("rope_s", [128, 2, 64])
    maskT = din("maskT", [128, H, 128], BF16)
    epsv = din("epsv", [128, 4, H])
    kdec = din("kdec", [128, 2, 4, H])
    offs = din("offs", [128, H, 4])
    identb = din("identb", [128, 128], BF16)
    identf = din("identf", [128, 128])

    wbf = nc.dram_tensor("wbf", [NPAN, 128, 2048], BF16, kind="Internal").ap()

    yp = dout("yp", [2, SEQ, D])
    ys = dout("ys", [64, D])
    rs_o = dout("rs_o", [4, H, 2, 128, DV])
    cs_o = dout("cs_o", [4, 128, 8, 2])
    fs_o = dout("fs_o", [4, 128, NFF, 2])
    out_res = []
    dbg_outs = {}

    es = ExitStack()
    with es:
        def sb(name, shape, dt=F32):
            return es.enter_context(nc.sbuf_tensor(name, list(shape), dt))

        ring = sb("ring", [128, NS, 2048], BF16)
        r_ring = [Res(f"ring{i}") for i in range(NS)]
        xa = sb("xa", [128, 2, 4, D])
        r_xa = [[Res(f"xa{p}{b}") for b in range(4)] for p in range(2)]
        ptm = sb("ptm", [128, 6, D])
        r_ptm = Res("ptm")
        gtc = sb("gtc", [128, 2, D])
        r_gtc = Res("gtc")
        S_p = sb("S_p", [128, H, 2, DV])
        r_S = [Res(f"S{h}") for h in range(H)]
        Sbf = sb("Sbf", [128, 2, 2, DV], BF16)
        r_Sbf = [Res("Sbf0"), Res("Sbf1")]
        rope = sb("rope", [128, 2, 512])
        r_rope = Res("rope")
        hT = sb("hT", [128, 8, 512], BF16)
        r_hT = Res("hT")
        qk = sb("qk", [128, 2, 2, 2, 512], BF16)
        r_qk = [Res("qk0"), Res("qk1")]
        mvbuf = sb("mvbuf", [128, 4096], BF16)
        vsb = mvbuf[:].rearrange("p (b r d) -> p b r d", b=2, r=4)
        r_v = [Res("v0"), Res("v1")]
        sgb = sb("sgb", [128, 2, 4, 512], BF16)
        r_sg = [Res("sg0"), Res("sg1")]
        ktok = sb("ktok", [128, 4, 256], BF16)
        r_ktok = Res("ktok")
        smT = sb("smT", [128, 10, 128], BF16)
        r_smT = Res("smT")
        onb = sb("onb", [128, 4, DV], BF16)
        r_onb = [Res(f"onb{i}") for i in range(4)]
        gbuf = sb("gbuf", [128, 12288], BF16)
        gatedT = gbuf[:, 0:8192].rearrange("p (c t) -> p c t", t=512)
        r_gated = Res("gatedT")
        bgc = gbuf[:, 8192:12288].rearrange("p (c t) -> p c t", t=512)
        r_bgc = Res("bgc")
        mergedT = mvbuf[:].rearrange("p (c t) -> p c t", t=512)
        r_merged = Res("mergedT")
        actT = gbuf[:, 0:NFF * 512].rearrange("p (c t) -> p c t", t=512)
        r_actT = [Res(f"actT{f}") for f in range(NFF)]
        xnb = sb("xnb", [128, D], BF16)
        r_xnb = Res("xnb")
        tmpA = sb("tmpA", [128, 4, 512])
        r_tmpA = [Res(f"tmpA{i}") for i in range(4)]
        ut = sb("ut", [128, 2, 520])
        r_ut = [Res("ut0"), Res("ut1")]
        uh = sb("uh", [128, 4, 8, 2])
        r_uh = Res("uh")
        ah = sb("ah", [128, 4, NFF, 2])
        r_ah = Res("ah")
        modfm = sb("modfm", [128, 48, 4])
        r_modfm = Res("modfm")
        mA = sb("mA", [128, 4, 8, 4])
        r_mA = Res("mA")
        cTt = sb("cTt", [128, 8, 4])
        vfmt = sb("vfmt", [128, 4, 8])
        cwt = sb("cwt", [128, 8, 4])
        fwt = sb("fwt", [128, NFF, 4])
        bmt = sb("bmt", [128, 48])
        browb = sb("browb", [1, 2 * D], BF16)
        ones = sb("ones", [1, 128], BF16)
        maskt = sb("maskt", [128, H, 128], BF16)
        epst = sb("epst", [128, 4, H])
        kdt = sb("kdt", [128, 2, 4, H])
        offt = sb("offt", [128, H, 4])
        idb = sb("idb", [128, 128], BF16)
        idf = sb("idf", [128, 128])
        bcl = sb("bcl", [128, 128])
        r_bcl = Res("bcl")
        r_const = Res("const")
        r_browb = Res("browb")
        r_ones = Res("ones")
        st6 = sb("st6", [128, 2, 6])
        mvt = sb("mvt", [128, 2])
        rsq = sb("rsq", [128, 8])
        rsqi = sb("rsqi", [128, 2], I32)
        rsq4 = sb("rsq4", [128, 3, 4])
        rsq4i = sb("rsq4i", [128, 4], I32)
        mv4 = sb("mv4", [128, 4, 2])
        st64 = sb("st64", [128, 4, 6])
        rs4 = sb("rs4", [128, 2, 4, 2])
        r_rs4 = [Res("rs4a"), Res("rs4b")]
        st6a = sb("st6a", [128, 4, 2, 6])
        mv4a = sb("mv4a", [128, 4, 2])
        rsA = sb("rsA", [128, 4, 2])
        r_rsA = Res("rsA")
        r_statA = Res("statA")
        st6b = sb("st6b", [128, 4, 2, 6])
        mv4b = sb("mv4b", [128, 4, 2])
        rsB = sb("rsB", [128, 4, 2])
        r_rsB = Res("rsB")
        r_statB = Res("statB")
        r_stat = Res("stat")
        rstd_t = sb("rstd_t", [128, 4, 2])
        r_rstd = [Res(f"rstd{i}") for i in range(4)]
        wm32 = tmpA[:].rearrange("p a b -> p (a b)").rearrange("p (b f) -> p b f", f=1024)
        r_wm = [Res("wm0"), Res("wm1")]

        psb = [es.enter_context(nc.psum_tensor(f"ps{i}", [128, 512], F32)) for i in range(8)]
        r_ps = [Res(f"ps{i}") for i in range(8)]
        ps_ctr = [0]

        ps_held = set()

        def psum(hold=False):
            while True:
                i = ps_ctr[0] % 8
                ps_ctr[0] += 1
                if i not in ps_held:
                    break
            if hold:
                ps_held.add(i)
            return psb[i], r_ps[i]

        def psum_unhold(r):
            ps_held.discard(r_ps.index(r))

        rot = {"tmp": 0, "rstd": 0}

        def tmp32():
            i = rot["tmp"] % 4
            rot["tmp"] += 1
            return tmpA[:, i, :], r_tmpA[i]

        def rstd_slot():
            i = rot["rstd"] % 4
            rot["rstd"] += 1
            return rstd_t[:, i, :], r_rstd[i]

        def mm(out, lhsT, rhs, start, stop, reads, writes, inc):
            tr.emit("pe", lambda t: t.matmul(out, lhsT=lhsT, rhs=rhs, start=start, stop=stop),
                    reads, writes, inc=inc, same_engine_ok=True)

        def transp(out, in_, ident, reads, writes, inc):
            tr.emit("pe", lambda t: t.transpose(out=out, in_=in_, identity=ident),
                    reads, writes, inc=inc, same_engine_ok=True)

        def act(out, in_, func, reads, writes, bias=None, scale=None):
            kw = {}
            if bias is not None:
                kw["bias"] = bias
            if scale is not None:
                kw["scale"] = scale
            tr.emit("act", lambda a: a.activation(out=out, in_=in_, func=func, **kw), reads, writes)

        def tt(eng, out, in0, in1, op, reads, writes):
            tr.emit(eng, lambda v: v.tensor_tensor(out=out, in0=in0, in1=in1, op=op), reads, writes)

        def ts(eng, out, in0, s1, s2, op0, op1, reads, writes):
            if op1 is None:
                tr.emit(eng, lambda v: v.tensor_scalar(out=out, in0=in0, scalar1=s1, scalar2=None, op0=op0), reads, writes)
            else:
                tr.emit(eng, lambda v: v.tensor_scalar(out=out, in0=in0, scalar1=s1, scalar2=s2, op0=op0, op1=op1), reads, writes)

        def stt(out, in0, scalar, in1, op0, op1, reads, writes):
            tr.emit("dve", lambda v: v.scalar_tensor_tensor(out=out, in0=in0, scalar=scalar, in1=in1, op0=op0, op1=op1), reads, writes)

        def cp(eng, out, in_, reads, writes):
            if eng == "act":
                tr.emit("act", lambda a: a.copy(out=out, in_=in_), reads, writes)
            else:
                tr.emit(eng, lambda v: v.tensor_copy(out=out, in_=in_), reads, writes)

        def dma(eng, key, out, in_, reads, writes):
            tr.dma(eng, key, lambda g: g.dma_start(out=out, in_=in_), reads, writes)

        def dbg(name, ap, shape, res, dt=F32):
            if name not in debug:
                return
            o = dout("dbg_" + name, shape, dt)
            r = Res("dbg_" + name)
            dma("sp", "dbg", o, ap, [res], [r])
            out_res.append(r)
            dbg_outs[name] = shape

        def rsqrt(n, mean_ap, var_ap, eps, rs_ap, reads, r_rs):
            xe = rsq[:n, 0:1]
            yy = rsq[:n, 1:2]
            t_ = rsq[:n, 2:3]
            ti = rsqi[:n, 0:1]
            if isinstance(eps, float):
                ts("dve", xe, var_ap, eps, None, ALU.add, None, reads + [r_stat], [r_stat])
            else:
                tt("dve", xe, var_ap, eps, ALU.add, reads + [r_stat, r_const], [r_stat])
            ts("dve", ti, xe.bitcast(I32), 1, None, ALU.arith_shift_right, None, [r_stat], [r_stat])
            ts("dve", yy.bitcast(I32), ti, -1.0, 1597463007.0, ALU.mult, ALU.add, [r_stat], [r_stat])
            for it in range(3):
                stt(t_, xe, yy, yy, ALU.mult, ALU.mult, [r_stat], [r_stat])
                ts("dve", t_, t_, -0.5, 1.5, ALU.mult, ALU.add, [r_stat], [r_stat])
                if it < 2:
                    tt("dve", yy, yy, t_, ALU.mult, [r_stat], [r_stat])
                else:
                    tt("dve", rs_ap[:n, 0:1], yy, t_, ALU.mult, [r_stat], [r_stat, r_rs])
            stt(rs_ap[:n, 1:2], mean_ap, -1.0, rs_ap[:n, 0:1], ALU.mult, ALU.mult, reads + [r_stat, r_rs], [r_rs])

        def rsqrt_multi(n, k, mean_ap, var_ap, eps_ap, rs3, reads, r_rs):
            xe = rsq4[:n, 0, 0:k]
            yy = rsq4[:n, 1, 0:k]
            t_ = rsq4[:n, 2, 0:k]
            ti = rsq4i[:n, 0:k]
            if isinstance(eps_ap, float):
                ts("dve", xe, var_ap, eps_ap, None, ALU.add, None, reads + [r_stat], [r_stat])
            else:
                tt("dve", xe, var_ap, eps_ap, ALU.add, reads + [r_stat, r_const], [r_stat])
            ts("dve", ti, xe.bitcast(I32), 1, None, ALU.arith_shift_right, None, [r_stat], [r_stat])
            ts("dve", yy.bitcast(I32), ti, -1.0, 1597463007.0, ALU.mult, ALU.add, [r_stat], [r_stat])
            for it in range(3):
                tt("dve", t_, xe, yy, ALU.mult, [r_stat], [r_stat])
                tt("dve", t_, t_, yy, ALU.mult, [r_stat], [r_stat])
                ts("dve", t_, t_, -0.5, 1.5, ALU.mult, ALU.add, [r_stat], [r_stat])
                if it < 2:
                    tt("dve", yy, yy, t_, ALU.mult, [r_stat], [r_stat])
                else:
                    tt("dve", rs3[:n, 0:k, 0], yy, t_, ALU.mult, [r_stat], [r_stat, r_rs])
            stt(rs3[:n, 0:k, 1], mean_ap, -1.0, rs3[:n, 0:k, 0], ALU.mult, ALU.mult, reads + [r_stat, r_rs], [r_rs])

        r_cast = [Res(f"cast{g}") for g in range(NCAST)]
        wst = {"next_load": 0, "next_use": 0}
        NPASS = 9
        TOTAL = NPASS * NPAN

        r_wbf = [Res(f"wbf{i}") for i in range(NPAN)]
        wunits = [(xa[:, 1, b, :], r_xa[1][b]) for b in range(4)] + [(xa[:, 0, b, :], r_xa[0][b]) for b in range(1, 4)]
        pend_store = []

        def w_flush_store(keep):
            while len(pend_store) > keep:
                pidx, slot = pend_store.pop(0)
                dma("sp", f"wst{slot}", wbf[pidx], ring[:, slot, :], [r_ring[slot]], [r_wbf[pidx]])

        pend_cast = []

        def emit_cast(g):
            slot = g % NS
            pidx = g % NPAN
            for hf in range(2):
                u = (2 * g + hf) % len(wunits)
                uap, ures = wunits[u]
                cp("act" if hf == 0 else "dve", ring[:, slot, hf * 1024:(hf + 1) * 1024], uap, [ures], [r_ring[slot]])
            pend_store.append((pidx, slot))

        def w_load_one():
            g = wst["next_load"]
            if g >= TOTAL:
                return
            wst["next_load"] += 1
            slot = g % NS
            pidx = g % NPAN
            if g < NPAN:
                for hf in range(2):
                    u = (2 * g + hf) % len(wunits)
                    uap, ures = wunits[u]
                    dma("sp", f"wld{u}", uap, wpan[pidx][:, hf * 1024:(hf + 1) * 1024], [], [ures])
                pend_cast.append(g)
                while len(pend_cast) > 2:
                    emit_cast(pend_cast.pop(0))
                w_flush_store(2)
            else:
                while pend_cast:
                    emit_cast(pend_cast.pop(0))
                w_flush_store(0)
                dma("sp", f"ring{slot}", ring[:, slot, :], wbf[pidx], [r_wbf[pidx]], [r_ring[slot]])

        pan_order = []

        def w_next(key):
            g = wst["next_use"]
            wst["next_use"] += 1
            if g < NPAN:
                pan_order.append(key)
            else:
                assert pan_order[g % NPAN] == key, (g, key, pan_order[g % NPAN])
            slot = g % NS
            return ring[:, slot, :], r_ring[slot]

        def w_release(k=1):
            for _ in range(k):
                w_load_one()

        cl = [(cTt[:], cT), (vfmt[:], vfm), (cwt[:], cw), (fwt[:], fw), (bmt[:], bmod),
              (maskt[:], maskT), (epst[:], epsv), (kdt[:], kdec), (offt[:], offs),
              (idb[:], identb), (idf[:], identf)]
        for o, i in cl:
            dma("sp", "const", o, i, [], [r_const])
        dma("sp", "ptm", ptm[:], vtm, [], [r_ptm])
        dma("sp", "hist_u", uh[:, 2:4, :, :], sconv.rearrange("s p c r -> p s c r"), [], [r_uh])
        dma("sp", "hist_a", ah[:, 2:4, :, :], sffn.rearrange("s p c r -> p s c r"), [], [r_ah])
        tr.emit("pool", lambda g: g.memset(uh[:, 0:2, :, :], 0.0), [], [r_uh])
        tr.emit("pool", lambda g: g.memset(ah[:, 0:2, :, :], 0.0), [], [r_ah])
        tr.emit("pool", lambda g: g.memset(ones[:], 1.0), [], [r_ones])
        tr.dma("pool", "browc", lambda q: q.dma_start(out=browb[:], in_=brow), [], [r_browb])
        for i_ in range(4):
            tr.emit("act", (lambda i_: (lambda a: a.activation(out=ptm[:, i_, :], in_=ptm[:, i_, :], func=AF.Copy, scale=float(ALPHA))))(i_),
                    [r_ptm], [r_ptm])
        pm, r_pm = psum()
        wmb = gbuf[:, 0:4096].rearrange("p (b f) -> p b f", f=1024)
        r_wmb = [Res(f"wmb{i}") for i in range(4)]
        wm32 = xa[:, 1, :, :]
        r_wm = [Res(f"wm{i}") for i in range(4)]
        cTb = gbuf[:, 4096:4128].rearrange("p (k s) -> p k s", s=4)
        r_cTb = Res("cTb")
        cp("dve", cTb, cTt[:], [r_const], [r_cTb])
        for oc in range(48):
            b_ = oc % 4
            dma("sp", f"wm{b_}", wm32[:, b_, :], wmod[oc], [], [r_wm[b_]])
            cp("act" if oc % 2 == 0 else "dve", wmb[:, b_, :], wm32[:, b_, :], [r_wm[b_]], [r_wmb[b_]])
            for kc in range(8):
                mm(pm[:, oc * 4:(oc + 1) * 4], wmb[:, b_, kc * 128:(kc + 1) * 128], cTb[:, kc, :],
                   kc == 0, kc == 7, [r_wmb[b_], r_cTb], [r_pm], kc == 7)
        tt("dve", modfm[:], pm[:, 0:192].rearrange("p (a b) -> p a b", b=4),
           bmt[:].unsqueeze(2).broadcast_to([128, 48, 4]), ALU.add, [r_pm, r_const], [r_modfm])
        for which, (gi, sci, shi) in enumerate([(0, 1, 0), (2, 4, 3)]):
            one_sc = tmpA[:, 0, 0:32].rearrange("p (c s) -> p c s", s=4)
            ts("dve", one_sc, modfm[:, sci * 8:(sci + 1) * 8, :], 1.0, None, ALU.add, None, [r_modfm], [r_tmpA[0]])
            tt("dve", mA[:, 2 * which, :, :], one_sc, vfmt[:, gi, :].unsqueeze(2).broadcast_to([128, 8, 4]), ALU.mult,
               [r_tmpA[0], r_const], [r_mA])
            tt("dve", mA[:, 2 * which + 1, :, :], one_sc, vfmt[:, gi + 1, :].unsqueeze(2).broadcast_to([128, 8, 4]), ALU.mult,
               [r_tmpA[0], r_const], [r_mA])
            tt("dve", mA[:, 2 * which + 1, :, :], mA[:, 2 * which + 1, :, :], modfm[:, shi * 8:(shi + 1) * 8, :], ALU.add,
               [r_mA, r_modfm], [r_mA])

        def build_gtc(part_slots):
            for which, mi in enumerate((2, 5)):
                for half in range(2):
                    pg, r_pg = psum()
                    for cc in range(4):
                        c = half * 4 + cc
                        for (p0, p1, slot) in part_slots:
                            cp("dve", bcl[:, p0:p1], modfm[:, mi * 8 + c, slot:slot + 1].broadcast_to([128, p1 - p0]),
                               [r_modfm], [r_bcl])
                        npart = part_slots[-1][1]
                        mm(pg[:npart, cc * 128:(cc + 1) * 128], bcl[:, 0:npart], idf[:], True, True,
                           [r_bcl, r_const], [r_pg], True)
                    npart = part_slots[-1][1]
                    cp("act", gtc[:npart, which, half * 512:(half + 1) * 512], pg[:npart, :], [r_pg], [r_gtc])

        for b_ in range(4):
            r_xa[1][b_].absorb([r_wm[b_]])
        for _ in range(NS):
            w_load_one()

        class Pass:
            pass

        def make_pass(idx, kind, seq, st):
            P = Pass()
            P.idx, P.kind, P.seq, P.st = idx, kind, seq, st
            P.par = idx % 2
            if kind == "prompt":
                P.T = 512
                P.blocks = [(b * 128, 128) for b in range(4)]
                P.segs = [(0, 512, seq)]
                P.n = 128
                P.kd = 0
                P.rope_src = rope_p[:, :, st * 512:(st + 1) * 512]
                P.last = (st == 3)
                P.Lr = 512
            else:
                P.T = 64
                P.blocks = [(0, 64)]
                P.segs = [(0, 32, 2), (32, 32, 3)]
                P.n = 32
                P.kd = 1
                P.rope_src = rope_s
                P.last = True
                P.Lr = 32
            P.gL = [GAMMA[h] ** P.Lr for h in range(H)]
            return P

        def seg_of_block(P, tok0, nb):
            return [(max(o, tok0), min(o + L, tok0 + nb), slot) for (o, L, slot) in P.segs
                    if max(o, tok0) < min(o + L, tok0 + nb)]

        def ln_block(P, bi, tok0, nb, load_from, whichA, do_T, affine_idx):
            ln_part1(P, bi, tok0, nb, load_from, whichA, do_T, affine_idx)
            if do_T:
                ln_part2(P, bi, tok0, nb, whichA)

        def ln_part1(P, bi, tok0, nb, load_from, whichA, do_T, affine_idx):
            par = P.par
            xab = xa[:nb, par, bi, :]
            rx = r_xa[par][bi]
            if load_from is not None:
                dma("sp", f"xld{par}{bi}", xab, load_from, [], [rx])
            for i in range(2):
                tr.emit("dve", (lambda i: (lambda v: v.bn_stats(out=st6[:nb, i, :], in_=xa[:nb, par, bi, i * 512:(i + 1) * 512])))(i),
                        [rx, r_stat], [r_stat])
            tr.emit("dve", lambda v: v.bn_aggr(out=mvt[:nb, :], in_=st6[:nb, :, :].rearrange("p a b -> p (a b)")),
                    [r_stat], [r_stat])
            rs_ap, r_rs = rstd_slot()
            rsqrt(nb, mvt[:nb, 0:1], mvt[:nb, 1:2], LN_EPS, rs_ap, [], r_rs)
            if do_T:
                act(xnb[:nb, :], xab, AF.Identity, [rx, r_rs], [r_xnb], bias=rs_ap[:nb, 1:2], scale=rs_ap[:nb, 0:1])
            act(xab, xab, AF.Identity, [rx, r_rs], [rx], bias=rs_ap[:nb, 1:2], scale=rs_ap[:nb, 0:1])
            tt("pool", xab, xab, ptm[:nb, affine_idx, :], ALU.mult, [rx, r_ptm], [rx])
            tt("pool", xab, xab, ptm[:nb, affine_idx + 1, :], ALU.add, [rx, r_ptm], [rx])

        def ln_part2(P, bi, tok0, nb, whichA):
            pt, r_pt = psum()
            ptb = pt[:].bitcast(BF16)
            for c in range(8):
                transp(ptb[:, c * 128:c * 128 + nb], xnb[:nb, c * 128:(c + 1) * 128], idb[:nb, :nb],
                       [r_xnb, r_const], [r_pt], c == 7)
            for c in range(8):
                for (a0, a1, slot) in seg_of_block(P, tok0, nb):
                    if c % 2 == 0:
                        ts("dve", hT[:, c, a0:a1], ptb[:, c * 128 + (a0 - tok0):c * 128 + (a1 - tok0)],
                           mA[:, 2 * whichA, c, slot:slot + 1], mA[:, 2 * whichA + 1, c, slot:slot + 1],
                           ALU.mult, ALU.add, [r_pt, r_mA], [r_hT])
                    else:
                        act(hT[:, c, a0:a1], ptb[:, c * 128 + (a0 - tok0):c * 128 + (a1 - tok0)], AF.Identity,
                            [r_pt, r_mA], [r_hT], bias=mA[:, 2 * whichA + 1, c, slot:slot + 1], scale=mA[:, 2 * whichA, c, slot:slot + 1])

        def stage_A(P):
            tr.tag = f"{P.kind}{P.seq}{P.st}.A"
            dma("sp", "rope", rope[:, :, 0:P.T], P.rope_src, [], [r_rope])
            for bi, (tok0, nb) in enumerate(P.blocks):
                src = xp[P.seq, P.st * 512 + tok0: P.st * 512 + tok0 + nb, :] if P.kind == "prompt" else xs
                ln_block(P, bi, tok0, nb, src, 0, True, 0)

        def stage_pre(P):
            if P.kind == "sample":
                build_gtc([(0, 32, 2), (32, 64, 3)])
            elif P.st == 0:
                build_gtc([(0, 128, P.seq)])
                for h in range(H):
                    tr.emit("pool", (lambda h: (lambda g: g.memset(S_p[:, h, :, :], 0.0)))(h), [], [r_S[h]])

        def task_P1(P, h):
            tr.tag = f"{P.kind}{P.seq}{P.st}.P1{h}"
            T = P.T
            hb = h % 2
            for which in range(2):
                wp_, r_wp = w_next(("q" if which == 0 else "k", h))
                pss = []
                for e in range(2):
                    p_, r_p = psum()
                    for kc in range(8):
                        mm(p_[:, 0:T], wp_[:, kc * 256 + e * 128: kc * 256 + (e + 1) * 128], hT[:, kc, 0:T],
                           kc == 0, kc == 7, [r_wp, r_hT], [r_p], kc == 7)
                    pss.append((p_, r_p))
                w_release()
                (x1, r1), (x2, r2) = pss
                cos = rope[:, 0, 0:T]
                sin = rope[:, 1, 0:T]
                t1, rt1 = tmp32()
                t2, rt2 = tmp32()
                tt("dve", t1[:, 0:T], x1[:, 0:T], cos, ALU.mult, [r1, r_rope], [rt1])
                tt("dve", t2[:, 0:T], x2[:, 0:T], sin, ALU.mult, [r2, r_rope], [rt2])
                tt("dve", qk[:, hb, which, 0, 0:T], t1[:, 0:T], t2[:, 0:T], ALU.subtract, [rt1, rt2], [r_qk[hb]])
                t3, rt3 = tmp32()
                t4, rt4 = tmp32()
                tt("dve", t3[:, 0:T], x1[:, 0:T], sin, ALU.mult, [r1, r_rope], [rt3])
                tt("dve", t4[:, 0:T], x2[:, 0:T], cos, ALU.mult, [r2, r_rope], [rt4])
                tt("dve", qk[:, hb, which, 1, 0:T], t3[:, 0:T], t4[:, 0:T], ALU.add, [rt3, rt4], [r_qk[hb]])

        def rb_list(P):
            out = []
            for si, (o, L, slot) in enumerate(P.segs):
                for ib in range(L // P.n):
                    out.append((si, ib, o + ib * P.n, slot))
            return out

        def task_P2(P, h):
            tr.tag = f"{P.kind}{P.seq}{P.st}.P2{h}"
            T = P.T
            n = P.n
            hb = h % 2
            wv0, r_wv0 = w_next(("v", h, 0))
            wv1, r_wv1 = w_next(("v", h, 1))
            for rb, (si, ib, t0, slot) in enumerate(rb_list(P)):
                p_, r_p = psum()
                for kc in range(8):
                    wv = wv0 if kc < 4 else wv1
                    rw = r_wv0 if kc < 4 else r_wv1
                    mm(p_[:n, :], hT[:, kc, t0:t0 + n], wv[:, (kc % 4) * 512:(kc % 4 + 1) * 512],
                       kc == 0, kc == 7, [rw, r_hT], [r_p], kc == 7)
                cp("act", vsb[:n, hb, rb, :], p_[:n, :], [r_p], [r_v[hb], r_merged])
            w_release(2)
            for j2 in range(2):
                wg, r_wg = w_next(("g", h, j2))
                for jj in range(2):
                    j = j2 * 2 + jj
                    p_, r_p = psum()
                    for kc in range(8):
                        mm(p_[:, 0:T], wg[:, kc * 256 + jj * 128: kc * 256 + (jj + 1) * 128], hT[:, kc, 0:T],
                           kc == 0, kc == 7, [r_wg, r_hT], [r_p], kc == 7)
                    act(sgb[:, hb, j, 0:T], p_[:, 0:T], AF.Silu, [r_p], [r_sg[hb]])
                w_release()

        def S_of(P, h, slot):
            if P.kind == "sample" and slot == 3:
                hh = (h + 2) % 4
            else:
                hh = h
            return S_p[:, hh, :, :], r_S[hh]

        def task_R1(P, h):
            tr.tag = f"{P.kind}{P.seq}{P.st}.R1{h}"
            n = P.n
            hb = h % 2
            qT = qk[:, hb, 0, :, :]
            kT = qk[:, hb, 1, :, :]
            rbl = rb_list(P)
            for si, (o, L, slot) in enumerate(P.segs):
                S_ap, r_Sx = S_of(P, h, slot)
                if P.kind == "sample":
                    dma("sp", f"sld{slot}{h}", S_ap, sret[slot - 2, h].rearrange("e p v -> p e v"), [], [r_Sx])
                sbi = hb if P.kind == "prompt" else si
                cp("act", Sbf[:, sbi, :, :], S_ap, [r_Sx], [r_Sbf[sbi]])
            blk = 0
            for rb, (si, ib, t0, slot) in enumerate(rbl):
                seg_t0 = P.segs[si][0]
                for jb in range(ib + 1):
                    tj = seg_t0 + jb * n
                    psc, r_psc = psum()
                    for e in range(2):
                        mm(psc[:n, 0:n], kT[:, e, tj:tj + n], qT[:, e, t0:t0 + n], e == 0, e == 1, [r_qk[hb]], [r_psc], e == 1)
                    if jb == ib:
                        stt(smT[:n, blk, 0:n], psc[:n, 0:n], float(GAMMA[h] ** (-128.0 * jb)), maskt[:n, h, 0:n], ALU.mult, ALU.mult,
                            [r_psc, r_const], [r_smT])
                    else:
                        act(smT[:n, blk, 0:n], psc[:n, 0:n], AF.Identity, [r_psc, r_const], [r_smT], scale=offt[:n, h, jb:jb + 1])
                    blk += 1
            for rb, (si, ib, t0, slot) in enumerate(rbl):
                pkt, r_pkt = psum()
                pktb = pkt[:].bitcast(BF16)
                for e in range(2):
                    transp(pktb[:n, e * 128:(e + 1) * 128], kT[:, e, t0:t0 + n], idb[:, :], [r_qk[hb], r_const], [r_pkt], e == 1)
                act(ktok[:n, rb, :], pktb[:n, 0:256], AF.Identity, [r_pkt, r_const], [r_ktok], scale=kdt[:n, P.kd, ib, h:h + 1])

        def task_R2a(P, h):
            tr.tag = f"{P.kind}{P.seq}{P.st}.R2a{h}"
            n = P.n
            hb = h % 2
            qT = qk[:, hb, 0, :, :]
            rbl = rb_list(P)
            nrb = len(rbl)
            blk = 0
            pos = []
            for rb, (si, ib, t0, slot) in enumerate(rbl):
                po, r_po = psum()
                for jb in range(ib + 1):
                    rbj = rb - ib + jb
                    mm(po[:n, :], smT[:n, blk, 0:n], vsb[:n, hb, rbj, :], jb == 0, False, [r_smT, r_v[hb]], [r_po], False)
                    blk += 1
                sbi = hb if P.kind == "prompt" else si
                for e in range(2):
                    mm(po[:n, :], qT[:, e, t0:t0 + n], Sbf[:, sbi, e, :], False, e == 1, [r_qk[hb], r_Sbf[sbi]], [r_po], e == 1)
                pos.append((po, r_po))
                tr.emit("dve", (lambda n, po, rb: (lambda v: v.bn_stats(out=st64[:n, rb, :], in_=po[:n, :])))(n, po, rb), [r_po, r_stat], [r_stat])
                tr.emit("dve", (lambda n, rb: (lambda v: v.bn_aggr(out=mv4[:n, rb, :], in_=st64[:n, rb, :])))(n, rb), [r_stat], [r_stat])
            if P.kind == "prompt":
                eps_ap = epst[:n, 0:nrb, h]
            else:
                eps_ap = epst[:n, 0:1, h].broadcast_to([n, nrb])
            rs3 = rs4[:, hb, :, :]
            rsqrt_multi(n, nrb, mv4[:n, 0:nrb, 0], mv4[:n, 0:nrb, 1], eps_ap, rs3, [], r_rs4[hb])
            for rb, (po, r_po) in enumerate(pos):
                act(onb[:n, rb, :], po[:n, :], AF.Identity, [r_po, r_rs4[hb]], [r_onb[rb]], bias=rs3[:n, rb, 1:2], scale=rs3[:n, rb, 0:1])
            for si, (o, L, slot) in enumerate(P.segs):
                S_ap, r_Sx = S_of(P, h, slot)
                rbs = [rb for rb, x in enumerate(rbl) if x[0] == si]
                for e in range(2):
                    pkv, r_pkv = psum()
                    for k_, rb in enumerate(rbs):
                        mm(pkv[:, :], ktok[:n, rb, e * 128:(e + 1) * 128], vsb[:n, hb, rb, :], k_ == 0, k_ == len(rbs) - 1,
                           [r_ktok, r_v[hb]], [r_pkv], k_ == len(rbs) - 1)
                    stt(S_ap[:, e, :], S_ap[:, e, :], float(P.gL[h]), pkv[:, :], ALU.mult, ALU.add, [r_Sx, r_pkv], [r_Sx])
                if P.last:
                    r_o = Res("rs_o")
                    dma("sp", f"strs{slot}{h}", rs_o[slot, h].rearrange("e p v -> p e v"), S_ap, [r_Sx], [r_o])
                    out_res.append(r_o)

        def task_R2b(P, h):
            tr.tag = f"{P.kind}{P.seq}{P.st}.R2b{h}"
            n = P.n
            hb = h % 2
            for rb, (si, ib, t0, slot) in enumerate(rb_list(P)):
                pot, r_pot = psum()
                potb = pot[:].bitcast(BF16)
                for d in range(4):
                    transp(potb[:, d * 128:d * 128 + n], onb[:n, rb, d * 128:(d + 1) * 128], idb[:n, :n],
                           [r_onb[rb], r_const], [r_pot], d == 3)
                tt("dve", gatedT[:, h * 4:(h + 1) * 4, t0:t0 + n],
                   potb[:, 0:512].rearrange("p (d t) -> p d t", t=128)[:, :, 0:n],
                   sgb[:, hb, :, t0:t0 + n], ALU.mult, [r_pot, r_sg[hb]], [r_gated] + r_actT)

        def task_C(P, cpair):
            tr.tag = f"{P.kind}{P.seq}{P.st}.C{cpair}"
            T = P.T
            wcx = [w_next(("cx", cpair * 2)), w_next(("cx", cpair * 2 + 1))]
            wbg, r_wbg = w_next(("bg", cpair))
            for ci in range(2):
                c = cpair * 2 + ci
                wc, r_wc = wcx[ci]
                pcg, r_pcg = psum()
                pxi, r_pxi = psum()
                pbg, r_pbg = psum()
                for kc in range(8):
                    mm(pcg[:, 0:T], wc[:, kc * 256: kc * 256 + 128], hT[:, kc, 0:T], kc == 0, kc == 7, [r_wc, r_hT], [r_pcg], kc == 7)
                for kc in range(8):
                    mm(pxi[:, 0:T], wc[:, kc * 256 + 128: kc * 256 + 256], hT[:, kc, 0:T], kc == 0, kc == 7, [r_wc, r_hT], [r_pxi], kc == 7)
                for kc in range(8):
                    mm(pbg[:, 0:T], wbg[:, kc * 256 + ci * 128: kc * 256 + (ci + 1) * 128], hT[:, kc, 0:T], kc == 0, kc == 7,
                       [r_wbg, r_hT], [r_pbg], kc == 7)
                cgs, r_cgs = tmp32()
                cp("act", cgs[:, 0:T], pcg[:, 0:T], [r_pcg], [r_cgs])
                ui = c % 2
                y1, r_y1 = tmp32()
                y2, r_y2 = tmp32()
                for si, (o, L, slot) in enumerate(P.segs):
                    u0 = si * (L + 2)
                    cp("pool", ut[:, ui, u0:u0 + 2], uh[:, slot, c, :], [r_uh], [r_ut[ui]])
                    tt("dve", ut[:, ui, u0 + 2:u0 + 2 + L], pxi[:, o:o + L], cgs[:, o:o + L], ALU.mult, [r_pxi, r_cgs], [r_ut[ui]])
                    cp("pool", uh[:, slot, c, :], ut[:, ui, u0 + L:u0 + L + 2], [r_ut[ui]], [r_uh])
                    act(y1[:, o:o + L], ut[:, ui, u0 + 2:u0 + 2 + L], AF.Identity, [r_ut[ui], r_const], [r_y1],
                        bias=cwt[:, c, 3:4], scale=cwt[:, c, 2:3])
                    stt(y2[:, o:o + L], ut[:, ui, u0 + 1:u0 + 1 + L], cwt[:, c, 1:2], y1[:, o:o + L], ALU.mult, ALU.add,
                        [r_ut[ui], r_y1, r_const], [r_y2])
                    stt(y1[:, o:o + L], ut[:, ui, u0:u0 + L], cwt[:, c, 0:1], y2[:, o:o + L], ALU.mult, ALU.add,
                        [r_ut[ui], r_y2, r_const], [r_y1])
                tt("dve", bgc[:, c, 0:T], y1[:, 0:T], pbg[:, 0:T], ALU.mult, [r_y1, r_pbg], [r_bgc] + r_actT)
            w_release(3)

        def stage_D(P):
            tr.tag = f"{P.kind}{P.seq}{P.st}.D"
            T = P.T
            woc = None
            for m in range(8):
                wor, r_wor = w_next(("wor", m))
                if m % 2 == 0:
                    woc, r_woc = w_next(("woc", m // 2))
                wgt, r_wgt = w_next(("gate", m))
                pgr, r_pgr = psum()
                pgc, r_pgc = psum()
                for kc in range(8):
                    mm(pgr[:, 0:T], wgt[:, kc * 256: kc * 256 + 128], hT[:, kc, 0:T], kc == 0, kc == 7, [r_wgt, r_hT], [r_pgr], kc == 7)
                for kc in range(8):
                    mm(pgc[:, 0:T], wgt[:, kc * 256 + 128: kc * 256 + 256], hT[:, kc, 0:T], kc == 0, kc == 7, [r_wgt, r_hT], [r_pgc], kc == 7)
                sr, r_sr = tmp32()
                sc, r_sc = tmp32()
                act(sr[:, 0:T], pgr[:, 0:T], AF.Sigmoid, [r_pgr], [r_sr])
                act(sc[:, 0:T], pgc[:, 0:T], AF.Sigmoid, [r_pgc], [r_sc])
                pyc, r_pyc = psum()
                mi = m % 2
                for kc in range(8):
                    mm(pyc[:, 0:T], woc[:, kc * 256 + mi * 128: kc * 256 + (mi + 1) * 128], bgc[:, kc, 0:T], kc == 0, kc == 7,
                       [r_woc, r_bgc], [r_pyc], kc == 7)
                pyr, r_pyr = psum()
                for kc in range(16):
                    mm(pyr[:, 0:T], wor[:, kc * 128:(kc + 1) * 128], gatedT[:, kc, 0:T], kc == 0, kc == 15, [r_wor, r_gated], [r_pyr], kc == 15)
                tt("dve", sc[:, 0:T], sc[:, 0:T], pyc[:, 0:T], ALU.mult, [r_sc, r_pyc], [r_sc])
                tt("dve", sr[:, 0:T], sr[:, 0:T], pyr[:, 0:T], ALU.mult, [r_sr, r_pyr], [r_sr])
                tt("pool", mergedT[:, m, 0:T], sr[:, 0:T], sc[:, 0:T], ALU.add, [r_sr, r_sc], [r_merged, r_v[0], r_v[1]])
                w_release(4 if m % 2 == 1 else 1)

        def stage_E(P):
            tr.tag = f"{P.kind}{P.seq}{P.st}.E"
            par = P.par
            pans = [[w_next(("wout", ch, half)) for half in range(2)] for ch in range(2)]
            for bi, (tok0, nb) in enumerate(P.blocks):
                rx = r_xa[par][bi]
                for ch in range(2):
                    pa, r_pa = psum()
                    for kc in range(8):
                        wp_, r_wp = pans[ch][kc // 4]
                        mm(pa[:nb, :], mergedT[:, kc, tok0:tok0 + nb], wp_[:, (kc % 4) * 512:(kc % 4 + 1) * 512],
                           kc == 0, False, [r_wp, r_merged], [r_pa], False)
                    mm(pa[:nb, :], ones[0:1, 0:nb], browb[0:1, ch * 512:(ch + 1) * 512], False, True, [r_ones, r_browb], [r_pa], True)
                    t_, r_t = tmp32()
                    tt("dve", t_[:nb, :], pa[:nb, :], gtc[:nb, 0, ch * 512:(ch + 1) * 512], ALU.mult, [r_pa, r_gtc], [r_t])
                    tt("dve", xa[:nb, par, bi, ch * 512:(ch + 1) * 512], xa[:nb, par, bi, ch * 512:(ch + 1) * 512], t_[:nb, :], ALU.add,
                       [rx, r_t], [rx])
                for i in range(2):
                    tr.emit("dve", (lambda i, bi, nb: (lambda v: v.bn_stats(out=st6b[:nb, bi, i, :], in_=xa[:nb, par, bi, i * 512:(i + 1) * 512])))(i, bi, nb),
                            [rx, r_statB], [r_statB])
                tr.emit("dve", (lambda bi, nb: (lambda v: v.bn_aggr(out=mv4b[:nb, bi, :], in_=st6b[:nb, bi, :, :].rearrange("p a b -> p (a b)"))))(bi, nb),
                        [r_statB], [r_statB])
            w_release(4)
            k = len(P.blocks)
            nbm = P.blocks[0][1]
            rsqrt_multi(nbm, k, mv4b[:nbm, 0:k, 0], mv4b[:nbm, 0:k, 1], LN_EPS, rsB, [r_statB], r_rsB)
            for bi, (tok0, nb) in enumerate(P.blocks):
                xab = xa[:nb, par, bi, :]
                rx = r_xa[par][bi]
                act(xnb[:nb, :], xab, AF.Identity, [rx, r_rsB], [r_xnb], bias=rsB[:nb, bi, 1:2], scale=rsB[:nb, bi, 0:1])
                ln_part2(P, bi, tok0, nb, 1)
                act(xab, xab, AF.Identity, [rx, r_rsB], [rx], bias=rsB[:nb, bi, 1:2], scale=rsB[:nb, bi, 0:1])
                tt("pool", xab, xab, ptm[:nb, 2, :], ALU.mult, [rx, r_ptm], [rx])
                tt("pool", xab, xab, ptm[:nb, 3, :], ALU.add, [rx, r_ptm], [rx])

        def ln_stats_multi(P, blocks, st6x, mv4x, r_statX, load_src):
            par = P.par
            for bi, (tok0, nb) in enumerate(blocks):
                rx = r_xa[par][bi]
                if load_src is not None:
                    dma("sp", f"xld{par}{bi}", xa[:nb, par, bi, :], load_src(tok0, nb), [], [rx])
                for i in range(2):
                    tr.emit("dve", (lambda i, bi, nb: (lambda v: v.bn_stats(out=st6x[:nb, bi, i, :], in_=xa[:nb, par, bi, i * 512:(i + 1) * 512])))(i, bi, nb),
                            [rx, r_statX], [r_statX])
                tr.emit("dve", (lambda bi, nb: (lambda v: v.bn_aggr(out=mv4x[:nb, bi, :], in_=st6x[:nb, bi, :, :].rearrange("p a b -> p (a b)"))))(bi, nb),
                        [r_statX], [r_statX])

        def make_astep(Pn):
            ablocks = list(enumerate(Pn.blocks)) if Pn is not None else []
            stA = {"i": 0, "phase": 0}

            def a_step():
                if Pn is None:
                    return
                tr.tag = f"{Pn.kind}{Pn.seq}{Pn.st}.A"
                par = Pn.par
                if stA["phase"] == 0:
                    stA["phase"] = 1
                    dma("sp", "rope", rope[:, :, 0:Pn.T], Pn.rope_src, [], [r_rope])
                    ln_stats_multi(Pn, Pn.blocks, st6a, mv4a, r_statA,
                                   lambda tok0, nb: xp[Pn.seq, Pn.st * 512 + tok0: Pn.st * 512 + tok0 + nb, :])
                    k = len(Pn.blocks)
                    rsqrt_multi(128, k, mv4a[:, 0:k, 0], mv4a[:, 0:k, 1], LN_EPS, rsA, [r_statA], r_rsA)
                    return
                if stA["i"] < len(ablocks):
                    bi, (tok0, nb) = ablocks[stA["i"]]
                    stA["i"] += 1
                    xab = xa[:nb, par, bi, :]
                    rx = r_xa[par][bi]
                    act(xnb[:nb, :], xab, AF.Identity, [rx, r_rsA], [r_xnb], bias=rsA[:nb, bi, 1:2], scale=rsA[:nb, bi, 0:1])
                    act(xab, xab, AF.Identity, [rx, r_rsA], [rx], bias=rsA[:nb, bi, 1:2], scale=rsA[:nb, bi, 0:1])
                    tt("pool", xab, xab, ptm[:nb, 0, :], ALU.mult, [rx, r_ptm], [rx])
                    tt("pool", xab, xab, ptm[:nb, 1, :], ALU.add, [rx, r_ptm], [rx])
                    ln_part2(Pn, bi, tok0, nb, 0)

            def a_done():
                return Pn is None or stA["i"] >= len(ablocks)
            return a_step, a_done

        def stage_FG(P, Pn):
            a_step, a_done = make_astep(Pn)
            par = P.par
            T = P.T
            accs = {}

            def g_panel(ch, pi):
                tr.tag = f"{P.kind}{P.seq}{P.st}.G"
                if ch not in accs:
                    accs[ch] = [psum(hold=True) for _ in P.blocks]
                wp_, r_wp = w_next(("down", ch, pi))
                kcs = list(range(pi * 4, min(pi * 4 + 4, NFF)))
                for bi, (tok0, nb) in enumerate(P.blocks):
                    pa, r_pa = accs[ch][bi]
                    for kl, kc in enumerate(kcs):
                        mm(pa[:nb, :], actT[:, kc, tok0:tok0 + nb], wp_[:, kl * 512:(kl + 1) * 512],
                           kc == 0, False, [r_wp, r_actT[kc]], [r_pa],
                           (pi < 5 and bi == len(P.blocks) - 1 and kl == len(kcs) - 1))
                    if pi == 5:
                        mm(pa[:nb, :], ones[0:1, 0:nb], browb[0:1, D + ch * 512: D + (ch + 1) * 512],
                           False, True, [r_ones, r_browb], [r_pa], True)
                w_release()

            def g_evac(ch):
                for bi, (tok0, nb) in enumerate(P.blocks):
                    pa, r_pa = accs[ch][bi]
                    rx = r_xa[par][bi]
                    t_, r_t = tmp32()
                    tt("dve", t_[:nb, :], pa[:nb, :], gtc[:nb, 1, ch * 512:(ch + 1) * 512], ALU.mult, [r_pa, r_gtc], [r_t])
                    tt("pool", xa[:nb, par, bi, ch * 512:(ch + 1) * 512], xa[:nb, par, bi, ch * 512:(ch + 1) * 512], t_[:nb, :], ALU.add,
                       [rx, r_t], [rx])
                    psum_unhold(r_pa)

            a_step_real = a_step
            if P.idx == 0:
                a_step = lambda: None
            for f in range(NFF):
                if f == 16:
                    a_step()
                tr.tag = f"{P.kind}{P.seq}{P.st}.F"
                wf, r_wf = w_next(("up", f))
                pa_, r_pa = psum()
                pg_, r_pg = psum()
                for kc in range(8):
                    mm(pa_[:, 0:T], wf[:, kc * 256: kc * 256 + 128], hT[:, kc, 0:T], kc == 0, kc == 7, [r_wf, r_hT], [r_pa], kc == 7)
                for kc in range(8):
                    mm(pg_[:, 0:T], wf[:, kc * 256 + 128: kc * 256 + 256], hT[:, kc, 0:T], kc == 0, kc == 7, [r_wf, r_hT], [r_pg], kc == 7)
                w_release()
                ui = f % 2
                y1, r_y1 = tmp32()
                y2, r_y2 = tmp32()
                for si, (o, L, slot) in enumerate(P.segs):
                    u0 = si * (L + 2)
                    cp("pool", ut[:, ui, u0:u0 + 2], ah[:, slot, f, :], [r_ah], [r_ut[ui]])
                    cp("act", ut[:, ui, u0 + 2:u0 + 2 + L], pa_[:, o:o + L], [r_pa], [r_ut[ui]])
                    cp("pool", ah[:, slot, f, :], ut[:, ui, u0 + L:u0 + L + 2], [r_ut[ui]], [r_ah])
                    act(y1[:, o:o + L], pa_[:, o:o + L], AF.Identity, [r_pa, r_const], [r_y1], bias=fwt[:, f, 3:4], scale=fwt[:, f, 2:3])
                    stt(y2[:, o:o + L], ut[:, ui, u0 + 1:u0 + 1 + L], fwt[:, f, 1:2], y1[:, o:o + L], ALU.mult, ALU.add,
                        [r_ut[ui], r_y1, r_const], [r_y2])
                    stt(y1[:, o:o + L], ut[:, ui, u0:u0 + L], fwt[:, f, 0:1], y2[:, o:o + L], ALU.mult, ALU.add,
                        [r_ut[ui], r_y2, r_const], [r_y1])
                act(y2[:, 0:T], y1[:, 0:T], AF.Gelu, [r_y1], [r_y2])
                tt("dve", actT[:, f, 0:T], y2[:, 0:T], pg_[:, 0:T], ALU.mult, [r_y2, r_pg], [r_actT[f], r_gated, r_bgc])
                if f >= 18:
                    g_panel(0, f - 18)
            g_panel(0, 4)
            a_step()
            g_panel(0, 5)
            g_evac(0)
            g_panel(1, 0)
            a_step()
            g_panel(1, 1)
            g_panel(1, 2)
            a_step()
            g_panel(1, 3)
            g_panel(1, 4)
            a_step()
            g_panel(1, 5)
            g_evac(1)
            while not a_done():
                a_step_real()
            deferred.append(lambda: ln2_tail(P))

        def ln2_tail(P):
            par = P.par
            tr.tag = f"{P.kind}{P.seq}{P.st}.G"
            ln_stats_multi(P, P.blocks, st6b, mv4b, r_statB, None)
            k = len(P.blocks)
            nbm = P.blocks[0][1]
            rsqrt_multi(nbm, k, mv4b[:nbm, 0:k, 0], mv4b[:nbm, 0:k, 1], LN_EPS, rsB, [r_statB], r_rsB)
            for bi, (tok0, nb) in enumerate(P.blocks):
                xab = xa[:nb, par, bi, :]
                rx = r_xa[par][bi]
                act(xab, xab, AF.Identity, [rx, r_rsB], [rx], bias=rsB[:nb, bi, 1:2], scale=rsB[:nb, bi, 0:1])
                tt("dve", xab, xab, ptm[:nb, 4, :], ALU.mult, [rx, r_ptm], [rx])
                tt("pool", xab, xab, ptm[:nb, 5, :], ALU.add, [rx, r_ptm], [rx])
                r_o = Res("y_o")
                if P.kind == "prompt":
                    dst = yp[P.seq, P.st * 512 + tok0: P.st * 512 + tok0 + nb, :]
                else:
                    dst = ys
                dma("sp", f"yst{par}{bi}", dst, xab, [rx], [r_o])
                out_res.append(r_o)
            if P.last:
                for (o, L, slot) in P.segs:
                    r_o = Res("cs_o")
                    dma("sp", f"stcs{slot}", cs_o[slot], uh[:, slot, :, :], [r_uh], [r_o])
                    out_res.append(r_o)
                    r_o = Res("fs_o")
                    dma("sp", f"stfs{slot}", fs_o[slot], ah[:, slot, :, :], [r_ah], [r_o])
                    out_res.append(r_o)

        deferred = []

        def run_deferred():
            while deferred:
                deferred.pop(0)()

        def stage_BC(P):
            order = [("P1", 0), ("P2", 0), ("P1", 1), ("R1", 0), ("P2", 1), ("R2a", 0), ("P1", 2), ("R2b", 0), ("R1", 1), ("P2", 2),
                     ("R2a", 1), ("P1", 3), ("R2b", 1), ("R1", 2), ("P2", 3), ("R2a", 2), ("C", 0), ("R2b", 2), ("R1", 3), ("C", 1),
                     ("R2a", 3), ("C", 2), ("R2b", 3), ("C", 3)]
            fn = {"P1": task_P1, "P2": task_P2, "R1": task_R1, "R2a": task_R2a, "R2b": task_R2b, "C": task_C}
            for (k, a) in order:
                fn[k](P, a)
                if (k, a) == ("P2", 0):
                    run_deferred()

        passes = [make_pass(0, "sample", 0, 0)]
        for seq in range(2):
            for st in range(4):
                passes.append(make_pass(len(passes), "prompt", seq, st))
        stage_A(passes[0])
        for i, P in enumerate(passes):
            stage_pre(P)
            stage_BC(P)
            stage_D(P)
            stage_E(P)
            stage_FG(P, passes[i + 1] if i + 1 < len(passes) else None)
        run_deferred()

        assert wst["next_use"] == TOTAL, (wst["next_use"], TOTAL)
        tr.wait_all("sp", out_res)
        tr.run(es)
    global _LAST_TRACKER
    _LAST_TRACKER = tr
    return nc, dbg_outs, pan_order


def _lhs_panel(W, cols):
    sub = W[:, cols]
    return np.ascontiguousarray(sub.reshape(8, 128, 256).transpose(1, 0, 2).reshape(128, 2048))


def _rhs_panel(W, kcs, cols):
    out = np.zeros((128, 4, 512), np.float32)
    for i, kc in enumerate(kcs):
        out[:, i, :] = W[kc * 128:(kc + 1) * 128, cols]
    return out.reshape(128, 2048)


def _build_wpan(order, w_in, w_o_ret, w_o_conv, w_out, w_up, w_down):
    a128 = np.arange(128)
    a256 = np.arange(256)
    a512 = np.arange(512)
    pans = []
    for key in order:
        k = key[0]
        if k in ("q", "k"):
            h = key[1]
            perm = np.concatenate([h * 256 + 2 * a128, h * 256 + 2 * a128 + 1])
            pans.append(_lhs_panel(w_in, perm + (0 if k == "q" else 1024)))
        elif k == "v":
            h, half = key[1], key[2]
            pans.append(_rhs_panel(w_in, list(range(4 * half, 4 * half + 4)), 2048 + h * 512 + a512))
        elif k == "g":
            h, j2 = key[1], key[2]
            pans.append(_lhs_panel(w_in, 4096 + h * 512 + j2 * 256 + a256))
        elif k == "cx":
            c = key[1]
            pans.append(_lhs_panel(w_in, np.concatenate([7168 + c * 128 + a128, 8192 + c * 128 + a128])))
        elif k == "bg":
            pans.append(_lhs_panel(w_in, 6144 + key[1] * 256 + a256))
        elif k == "wor":
            m = key[1]
            wor = w_o_ret[:, m * 128:(m + 1) * 128].reshape(16, 128, 128).transpose(1, 0, 2).reshape(128, 2048)
            pans.append(np.ascontiguousarray(wor))
        elif k == "woc":
            pans.append(_lhs_panel(w_o_conv, key[1] * 256 + a256))
        elif k == "gate":
            m = key[1]
            pans.append(_lhs_panel(w_in, np.concatenate([9216 + m * 128 + a128, 10240 + m * 128 + a128])))
        elif k == "wout":
            ch, half = key[1], key[2]
            pans.append(_rhs_panel(w_out, list(range(4 * half, 4 * half + 4)), ch * 512 + a512))
        elif k == "up":
            f = key[1]
            pans.append(_lhs_panel(w_up, np.concatenate([f * 128 + a128, DFF + f * 128 + a128])))
        elif k == "down":
            ch, i = key[1], key[2]
            pans.append(_rhs_panel(w_down, list(range(4 * i, min(4 * i + 4, NFF))), ch * 512 + a512))
        else:
            raise KeyError(key)
    assert len(pans) == NPAN
    return np.stack(pans).astype(np.float32)


def _consts():
    half = 128
    inv_freq = (np.float32(10000.0) ** (-(np.arange(half, dtype=np.float32) / np.float32(half)))).astype(np.float32)

    def tab(pos):
        ang = (pos.astype(np.float32)[None, :] * inv_freq[:, None]).astype(np.float32)
        a64 = ang.astype(np.float64)
        return np.stack([np.cos(a64), np.sin(a64)], axis=1).astype(np.float32)

    rope_p = tab(np.arange(SEQ))
    ps = PAST + np.arange(DEC_SEQ)
    rope_s = tab(np.concatenate([ps, ps]))
    g = np.array(GAMMA, np.float64)
    j = np.arange(128)
    maskT = np.zeros((128, H, 128), np.float64)
    offs = np.zeros((128, H, 4), np.float64)
    epsv = np.zeros((128, 4, H), np.float64)
    kdec = np.zeros((128, 2, 4, H), np.float64)
    for h in range(H):
        for jb in range(4):
            col = g[h] ** (-(128.0 * jb + j + 1.0)) / 16.0
            if jb == 0:
                maskT[:, h, :] = col[:, None] * (j[None, :] >= j[:, None])
            offs[:, h, jb] = col
            epsv[:, jb, h] = LN_EPS * g[h] ** (-2.0 * (128.0 * jb + j + 1.0))
            kdec[:, 0, jb, h] = g[h] ** (511.0 - 128.0 * jb - j) / 16.0
        kdec[:32, 1, 0, h] = g[h] ** (31.0 - j[:32]) / 16.0
    return dict(rope_p=rope_p, rope_s=rope_s, maskT=maskT.astype(ml_dtypes.bfloat16),
                epsv=epsv.astype(np.float32), kdec=kdec.astype(np.float32), offs=offs.astype(np.float32),
                identb=np.eye(128).astype(ml_dtypes.bfloat16), identf=np.eye(128, dtype=np.float32))


_CACHE = {}


def kernel(x_prompt, x_sample, c_prompt, c_sample, state_retention, state_shortconv, state_ffn_conv,
           ln_in_g, ln_in_b, w_mod, b_mod, w_in, w_o_ret, conv_w, conv_b, w_o_conv, w_out, b_out,
           ln1_g, ln1_b, w_up, ffn_conv_w, ffn_conv_b, w_down, b_down, ln2_g, ln2_b, _debug=()):
    f = lambda a: np.asarray(a, dtype=np.float32)
    x_prompt, x_sample, c_prompt, c_sample = f(x_prompt), f(x_sample), f(c_prompt), f(c_sample)
    state_retention, state_shortconv, state_ffn_conv = f(state_retention), f(state_shortconv), f(state_ffn_conv)
    key = tuple(_debug)
    if key not in _CACHE:
        _CACHE[key] = build_program(debug=_debug)
    nc, dbg_outs, pan_order = _CACHE[key]

    shared = _consts()
    shared["wpan"] = _build_wpan(pan_order, f(w_in)[0], f(w_o_ret)[0], f(w_o_conv)[0], f(w_out)[0], f(w_up)[0], f(w_down)[0])
    wm = f(w_mod)[0]
    shared["wmod"] = np.ascontiguousarray(wm.reshape(8, 128, 48, 128).transpose(2, 1, 0, 3).reshape(48, 128, 1024))
    shared["bmod"] = np.ascontiguousarray(f(b_mod)[0].reshape(48, 128).T)
    vecs = np.stack([f(ln_in_g), f(ln_in_b), f(ln1_g)[0], f(ln1_b)[0], f(ln2_g)[0], f(ln2_b)[0]])
    shared["vtm"] = np.ascontiguousarray(np.broadcast_to(vecs[None], (128, 6, D)))
    shared["vfm"] = np.ascontiguousarray(vecs[:4].reshape(4, 8, 128).transpose(2, 0, 1))
    cwa = np.concatenate([f(conv_w)[0], f(conv_b)], axis=0)
    shared["cw"] = np.ascontiguousarray(cwa.reshape(4, 8, 128).transpose(2, 1, 0))
    fwa = np.concatenate([f(ffn_conv_w)[0], f(ffn_conv_b)], axis=0)
    shared["fw"] = np.ascontiguousarray(fwa.reshape(4, NFF, 128).transpose(2, 1, 0))
    shared["brow"] = np.concatenate([f(b_out)[0], f(b_down)[0]])[None, :].copy()

    in_maps = []
    for i in range(8):
        m = dict(shared)
        m["xp"] = np.ascontiguousarray(x_prompt[2 * i:2 * i + 2])
        m["xs"] = np.ascontiguousarray(x_sample[2 * i:2 * i + 2].reshape(64, D))
        c4 = np.concatenate([c_prompt[2 * i:2 * i + 2], c_sample[2 * i:2 * i + 2]], axis=0)
        m["cT"] = np.ascontiguousarray(c4.reshape(4, 8, 128).transpose(2, 1, 0))
        sr = state_retention[0, 2 * i:2 * i + 2]
        m["sret"] = np.ascontiguousarray(sr.reshape(2, H, 128, 2, DV).transpose(0, 1, 3, 2, 4))
        sc = state_shortconv[0, 2 * i:2 * i + 2]
        m["sconv"] = np.ascontiguousarray(sc.reshape(2, 2, 8, 128).transpose(0, 3, 2, 1))
        sf = state_ffn_conv[0, 2 * i:2 * i + 2]
        m["sffn"] = np.ascontiguousarray(sf.reshape(2, 2, NFF, 128).transpose(0, 3, 2, 1))
        in_maps.append(m)

    res = run_bass_kernel_spmd(nc, in_maps, core_ids=list(range(8)))
    R = res.results
    y_prompt = np.concatenate([r["yp"] for r in R], axis=0)
    y_sample = np.concatenate([r["ys"].reshape(2, DEC_SEQ, D) for r in R], axis=0)

    def ret_state(lo):
        outs = []
        for r in R:
            a = r["rs_o"][lo:lo + 2]
            outs.append(a.transpose(0, 1, 3, 2, 4).reshape(2, H, DK, DV))
        return np.concatenate(outs, axis=0)[None]

    def hist_state(name, lo, nch):
        outs = []
        for r in R:
            a = r[name][lo:lo + 2]
            outs.append(a.transpose(0, 3, 2, 1).reshape(2, 2, nch * 128))
        return np.concatenate(outs, axis=0)[None]

    outs = (y_prompt, y_sample,
            ret_state(0), hist_state("cs_o", 0, 8), hist_state("fs_o", 0, NFF),
            ret_state(2), hist_state("cs_o", 2, 8), hist_state("fs_o", 2, NFF))
    outs = tuple(np.ascontiguousarray(o, dtype=np.float32) for o in outs)
    if _debug:
        return outs, [{k: r["dbg_" + k] for k in dbg_outs} for r in R]
    return outs
```

```python
import numpy as np
import ml_dtypes
from contextlib import ExitStack
import concourse.bass as bass
import concourse.mybir as mybir
from concourse.bass_utils import run_bass_kernel_spmd

F32 = mybir.dt.float32
BF16 = mybir.dt.bfloat16
I32 = mybir.dt.int32
AF = mybir.ActivationFunctionType
ALU = mybir.AluOpType

D = 1024
H = 4
DK = 256
DV = 512
DFF = 2816
NFF = 22
SEQ = 2048
DEC_SEQ = 32
PAST = 2048
LN_EPS = 1e-5
ALPHA = 2.0 ** 0.25
NPAN = 94
NS = 7
NCAST = 24
GAMMA = [1.0 - 2.0 ** (-5.0 - h) for h in range(H)]


class Res:
    __slots__ = ("name", "writer", "readers")

    def __init__(self, name):
        self.name = name
        self.writer = None
        self.readers = {}

    def absorb(self, others):
        for o in others:
            if o.writer is not None:
                self.readers[o.writer[0]] = max(self.readers.get(o.writer[0], 0), o.writer[1])
            for k, v in o.readers.items():
                self.readers[k] = max(self.readers.get(k, 0), v)


class Eng:
    def __init__(self, name):
        self.name = name
        self.count = 0
        self.pending = False
        self.waited = {}
        self.prog = []


class Tracker:
    ENGS = ("pe", "act", "dve", "pool", "sp")

    def __init__(self, nc):
        self.nc = nc
        self.engs = {n: Eng(n) for n in self.ENGS}
        self.sems = {}
        self.dma_vals = {}
        self.ninst = 0

    def _need(self, e, key, val):
        if e.waited.get(key, 0) >= val:
            return
        if key in self.engs:
            src = self.engs[key]
            if val > src.count:
                raise RuntimeError(f"dep on future inc: {e.name} waits {key}>={val} but count={src.count}")
        e.waited[key] = val
        e.prog.append(("wait", key, val))

    def _deps(self, e, reads, writes, same_engine_ok):
        for r in reads:
            if r.writer is not None:
                if not (same_engine_ok and r.writer[0] == e.name):
                    self._need(e, *r.writer)
        for w in writes:
            if w.writer is not None:
                if not (same_engine_ok and w.writer[0] == e.name):
                    self._need(e, *w.writer)
            for k, v in w.readers.items():
                if k == e.name:
                    continue
                self._need(e, k, v)

    def emit(self, eng, fn, reads=(), writes=(), inc=True, same_engine_ok=False):
        e = self.engs[eng]
        self._deps(e, reads, writes, same_engine_ok)
        val = e.count + 1
        e.prog.append(("inst", fn, inc, getattr(self, "tag", "")))
        if inc:
            e.count = val
            e.pending = False
        else:
            e.pending = True
        for w in writes:
            w.writer = (eng, val)
            w.readers = {}
        for r in reads:
            r.readers[eng] = max(r.readers.get(eng, 0), val)
        self.ninst += 1

    def dma(self, eng, semkey, fn, reads=(), writes=()):
        e = self.engs[eng]
        self._deps(e, reads, writes, False)
        val = self.dma_vals.get(semkey, 0) + 16
        self.dma_vals[semkey] = val
        e.prog.append(("dma", fn, semkey))
        for w in writes:
            w.writer = (semkey, val)
            w.readers = {}
        for r in reads:
            r.readers[semkey] = max(r.readers.get(semkey, 0), val)
        self.ninst += 1

    def wait_all(self, eng, resources):
        e = self.engs[eng]
        for r in resources:
            if r.writer is not None:
                self._need(e, *r.writer)
            for k, v in r.readers.items():
                self._need(e, k, v)

    def run(self, es):
        nc = self.nc
        keys = list(self.ENGS) + sorted(self.dma_vals.keys())
        for k in keys:
            self.sems[k] = es.enter_context(nc.semaphore("s_" + k))
        for e in self.engs.values():
            if e.pending:
                raise RuntimeError(f"engine {e.name} has trailing non-inc instruction")
        block = es.enter_context(nc.Block())
        sems = self.sems

        def runner(e):
            def body(h):
                mysem = sems[e.name]
                for item in e.prog:
                    if item[0] == "wait":
                        h.wait_ge(sems[item[1]], item[2])
                    elif item[0] == "inst":
                        ins = item[1](h)
                        if item[2]:
                            ins.then_inc(mysem, 1)
                    else:
                        ins = item[1](h)
                        ins.then_inc(sems[item[2]], 16)
            return body

        block.tensor(runner(self.engs["pe"]))
        block.scalar(runner(self.engs["act"]))
        block.vector(runner(self.engs["dve"]))
        block.gpsimd(runner(self.engs["pool"]))
        block.sync(runner(self.engs["sp"]))


def build_program(debug=()):
    nc = bass.Bass("TRN2", target_bir_lowering=False)
    tr = Tracker(nc)

    def din(name, shape, dt=F32):
        return nc.dram_tensor(name, list(shape), dt, kind="ExternalInput").ap()

    def dout(name, shape, dt=F32):
        return nc.dram_tensor(name, list(shape), dt, kind="ExternalOutput").ap()

    xp = din("xp", [2, SEQ, D])
    xs = din("xs", [64, D])
    cT = din("cT", [128, 8, 4])
    sret = din("sret", [2, H, 2, 128, DV])
    sconv = din("sconv", [2, 128, 8, 2])
    sffn = din("sffn", [2, 128, NFF, 2])
    wpan = din("wpan", [NPAN, 128, 2048])
    wmod = din("wmod", [48, 128, 1024])
    bmod = din("bmod", [128, 48])
    vtm = din("vtm", [128, 6, D])
    vfm = din("vfm", [128, 4, 8])
    cw = din("cw", [128, 8, 4])
    fw = din("fw", [128, NFF, 4])
    brow = din("brow", [1, 2 * D])
    rope_p = din("rope_p", [128, 2, SEQ])
    rope_s = din("rope_s", [128, 2, 64])
    maskT = din("maskT", [128, H, 128], BF16)
    epsv = din("epsv", [128, 4, H])
    kdec = din("kdec", [128, 2, 4, H])
    offs = din("offs", [128, H, 4])
    identb = din("identb", [128, 128], BF16)
    identf = din("identf", [128, 128])

    wbf = nc.dram_tensor("wbf", [NPAN, 128, 2048], BF16, kind="Internal").ap()

    yp = dout("yp", [2, SEQ, D])
    ys = dout("ys", [64, D])
    rs_o = dout("rs_o", [4, H, 2, 128, DV])
    cs_o = dout("cs_o", [4, 128, 8, 2])
    fs_o = dout("fs_o", [4, 128, NFF, 2])
    out_res = []
    dbg_outs = {}

    es = ExitStack()
    with es:
        def sb(name, shape, dt=F32):
            return es.enter_context(nc.sbuf_tensor(name, list(shape), dt))

        ring = sb("ring", [128, NS, 2048], BF16)
        r_ring = [Res(f"ring{i}") for i in range(NS)]
        xa = sb("xa", [128, 2, 4, D])
        r_xa = [[Res(f"xa{p}{b}") for b in range(4)] for p in range(2)]
        ptm = sb("ptm", [128, 6, D])
        r_ptm = Res("ptm")
        gtc = sb("gtc", [128, 2, D])
        r_gtc = Res("gtc")
        S_p = sb("S_p", [128, H, 2, DV])
        r_S = [Res(f"S{h}") for h in range(H)]
        Sbf = sb("Sbf", [128, 2, 2, DV], BF16)
        r_Sbf = [Res("Sbf0"), Res("Sbf1")]
        rope = sb("rope", [128, 2, 512])
        r_rope = Res("rope")
        hT = sb("hT", [128, 8, 512], BF16)
        r_hT = Res("hT")
        qk = sb("qk", [128, 2, 2, 2, 512], BF16)
        r_qk = [Res("qk0"), Res("qk1")]
        mvbuf = sb("mvbuf", [128, 4096], BF16)
        vsb = mvbuf[:].rearrange("p (b r d) -> p b r d", b=2, r=4)
        r_v = [Res("v0"), Res("v1")]
        sgb = sb("sgb", [128, 2, 4, 512], BF16)
        r_sg = [Res("sg0"), Res("sg1")]
        ktok = sb("ktok", [128, 4, 256], BF16)
        r_ktok = Res("ktok")
        smT = sb("smT", [128, 10, 128], BF16)
        r_smT = Res("smT")
        onb = sb("onb", [128, 4, DV], BF16)
        r_onb = [Res(f"onb{i}") for i in range(4)]
        gbuf = sb("gbuf", [128, 12288], BF16)
        gatedT = gbuf[:, 0:8192].rearrange("p (c t) -> p c t", t=512)
        r_gated = Res("gatedT")
        bgc = gbuf[:, 8192:12288].rearrange("p (c t) -> p c t", t=512)
        r_bgc = Res("bgc")
        mergedT = mvbuf[:].rearrange("p (c t) -> p c t", t=512)
        r_merged = Res("mergedT")
        actT = gbuf[:, 0:NFF * 512].rearrange("p (c t) -> p c t", t=512)
        r_actT = [Res(f"actT{f}") for f in range(NFF)]
        xnb = sb("xnb", [128, D], BF16)
        r_xnb = Res("xnb")
        tmpA = sb("tmpA", [128, 4, 512])
        r_tmpA = [Res(f"tmpA{i}") for i in range(4)]
        ut = sb("ut", [128, 2, 520])
        r_ut = [Res("ut0"), Res("ut1")]
        uh = sb("uh", [128, 4, 8, 2])
        r_uh = Res("uh")
        ah = sb("ah", [128, 4, NFF, 2])
        r_ah = Res("ah")
        modfm = sb("modfm", [128, 48, 4])
        r_modfm = Res("modfm")
        mA = sb("mA", [128, 4, 8, 4])
        r_mA = Res("mA")
        cTt = sb("cTt", [128, 8, 4])
        vfmt = sb("vfmt", [128, 4, 8])
        cwt = sb("cwt", [128, 8, 4])
        fwt = sb("fwt", [128, NFF, 4])
        bmt = sb("bmt", [128, 48])
        browb = sb("browb", [1, 2 * D], BF16)
        ones = sb("ones", [1, 128], BF16)
        maskt = sb("maskt", [128, H, 128], BF16)
        epst = sb("epst", [128, 4, H])
        kdt = sb("kdt", [128, 2, 4, H])
        offt = sb("offt", [128, H, 4])
        idb = sb("idb", [128, 128], BF16)
        idf = sb("idf", [128, 128])
        bcl = sb("bcl", [128, 128])
        r_bcl = Res("bcl")
        r_const = Res("const")
        r_browb = Res("browb")
        r_ones = Res("ones")
        st6 = sb("st6", [128, 2, 6])
        mvt = sb("mvt", [128, 2])
        rsq = sb("rsq", [128, 8])
        rsqi = sb("rsqi", [128, 2], I32)
        rsq4 = sb("rsq4", [128, 3, 4])
        rsq4i = sb("rsq4i", [128, 4], I32)
        mv4 = sb("mv4", [128, 4, 2])
        st64 = sb("st64", [128, 4, 6])
        rs4 = sb("rs4", [128, 2, 4, 2])
        r_rs4 = [Res("rs4a"), Res("rs4b")]
        st6a = sb("st6a", [128, 4, 2, 6])
        mv4a = sb("mv4a", [128, 4, 2])
        rsA = sb("rsA", [128, 4, 2])
        r_rsA = Res("rsA")
        r_statA = Res("statA")
        st6b = sb("st6b", [128, 4, 2, 6])
        mv4b = sb("mv4b", [128, 4, 2])
        rsB = sb("rsB", [128, 4, 2])
        r_rsB = Res("rsB")
        r_statB = Res("statB")
        r_stat = Res("stat")
        rstd_t = sb("rstd_t", [128, 4, 2])
        r_rstd = [Res(f"rstd{i}") for i in range(4)]
        wm32 = tmpA[:].rearrange("p a b -> p (a b)").rearrange("p (b f) -> p b f", f=1024)
        r_wm = [Res("wm0"), Res("wm1")]

        psb = [es.enter_context(nc.psum_tensor(f"ps{i}", [128, 512], F32)) for i in range(8)]
        r_ps = [Res(f"ps{i}") for i in range(8)]
        ps_ctr = [0]

        ps_held = set()

        def psum(hold=False):
            while True:
                i = ps_ctr[0] % 8
                ps_ctr[0] += 1
                if i not in ps_held:
                    break
            if hold:
                ps_held.add(i)
            return psb[i], r_ps[i]

        def psum_unhold(r):
            ps_held.discard(r_ps.index(r))

        rot = {"tmp": 0, "rstd": 0}

        def tmp32():
            i = rot["tmp"] % 4
            rot["tmp"] += 1
            return tmpA[:, i, :], r_tmpA[i]

        def rstd_slot():
            i = rot["rstd"] % 4
            rot["rstd"] += 1
            return rstd_t[:, i, :], r_rstd[i]

        def mm(out, lhsT, rhs, start, stop, reads, writes, inc):
            tr.emit("pe", lambda t: t.matmul(out, lhsT=lhsT, rhs=rhs, start=start, stop=stop),
                    reads, writes, inc=inc, same_engine_ok=True)

        def transp(out, in_, ident, reads, writes, inc):
            tr.emit("pe", lambda t: t.transpose(out=out, in_=in_, identity=ident),
                    reads, writes, inc=inc, same_engine_ok=True)

        def act(out, in_, func, reads, writes, bias=None, scale=None):
            kw = {}
            if bias is not None:
                kw["bias"] = bias
            if scale is not None:
                kw["scale"] = scale
            tr.emit("act", lambda a: a.activation(out=out, in_=in_, func=func, **kw), reads, writes)

        def tt(eng, out, in0, in1, op, reads, writes):
            tr.emit(eng, lambda v: v.tensor_tensor(out=out, in0=in0, in1=in1, op=op), reads, writes)

        def ts(eng, out, in0, s1, s2, op0, op1, reads, writes):
            if op1 is None:
                tr.emit(eng, lambda v: v.tensor_scalar(out=out, in0=in0, scalar1=s1, scalar2=None, op0=op0), reads, writes)
            else:
                tr.emit(eng, lambda v: v.tensor_scalar(out=out, in0=in0, scalar1=s1, scalar2=s2, op0=op0, op1=op1), reads, writes)

        def stt(out, in0, scalar, in1, op0, op1, reads, writes):
            tr.emit("dve", lambda v: v.scalar_tensor_tensor(out=out, in0=in0, scalar=scalar, in1=in1, op0=op0, op1=op1), reads, writes)

        def cp(eng, out, in_, reads, writes):
            if eng == "act":
                tr.emit("act", lambda a: a.copy(out=out, in_=in_), reads, writes)
            else:
                tr.emit(eng, lambda v: v.tensor_copy(out=out, in_=in_), reads, writes)

        def dma(eng, key, out, in_, reads, writes):
            tr.dma(eng, key, lambda g: g.dma_start(out=out, in_=in_), reads, writes)

        def dbg(name, ap, shape, res, dt=F32):
            if name not in debug:
                return
            o = dout("dbg_" + name, shape, dt)
            r = Res("dbg_" + name)
            dma("sp", "dbg", o, ap, [res], [r])
            out_res.append(r)
            dbg_outs[name] = shape

        def rsqrt(n, mean_ap, var_ap, eps, rs_ap, reads, r_rs):
            xe = rsq[:n, 0:1]
            yy = rsq[:n, 1:2]
            t_ = rsq[:n, 2:3]
            ti = rsqi[:n, 0:1]
            if isinstance(eps, float):
                ts("dve", xe, var_ap, eps, None, ALU.add, None, reads + [r_stat], [r_stat])
            else:
                tt("dve", xe, var_ap, eps, ALU.add, reads + [r_stat, r_const], [r_stat])
            ts("dve", ti, xe.bitcast(I32), 1, None, ALU.arith_shift_right, None, [r_stat], [r_stat])
            ts("dve", yy.bitcast(I32), ti, -1.0, 1597463007.0, ALU.mult, ALU.add, [r_stat], [r_stat])
            for it in range(3):
                stt(t_, xe, yy, yy, ALU.mult, ALU.mult, [r_stat], [r_stat])
                ts("dve", t_, t_, -0.5, 1.5, ALU.mult, ALU.add, [r_stat], [r_stat])
                if it < 2:
                    tt("dve", yy, yy, t_, ALU.mult, [r_stat], [r_stat])
                else:
                    tt("dve", rs_ap[:n, 0:1], yy, t_, ALU.mult, [r_stat], [r_stat, r_rs])
            stt(rs_ap[:n, 1:2], mean_ap, -1.0, rs_ap[:n, 0:1], ALU.mult, ALU.mult, reads + [r_stat, r_rs], [r_rs])

        def rsqrt_multi(n, k, mean_ap, var_ap, eps_ap, rs3, reads, r_rs):
            xe = rsq4[:n, 0, 0:k]
            yy = rsq4[:n, 1, 0:k]
            t_ = rsq4[:n, 2, 0:k]
            ti = rsq4i[:n, 0:k]
            if isinstance(eps_ap, float):
                ts("dve", xe, var_ap, eps_ap, None, ALU.add, None, reads + [r_stat], [r_stat])
            else:
                tt("dve", xe, var_ap, eps_ap, ALU.add, reads + [r_stat, r_const], [r_stat])
            ts("dve", ti, xe.bitcast(I32), 1, None, ALU.arith_shift_right, None, [r_stat], [r_stat])
            ts("dve", yy.bitcast(I32), ti, -1.0, 1597463007.0, ALU.mult, ALU.add, [r_stat], [r_stat])
            for it in range(3):
                tt("dve", t_, xe, yy, ALU.mult, [r_stat], [r_stat])
                tt("dve", t_, t_, yy, ALU.mult, [r_stat], [r_stat])
                ts("dve", t_, t_, -0.5, 1.5, ALU.mult, ALU.add, [r_stat], [r_stat])
                if it < 2:
                    tt("dve", yy, yy, t_, ALU.mult, [r_stat], [r_stat])
                else:
                    tt("dve", rs3[:n, 0:k, 0], yy, t_, ALU.mult, [r_stat], [r_stat, r_rs])
            stt(rs3[:n, 0:k, 1], mean_ap, -1.0, rs3[:n, 0:k, 0], ALU.mult, ALU.mult, reads + [r_stat, r_rs], [r_rs])

        r_cast = [Res(f"cast{g}") for g in range(NCAST)]
        wst = {"next_load": 0, "next_use": 0}
        NPASS = 9
        TOTAL = NPASS * NPAN

        r_wbf = [Res(f"wbf{i}") for i in range(NPAN)]
        wunits = [(xa[:, 1, b, :], r_xa[1][b]) for b in range(4)] + [(xa[:, 0, b, :], r_xa[0][b]) for b in range(1, 4)]
        pend_store = []

        def w_flush_store(keep):
            while len(pend_store) > keep:
                pidx, slot = pend_store.pop(0)
                dma("sp", f"wst{slot}", wbf[pidx], ring[:, slot, :], [r_ring[slot]], [r_wbf[pidx]])

        pend_cast = []

        def emit_cast(g):
            slot = g % NS
            pidx = g % NPAN
            for hf in range(2):
                u = (2 * g + hf) % len(wunits)
                uap, ures = wunits[u]
                cp("act" if hf == 0 else "dve", ring[:, slot, hf * 1024:(hf + 1) * 1024], uap, [ures], [r_ring[slot]])
            pend_store.append((pidx, slot))

        def w_load_one():
            g = wst["next_load"]
            if g >= TOTAL:
                return
            wst["next_load"] += 1
            slot = g % NS
            pidx = g % NPAN
            if g < NPAN:
                for hf in range(2):
                    u = (2 * g + hf) % len(wunits)
                    uap, ures = wunits[u]
                    dma("sp", f"wld{u}", uap, wpan[pidx][:, hf * 1024:(hf + 1) * 1024], [], [ures])
                pend_cast.append(g)
                while len(pend_cast) > 2:
                    emit_cast(pend_cast.pop(0))
                w_flush_store(2)
            else:
                while pend_cast:
                    emit_cast(pend_cast.pop(0))
                w_flush_store(0)
                dma("sp", f"ring{slot}", ring[:, slot, :], wbf[pidx], [r_wbf[pidx]], [r_ring[slot]])

        pan_order = []

        def w_next(key):
            g = wst["next_use"]
            wst["next_use"] += 1
            if g < NPAN:
                pan_order.append(key)
            else:
                assert pan_order[g % NPAN] == key, (g, key, pan_order[g % NPAN])
            slot = g % NS
            return ring[:, slot, :], r_ring[slot]

        def w_release(k=1):
            for _ in range(k):
                w_load_one()

        cl = [(cTt[:], cT), (vfmt[:], vfm), (cwt[:], cw), (fwt[:], fw), (bmt[:], bmod),
              (maskt[:], maskT), (epst[:], epsv), (kdt[:], kdec), (offt[:], offs),
              (idb[:], identb), (idf[:], identf)]
        for o, i in cl:
            dma("sp", "const", o, i, [], [r_const])
        dma("sp", "ptm", ptm[:], vtm, [], [r_ptm])
        dma("sp", "hist_u", uh[:, 2:4, :, :], sconv.rearrange("s p c r -> p s c r"), [], [r_uh])
        dma("sp", "hist_a", ah[:, 2:4, :, :], sffn.rearrange("s p c r -> p s c r"), [], [r_ah])
        tr.emit("pool", lambda g: g.memset(uh[:, 0:2, :, :], 0.0), [], [r_uh])
        tr.emit("pool", lambda g: g.memset(ah[:, 0:2, :, :], 0.0), [], [r_ah])
        tr.emit("pool", lambda g: g.memset(ones[:], 1.0), [], [r_ones])
        tr.dma("pool", "browc", lambda q: q.dma_start(out=browb[:], in_=brow), [], [r_browb])
        for i_ in range(4):
            tr.emit("act", (lambda i_: (lambda a: a.activation(out=ptm[:, i_, :], in_=ptm[:, i_, :], func=AF.Copy, scale=float(ALPHA))))(i_),
                    [r_ptm], [r_ptm])
        pm, r_pm = psum()
        wmb = gbuf[:, 0:4096].rearrange("p (b f) -> p b f", f=1024)
        r_wmb = [Res(f"wmb{i}") for i in range(4)]
        wm32 = xa[:, 1, :, :]
        r_wm = [Res(f"wm{i}") for i in range(4)]
        cTb = gbuf[:, 4096:4128].rearrange("p (k s) -> p k s", s=4)
        r_cTb = Res("cTb")
        cp("dve", cTb, cTt[:], [r_const], [r_cTb])
        for oc in range(48):
            b_ = oc % 4
            dma("sp", f"wm{b_}", wm32[:, b_, :], wmod[oc], [], [r_wm[b_]])
            cp("act" if oc % 2 == 0 else "dve", wmb[:, b_, :], wm32[:, b_, :], [r_wm[b_]], [r_wmb[b_]])
            for kc in range(8):
                mm(pm[:, oc * 4:(oc + 1) * 4], wmb[:, b_, kc * 128:(kc + 1) * 128], cTb[:, kc, :],
                   kc == 0, kc == 7, [r_wmb[b_], r_cTb], [r_pm], kc == 7)
        tt("dve", modfm[:], pm[:, 0:192].rearrange("p (a b) -> p a b", b=4),
           bmt[:].unsqueeze(2).broadcast_to([128, 48, 4]), ALU.add, [r_pm, r_const], [r_modfm])
        for which, (gi, sci, shi) in enumerate([(0, 1, 0), (2, 4, 3)]):
            one_sc = tmpA[:, 0, 0:32].rearrange("p (c s) -> p c s", s=4)
            ts("dve", one_sc, modfm[:, sci * 8:(sci + 1) * 8, :], 1.0, None, ALU.add, None, [r_modfm], [r_tmpA[0]])
            tt("dve", mA[:, 2 * which, :, :], one_sc, vfmt[:, gi, :].unsqueeze(2).broadcast_to([128, 8, 4]), ALU.mult,
               [r_tmpA[0], r_const], [r_mA])
            tt("dve", mA[:, 2 * which + 1, :, :], one_sc, vfmt[:, gi + 1, :].unsqueeze(2).broadcast_to([128, 8, 4]), ALU.mult,
               [r_tmpA[0], r_const], [r_mA])
            tt("dve", mA[:, 2 * which + 1, :, :], mA[:, 2 * which + 1, :, :], modfm[:, shi * 8:(shi + 1) * 8, :], ALU.add,
               [r_mA, r_modfm], [r_mA])

        def build_gtc(part_slots):
            for which, mi in enumerate((2, 5)):
                for half in range(2):
                    pg, r_pg = psum()
                    for cc in range(4):
                        c = half * 4 + cc
                        for (p0, p1, slot) in part_slots:
                            cp("dve", bcl[:, p0:p1], modfm[:, mi * 8 + c, slot:slot + 1].broadcast_to([128, p1 - p0]),
                               [r_modfm], [r_bcl])
                        npart = part_slots[-1][1]
                        mm(pg[:npart, cc * 128:(cc + 1) * 128], bcl[:, 0:npart], idf[:], True, True,
                           [r_bcl, r_const], [r_pg], True)
                    npart = part_slots[-1][1]
                    cp("act", gtc[:npart, which, half * 512:(half + 1) * 512], pg[:npart, :], [r_pg], [r_gtc])

        for b_ in range(4):
            r_xa[1][b_].absorb([r_wm[b_]])
        for _ in range(NS):
            w_load_one()

        class Pass:
            pass

        def make_pass(idx, kind, seq, st):
            P = Pass()
            P.idx, P.kind, P.seq, P.st = idx, kind, seq, st
            P.par = idx % 2
            if kind == "prompt":
                P.T = 512
                P.blocks = [(b * 128, 128) for b in range(4)]
                P.segs = [(0, 512, seq)]
                P.n = 128
                P.kd = 0
                P.rope_src = rope_p[:, :, st * 512:(st + 1) * 512]
                P.last = (st == 3)
                P.Lr = 512
            else:
                P.T = 64
                P.blocks = [(0, 64)]
                P.segs = [(0, 32, 2), (32, 32, 3)]
                P.n = 32
                P.kd = 1
                P.rope_src = rope_s
                P.last = True
                P.Lr = 32
            P.gL = [GAMMA[h] ** P.Lr for h in range(H)]
            return P

        def seg_of_block(P, tok0, nb):
            return [(max(o, tok0), min(o + L, tok0 + nb), slot) for (o, L, slot) in P.segs
                    if max(o, tok0) < min(o + L, tok0 + nb)]

        def ln_block(P, bi, tok0, nb, load_from, whichA, do_T, affine_idx):
            ln_part1(P, bi, tok0, nb, load_from, whichA, do_T, affine_idx)
            if do_T:
                ln_part2(P, bi, tok0, nb, whichA)

        def ln_part1(P, bi, tok0, nb, load_from, whichA, do_T, affine_idx):
            par = P.par
            xab = xa[:nb, par, bi, :]
            rx = r_xa[par][bi]
            if load_from is not None:
                dma("sp", f"xld{par}{bi}", xab, load_from, [], [rx])
            for i in range(2):
                tr.emit("dve", (lambda i: (lambda v: v.bn_stats(out=st6[:nb, i, :], in_=xa[:nb, par, bi, i * 512:(i + 1) * 512])))(i),
                        [rx, r_stat], [r_stat])
            tr.emit("dve", lambda v: v.bn_aggr(out=mvt[:nb, :], in_=st6[:nb, :, :].rearrange("p a b -> p (a b)")),
                    [r_stat], [r_stat])
            rs_ap, r_rs = rstd_slot()
            rsqrt(nb, mvt[:nb, 0:1], mvt[:nb, 1:2], LN_EPS, rs_ap, [], r_rs)
            if do_T:
                act(xnb[:nb, :], xab, AF.Identity, [rx, r_rs], [r_xnb], bias=rs_ap[:nb, 1:2], scale=rs_ap[:nb, 0:1])
            act(xab, xab, AF.Identity, [rx, r_rs], [rx], bias=rs_ap[:nb, 1:2], scale=rs_ap[:nb, 0:1])
            tt("pool", xab, xab, ptm[:nb, affine_idx, :], ALU.mult, [rx, r_ptm], [rx])
            tt("pool", xab, xab, ptm[:nb, affine_idx + 1, :], ALU.add, [rx, r_ptm], [rx])

        def ln_part2(P, bi, tok0, nb, whichA):
            pt, r_pt = psum()
            ptb = pt[:].bitcast(BF16)
            for c in range(8):
                transp(ptb[:, c * 128:c * 128 + nb], xnb[:nb, c * 128:(c + 1) * 128], idb[:nb, :nb],
                       [r_xnb, r_const], [r_pt], c == 7)
            for c in range(8):
                for (a0, a1, slot) in seg_of_block(P, tok0, nb):
                    if c % 2 == 0:
                        ts("dve", hT[:, c, a0:a1], ptb[:, c * 128 + (a0 - tok0):c * 128 + (a1 - tok0)],
                           mA[:, 2 * whichA, c, slot:slot + 1], mA[:, 2 * whichA + 1, c, slot:slot + 1],
                           ALU.mult, ALU.add, [r_pt, r_mA], [r_hT])
                    else:
                        act(hT[:, c, a0:a1], ptb[:, c * 128 + (a0 - tok0):c * 128 + (a1 - tok0)], AF.Identity,
                            [r_pt, r_mA], [r_hT], bias=mA[:, 2 * whichA + 1, c, slot:slot + 1], scale=mA[:, 2 * whichA, c, slot:slot + 1])

        def stage_A(P):
            tr.tag = f"{P.kind}{P.seq}{P.st}.A"
            dma("sp", "rope", rope[:, :, 0:P.T], P.rope_src, [], [r_rope])
            for bi, (tok0, nb) in enumerate(P.blocks):
                src = xp[P.seq, P.st * 512 + tok0: P.st * 512 + tok0 + nb, :] if P.kind == "prompt" else xs
                ln_block(P, bi, tok0, nb, src, 0, True, 0)

        def stage_pre(P):
            if P.kind == "sample":
                build_gtc([(0, 32, 2), (32, 64, 3)])
            elif P.st == 0:
                build_gtc([(0, 128, P.seq)])
                for h in range(H):
                    tr.emit("pool", (lambda h: (lambda g: g.memset(S_p[:, h, :, :], 0.0)))(h), [], [r_S[h]])

        def task_P1(P, h):
            tr.tag = f"{P.kind}{P.seq}{P.st}.P1{h}"
            T = P.T
            hb = h % 2
            for which in range(2):
                wp_, r_wp = w_next(("q" if which == 0 else "k", h))
                pss = []
                for e in range(2):
                    p_, r_p = psum()
                    for kc in range(8):
                        mm(p_[:, 0:T], wp_[:, kc * 256 + e * 128: kc * 256 + (e + 1) * 128], hT[:, kc, 0:T],
                           kc == 0, kc == 7, [r_wp, r_hT], [r_p], kc == 7)
                    pss.append((p_, r_p))
                w_release()
                (x1, r1), (x2, r2) = pss
                cos = rope[:, 0, 0:T]
                sin = rope[:, 1, 0:T]
                t1, rt1 = tmp32()
                t2, rt2 = tmp32()
                tt("dve", t1[:, 0:T], x1[:, 0:T], cos, ALU.mult, [r1, r_rope], [rt1])
                tt("dve", t2[:, 0:T], x2[:, 0:T], sin, ALU.mult, [r2, r_rope], [rt2])
                tt("dve", qk[:, hb, which, 0, 0:T], t1[:, 0:T], t2[:, 0:T], ALU.subtract, [rt1, rt2], [r_qk[hb]])
                t3, rt3 = tmp32()
                t4, rt4 = tmp32()
                tt("dve", t3[:, 0:T], x1[:, 0:T], sin, ALU.mult, [r1, r_rope], [rt3])
                tt("dve", t4[:, 0:T], x2[:, 0:T], cos, ALU.mult, [r2, r_rope], [rt4])
                tt("dve", qk[:, hb, which, 1, 0:T], t3[:, 0:T], t4[:, 0:T], ALU.add, [rt3, rt4], [r_qk[hb]])

        def rb_list(P):
            out = []
            for si, (o, L, slot) in enumerate(P.segs):
                for ib in range(L // P.n):
                    out.append((si, ib, o + ib * P.n, slot))
            return out

        def task_P2(P, h):
            tr.tag = f"{P.kind}{P.seq}{P.st}.P2{h}"
            T = P.T
            n = P.n
            hb = h % 2
            wv0, r_wv0 = w_next(("v", h, 0))
            wv1, r_wv1 = w_next(("v", h, 1))
            for rb, (si, ib, t0, slot) in enumerate(rb_list(P)):
                p_, r_p = psum()
                for kc in range(8):
                    wv = wv0 if kc < 4 else wv1
                    rw = r_wv0 if kc < 4 else r_wv1
                    mm(p_[:n, :], hT[:, kc, t0:t0 + n], wv[:, (kc % 4) * 512:(kc % 4 + 1) * 512],
                       kc == 0, kc == 7, [rw, r_hT], [r_p], kc == 7)
                cp("act", vsb[:n, hb, rb, :], p_[:n, :], [r_p], [r_v[hb], r_merged])
            w_release(2)
            for j2 in range(2):
                wg, r_wg = w_next(("g", h, j2))
                for jj in range(2):
                    j = j2 * 2 + jj
                    p_, r_p = psum()
                    for kc in range(8):
                        mm(p_[:, 0:T], wg[:, kc * 256 + jj * 128: kc * 256 + (jj + 1) * 128], hT[:, kc, 0:T],
                           kc == 0, kc == 7, [r_wg, r_hT], [r_p], kc == 7)
                    act(sgb[:, hb, j, 0:T], p_[:, 0:T], AF.Silu, [r_p], [r_sg[hb]])
                w_release()

        def S_of(P, h, slot):
            if P.kind == "sample" and slot == 3:
                hh = (h + 2) % 4
            else:
                hh = h
            return S_p[:, hh, :, :], r_S[hh]

        def task_R1(P, h):
            tr.tag = f"{P.kind}{P.seq}{P.st}.R1{h}"
            n = P.n
            hb = h % 2
            qT = qk[:, hb, 0, :, :]
            kT = qk[:, hb, 1, :, :]
            rbl = rb_list(P)
            for si, (o, L, slot) in enumerate(P.segs):
                S_ap, r_Sx = S_of(P, h, slot)
                if P.kind == "sample":
                    dma("sp", f"sld{slot}{h}", S_ap, sret[slot - 2, h].rearrange("e p v -> p e v"), [], [r_Sx])
                sbi = hb if P.kind == "prompt" else si
                cp("act", Sbf[:, sbi, :, :], S_ap, [r_Sx], [r_Sbf[sbi]])
            blk = 0
            for rb, (si, ib, t0, slot) in enumerate(rbl):
                seg_t0 = P.segs[si][0]
                for jb in range(ib + 1):
                    tj = seg_t0 + jb * n
                    psc, r_psc = psum()
                    for e in range(2):
                        mm(psc[:n, 0:n], kT[:, e, tj:tj + n], qT[:, e, t0:t0 + n], e == 0, e == 1, [r_qk[hb]], [r_psc], e == 1)
                    if jb == ib:
                        stt(smT[:n, blk, 0:n], psc[:n, 0:n], float(GAMMA[h] ** (-128.0 * jb)), maskt[:n, h, 0:n], ALU.mult, ALU.mult,
                            [r_psc, r_const], [r_smT])
                    else:
                        ts("dve", smT[:n, blk, 0:n], psc[:n, 0:n], offt[:n, h, jb:jb + 1], None, ALU.mult, None,
                           [r_psc, r_const], [r_smT])
                    blk += 1
            for rb, (si, ib, t0, slot) in enumerate(rbl):
                pkt, r_pkt = psum()
                pktb = pkt[:].bitcast(BF16)
                for e in range(2):
                    transp(pktb[:n, e * 128:(e + 1) * 128], kT[:, e, t0:t0 + n], idb[:, :], [r_qk[hb], r_const], [r_pkt], e == 1)
                ts("dve", ktok[:n, rb, :], pktb[:n, 0:256], kdt[:n, P.kd, ib, h:h + 1], None, ALU.mult, None,
                   [r_pkt, r_const], [r_ktok])

        def task_R2a(P, h):
            tr.tag = f"{P.kind}{P.seq}{P.st}.R2a{h}"
            n = P.n
            hb = h % 2
            qT = qk[:, hb, 0, :, :]
            rbl = rb_list(P)
            nrb = len(rbl)
            blk = 0
            pos = []
            for rb, (si, ib, t0, slot) in enumerate(rbl):
                po, r_po = psum()
                for jb in range(ib + 1):
                    rbj = rb - ib + jb
                    mm(po[:n, :], smT[:n, blk, 0:n], vsb[:n, hb, rbj, :], jb == 0, False, [r_smT, r_v[hb]], [r_po], False)
                    blk += 1
                sbi = hb if P.kind == "prompt" else si
                for e in range(2):
                    mm(po[:n, :], qT[:, e, t0:t0 + n], Sbf[:, sbi, e, :], False, e == 1, [r_qk[hb], r_Sbf[sbi]], [r_po], e == 1)
                pos.append((po, r_po))
                tr.emit("dve", (lambda n, po, rb: (lambda v: v.bn_stats(out=st64[:n, rb, :], in_=po[:n, :])))(n, po, rb), [r_po, r_stat], [r_stat])
                tr.emit("dve", (lambda n, rb: (lambda v: v.bn_aggr(out=mv4[:n, rb, :], in_=st64[:n, rb, :])))(n, rb), [r_stat], [r_stat])
            if P.kind == "prompt":
                eps_ap = epst[:n, 0:nrb, h]
            else:
                eps_ap = epst[:n, 0:1, h].broadcast_to([n, nrb])
            rs3 = rs4[:, hb, :, :]
            rsqrt_multi(n, nrb, mv4[:n, 0:nrb, 0], mv4[:n, 0:nrb, 1], eps_ap, rs3, [], r_rs4[hb])
            for rb, (po, r_po) in enumerate(pos):
                act(onb[:n, rb, :], po[:n, :], AF.Identity, [r_po, r_rs4[hb]], [r_onb[rb]], bias=rs3[:n, rb, 1:2], scale=rs3[:n, rb, 0:1])
            for si, (o, L, slot) in enumerate(P.segs):
                S_ap, r_Sx = S_of(P, h, slot)
                rbs = [rb for rb, x in enumerate(rbl) if x[0] == si]
                for e in range(2):
                    pkv, r_pkv = psum()
                    for k_, rb in enumerate(rbs):
                        mm(pkv[:, :], ktok[:n, rb, e * 128:(e + 1) * 128], vsb[:n, hb, rb, :], k_ == 0, k_ == len(rbs) - 1,
                           [r_ktok, r_v[hb]], [r_pkv], k_ == len(rbs) - 1)
                    stt(S_ap[:, e, :], S_ap[:, e, :], float(P.gL[h]), pkv[:, :], ALU.mult, ALU.add, [r_Sx, r_pkv], [r_Sx])
                if P.last:
                    r_o = Res("rs_o")
                    dma("sp", f"strs{slot}{h}", rs_o[slot, h].rearrange("e p v -> p e v"), S_ap, [r_Sx], [r_o])
                    out_res.append(r_o)

        def task_R2b(P, h):
            tr.tag = f"{P.kind}{P.seq}{P.st}.R2b{h}"
            n = P.n
            hb = h % 2
            for rb, (si, ib, t0, slot) in enumerate(rb_list(P)):
                pot, r_pot = psum()
                potb = pot[:].bitcast(BF16)
                for d in range(4):
                    transp(potb[:, d * 128:d * 128 + n], onb[:n, rb, d * 128:(d + 1) * 128], idb[:n, :n],
                           [r_onb[rb], r_const], [r_pot], d == 3)
                tt("dve", gatedT[:, h * 4:(h + 1) * 4, t0:t0 + n],
                   potb[:, 0:512].rearrange("p (d t) -> p d t", t=128)[:, :, 0:n],
                   sgb[:, hb, :, t0:t0 + n], ALU.mult, [r_pot, r_sg[hb]], [r_gated] + r_actT)

        def task_C(P, cpair):
            tr.tag = f"{P.kind}{P.seq}{P.st}.C{cpair}"
            T = P.T
            wcx = [w_next(("cx", cpair * 2)), w_next(("cx", cpair * 2 + 1))]
            wbg, r_wbg = w_next(("bg", cpair))
            for ci in range(2):
                c = cpair * 2 + ci
                wc, r_wc = wcx[ci]
                pcg, r_pcg = psum()
                pxi, r_pxi = psum()
                pbg, r_pbg = psum()
                for kc in range(8):
                    mm(pcg[:, 0:T], wc[:, kc * 256: kc * 256 + 128], hT[:, kc, 0:T], kc == 0, kc == 7, [r_wc, r_hT], [r_pcg], kc == 7)
                for kc in range(8):
                    mm(pxi[:, 0:T], wc[:, kc * 256 + 128: kc * 256 + 256], hT[:, kc, 0:T], kc == 0, kc == 7, [r_wc, r_hT], [r_pxi], kc == 7)
                for kc in range(8):
                    mm(pbg[:, 0:T], wbg[:, kc * 256 + ci * 128: kc * 256 + (ci + 1) * 128], hT[:, kc, 0:T], kc == 0, kc == 7,
                       [r_wbg, r_hT], [r_pbg], kc == 7)
                cgs, r_cgs = tmp32()
                cp("act", cgs[:, 0:T], pcg[:, 0:T], [r_pcg], [r_cgs])
                ui = c % 2
                y1, r_y1 = tmp32()
                y2, r_y2 = tmp32()
                for si, (o, L, slot) in enumerate(P.segs):
                    u0 = si * (L + 2)
                    cp("pool", ut[:, ui, u0:u0 + 2], uh[:, slot, c, :], [r_uh], [r_ut[ui]])
                    tt("dve", ut[:, ui, u0 + 2:u0 + 2 + L], pxi[:, o:o + L], cgs[:, o:o + L], ALU.mult, [r_pxi, r_cgs], [r_ut[ui]])
                    cp("pool", uh[:, slot, c, :], ut[:, ui, u0 + L:u0 + L + 2], [r_ut[ui]], [r_uh])
                    act(y1[:, o:o + L], ut[:, ui, u0 + 2:u0 + 2 + L], AF.Identity, [r_ut[ui], r_const], [r_y1],
                        bias=cwt[:, c, 3:4], scale=cwt[:, c, 2:3])
                    stt(y2[:, o:o + L], ut[:, ui, u0 + 1:u0 + 1 + L], cwt[:, c, 1:2], y1[:, o:o + L], ALU.mult, ALU.add,
                        [r_ut[ui], r_y1, r_const], [r_y2])
                    stt(y1[:, o:o + L], ut[:, ui, u0:u0 + L], cwt[:, c, 0:1], y2[:, o:o + L], ALU.mult, ALU.add,
                        [r_ut[ui], r_y2, r_const], [r_y1])
                tt("dve", bgc[:, c, 0:T], y1[:, 0:T], pbg[:, 0:T], ALU.mult, [r_y1, r_pbg], [r_bgc] + r_actT)
            w_release(3)

        def stage_D(P):
            tr.tag = f"{P.kind}{P.seq}{P.st}.D"
            T = P.T
            woc = None
            for m in range(8):
                wor, r_wor = w_next(("wor", m))
                if m % 2 == 0:
                    woc, r_woc = w_next(("woc", m // 2))
                wgt, r_wgt = w_next(("gate", m))
                pgr, r_pgr = psum()
                pgc, r_pgc = psum()
                for kc in range(8):
                    mm(pgr[:, 0:T], wgt[:, kc * 256: kc * 256 + 128], hT[:, kc, 0:T], kc == 0, kc == 7, [r_wgt, r_hT], [r_pgr], kc == 7)
                for kc in range(8):
                    mm(pgc[:, 0:T], wgt[:, kc * 256 + 128: kc * 256 + 256], hT[:, kc, 0:T], kc == 0, kc == 7, [r_wgt, r_hT], [r_pgc], kc == 7)
                sr, r_sr = tmp32()
                sc, r_sc = tmp32()
                act(sr[:, 0:T], pgr[:, 0:T], AF.Sigmoid, [r_pgr], [r_sr])
                act(sc[:, 0:T], pgc[:, 0:T], AF.Sigmoid, [r_pgc], [r_sc])
                pyc, r_pyc = psum()
                mi = m % 2
                for kc in range(8):
                    mm(pyc[:, 0:T], woc[:, kc * 256 + mi * 128: kc * 256 + (mi + 1) * 128], bgc[:, kc, 0:T], kc == 0, kc == 7,
                       [r_woc, r_bgc], [r_pyc], kc == 7)
                pyr, r_pyr = psum()
                for kc in range(16):
                    mm(pyr[:, 0:T], wor[:, kc * 128:(kc + 1) * 128], gatedT[:, kc, 0:T], kc == 0, kc == 15, [r_wor, r_gated], [r_pyr], kc == 15)
                tt("dve", sc[:, 0:T], sc[:, 0:T], pyc[:, 0:T], ALU.mult, [r_sc, r_pyc], [r_sc])
                tt("dve", sr[:, 0:T], sr[:, 0:T], pyr[:, 0:T], ALU.mult, [r_sr, r_pyr], [r_sr])
                tt("pool", mergedT[:, m, 0:T], sr[:, 0:T], sc[:, 0:T], ALU.add, [r_sr, r_sc], [r_merged, r_v[0], r_v[1]])
                w_release(4 if m % 2 == 1 else 1)

        def stage_E(P):
            tr.tag = f"{P.kind}{P.seq}{P.st}.E"
            par = P.par
            pans = [[w_next(("wout", ch, half)) for half in range(2)] for ch in range(2)]
            for bi, (tok0, nb) in enumerate(P.blocks):
                rx = r_xa[par][bi]
                for ch in range(2):
                    pa, r_pa = psum()
                    for kc in range(8):
                        wp_, r_wp = pans[ch][kc // 4]
                        mm(pa[:nb, :], mergedT[:, kc, tok0:tok0 + nb], wp_[:, (kc % 4) * 512:(kc % 4 + 1) * 512],
                           kc == 0, False, [r_wp, r_merged], [r_pa], False)
                    mm(pa[:nb, :], ones[0:1, 0:nb], browb[0:1, ch * 512:(ch + 1) * 512], False, True, [r_ones, r_browb], [r_pa], True)
                    t_, r_t = tmp32()
                    tt("dve", t_[:nb, :], pa[:nb, :], gtc[:nb, 0, ch * 512:(ch + 1) * 512], ALU.mult, [r_pa, r_gtc], [r_t])
                    tt("dve", xa[:nb, par, bi, ch * 512:(ch + 1) * 512], xa[:nb, par, bi, ch * 512:(ch + 1) * 512], t_[:nb, :], ALU.add,
                       [rx, r_t], [rx])
                for i in range(2):
                    tr.emit("dve", (lambda i, bi, nb: (lambda v: v.bn_stats(out=st6b[:nb, bi, i, :], in_=xa[:nb, par, bi, i * 512:(i + 1) * 512])))(i, bi, nb),
                            [rx, r_statB], [r_statB])
                tr.emit("dve", (lambda bi, nb: (lambda v: v.bn_aggr(out=mv4b[:nb, bi, :], in_=st6b[:nb, bi, :, :].rearrange("p a b -> p (a b)"))))(bi, nb),
                        [r_statB], [r_statB])
            w_release(4)
            k = len(P.blocks)
            nbm = P.blocks[0][1]
            rsqrt_multi(nbm, k, mv4b[:nbm, 0:k, 0], mv4b[:nbm, 0:k, 1], LN_EPS, rsB, [r_statB], r_rsB)
            for bi, (tok0, nb) in enumerate(P.blocks):
                xab = xa[:nb, par, bi, :]
                rx = r_xa[par][bi]
                act(xnb[:nb, :], xab, AF.Identity, [rx, r_rsB], [r_xnb], bias=rsB[:nb, bi, 1:2], scale=rsB[:nb, bi, 0:1])
                ln_part2(P, bi, tok0, nb, 1)
                act(xab, xab, AF.Identity, [rx, r_rsB], [rx], bias=rsB[:nb, bi, 1:2], scale=rsB[:nb, bi, 0:1])
                tt("pool", xab, xab, ptm[:nb, 2, :], ALU.mult, [rx, r_ptm], [rx])
                tt("pool", xab, xab, ptm[:nb, 3, :], ALU.add, [rx, r_ptm], [rx])

        def ln_stats_multi(P, blocks, st6x, mv4x, r_statX, load_src):
            par = P.par
            for bi, (tok0, nb) in enumerate(blocks):
                rx = r_xa[par][bi]
                if load_src is not None:
                    dma("sp", f"xld{par}{bi}", xa[:nb, par, bi, :], load_src(tok0, nb), [], [rx])
                for i in range(2):
                    tr.emit("dve", (lambda i, bi, nb: (lambda v: v.bn_stats(out=st6x[:nb, bi, i, :], in_=xa[:nb, par, bi, i * 512:(i + 1) * 512])))(i, bi, nb),
                            [rx, r_statX], [r_statX])
                tr.emit("dve", (lambda bi, nb: (lambda v: v.bn_aggr(out=mv4x[:nb, bi, :], in_=st6x[:nb, bi, :, :].rearrange("p a b -> p (a b)"))))(bi, nb),
                        [r_statX], [r_statX])

        def make_astep(Pn):
            ablocks = list(enumerate(Pn.blocks)) if Pn is not None else []
            stA = {"i": 0, "phase": 0}

            def a_step():
                if Pn is None:
                    return
                tr.tag = f"{Pn.kind}{Pn.seq}{Pn.st}.A"
                par = Pn.par
                if stA["phase"] == 0:
                    stA["phase"] = 1
                    dma("sp", "rope", rope[:, :, 0:Pn.T], Pn.rope_src, [], [r_rope])
                    ln_stats_multi(Pn, Pn.blocks, st6a, mv4a, r_statA,
                                   lambda tok0, nb: xp[Pn.seq, Pn.st * 512 + tok0: Pn.st * 512 + tok0 + nb, :])
                    k = len(Pn.blocks)
                    rsqrt_multi(128, k, mv4a[:, 0:k, 0], mv4a[:, 0:k, 1], LN_EPS, rsA, [r_statA], r_rsA)
                    return
                if stA["i"] < len(ablocks):
                    bi, (tok0, nb) = ablocks[stA["i"]]
                    stA["i"] += 1
                    xab = xa[:nb, par, bi, :]
                    rx = r_xa[par][bi]
                    act(xnb[:nb, :], xab, AF.Identity, [rx, r_rsA], [r_xnb], bias=rsA[:nb, bi, 1:2], scale=rsA[:nb, bi, 0:1])
                    act(xab, xab, AF.Identity, [rx, r_rsA], [rx], bias=rsA[:nb, bi, 1:2], scale=rsA[:nb, bi, 0:1])
                    tt("pool", xab, xab, ptm[:nb, 0, :], ALU.mult, [rx, r_ptm], [rx])
                    tt("pool", xab, xab, ptm[:nb, 1, :], ALU.add, [rx, r_ptm], [rx])
                    ln_part2(Pn, bi, tok0, nb, 0)

            def a_done():
                return Pn is None or stA["i"] >= len(ablocks)
            return a_step, a_done

        def stage_FG(P, Pn):
            a_step, a_done = make_astep(Pn)
            par = P.par
            T = P.T
            accs = {}

            def g_panel(ch, pi):
                tr.tag = f"{P.kind}{P.seq}{P.st}.G"
                if ch not in accs:
                    accs[ch] = [psum(hold=True) for _ in P.blocks]
                wp_, r_wp = w_next(("down", ch, pi))
                kcs = list(range(pi * 4, min(pi * 4 + 4, NFF)))
                for bi, (tok0, nb) in enumerate(P.blocks):
                    pa, r_pa = accs[ch][bi]
                    for kl, kc in enumerate(kcs):
                        mm(pa[:nb, :], actT[:, kc, tok0:tok0 + nb], wp_[:, kl * 512:(kl + 1) * 512],
                           kc == 0, False, [r_wp, r_actT[kc]], [r_pa],
                           (pi < 5 and bi == len(P.blocks) - 1 and kl == len(kcs) - 1))
                    if pi == 5:
                        mm(pa[:nb, :], ones[0:1, 0:nb], browb[0:1, D + ch * 512: D + (ch + 1) * 512],
                           False, True, [r_ones, r_browb], [r_pa], True)
                w_release()

            def g_evac(ch):
                for bi, (tok0, nb) in enumerate(P.blocks):
                    pa, r_pa = accs[ch][bi]
                    rx = r_xa[par][bi]
                    t_, r_t = tmp32()
                    tt("dve", t_[:nb, :], pa[:nb, :], gtc[:nb, 1, ch * 512:(ch + 1) * 512], ALU.mult, [r_pa, r_gtc], [r_t])
                    tt("pool", xa[:nb, par, bi, ch * 512:(ch + 1) * 512], xa[:nb, par, bi, ch * 512:(ch + 1) * 512], t_[:nb, :], ALU.add,
                       [rx, r_t], [rx])
                    psum_unhold(r_pa)

            a_step_real = a_step
            if P.idx == 0:
                a_step = lambda: None
            for f in range(NFF):
                if f == 16:
                    a_step()
                tr.tag = f"{P.kind}{P.seq}{P.st}.F"
                wf, r_wf = w_next(("up", f))
                pa_, r_pa = psum()
                pg_, r_pg = psum()
                for kc in range(8):
                    mm(pa_[:, 0:T], wf[:, kc * 256: kc * 256 + 128], hT[:, kc, 0:T], kc == 0, kc == 7, [r_wf, r_hT], [r_pa], kc == 7)
                for kc in range(8):
                    mm(pg_[:, 0:T], wf[:, kc * 256 + 128: kc * 256 + 256], hT[:, kc, 0:T], kc == 0, kc == 7, [r_wf, r_hT], [r_pg], kc == 7)
                w_release()
                ui = f % 2
                y1, r_y1 = tmp32()
                y2, r_y2 = tmp32()
                for si, (o, L, slot) in enumerate(P.segs):
                    u0 = si * (L + 2)
                    cp("pool", ut[:, ui, u0:u0 + 2], ah[:, slot, f, :], [r_ah], [r_ut[ui]])
                    cp("act", ut[:, ui, u0 + 2:u0 + 2 + L], pa_[:, o:o + L], [r_pa], [r_ut[ui]])
                    cp("pool", ah[:, slot, f, :], ut[:, ui, u0 + L:u0 + L + 2], [r_ut[ui]], [r_ah])
                    act(y1[:, o:o + L], pa_[:, o:o + L], AF.Identity, [r_pa, r_const], [r_y1], bias=fwt[:, f, 3:4], scale=fwt[:, f, 2:3])
                    stt(y2[:, o:o + L], ut[:, ui, u0 + 1:u0 + 1 + L], fwt[:, f, 1:2], y1[:, o:o + L], ALU.mult, ALU.add,
                        [r_ut[ui], r_y1, r_const], [r_y2])
                    stt(y1[:, o:o + L], ut[:, ui, u0:u0 + L], fwt[:, f, 0:1], y2[:, o:o + L], ALU.mult, ALU.add,
                        [r_ut[ui], r_y2, r_const], [r_y1])
                act(y2[:, 0:T], y1[:, 0:T], AF.Gelu, [r_y1], [r_y2])
                tt("dve", actT[:, f, 0:T], y2[:, 0:T], pg_[:, 0:T], ALU.mult, [r_y2, r_pg], [r_actT[f], r_gated, r_bgc])
                if f >= 20:
                    g_panel(0, f - 20)
            g_panel(0, 2)
            g_panel(0, 3)
            g_panel(0, 4)
            a_step()
            g_panel(0, 5)
            g_evac(0)
            g_panel(1, 0)
            a_step()
            g_panel(1, 1)
            g_panel(1, 2)
            a_step()
            g_panel(1, 3)
            g_panel(1, 4)
            a_step()
            g_panel(1, 5)
            g_evac(1)
            while not a_done():
                a_step_real()
            deferred.append(lambda: ln2_tail(P))

        def ln2_tail(P):
            par = P.par
            tr.tag = f"{P.kind}{P.seq}{P.st}.G"
            ln_stats_multi(P, P.blocks, st6b, mv4b, r_statB, None)
            k = len(P.blocks)
            nbm = P.blocks[0][1]
            rsqrt_multi(nbm, k, mv4b[:nbm, 0:k, 0], mv4b[:nbm, 0:k, 1], LN_EPS, rsB, [r_statB], r_rsB)
            for bi, (tok0, nb) in enumerate(P.blocks):
                xab = xa[:nb, par, bi, :]
                rx = r_xa[par][bi]
                act(xab, xab, AF.Identity, [rx, r_rsB], [rx], bias=rsB[:nb, bi, 1:2], scale=rsB[:nb, bi, 0:1])
                tt("dve", xab, xab, ptm[:nb, 4, :], ALU.mult, [rx, r_ptm], [rx])
                tt("pool", xab, xab, ptm[:nb, 5, :], ALU.add, [rx, r_ptm], [rx])
                r_o = Res("y_o")
                if P.kind == "prompt":
                    dst = yp[P.seq, P.st * 512 + tok0: P.st * 512 + tok0 + nb, :]
                else:
                    dst = ys
                dma("sp", f"yst{par}{bi}", dst, xab, [rx], [r_o])
                out_res.append(r_o)
            if P.last:
                for (o, L, slot) in P.segs:
                    r_o = Res("cs_o")
                    dma("sp", f"stcs{slot}", cs_o[slot], uh[:, slot, :, :], [r_uh], [r_o])
                    out_res.append(r_o)
                    r_o = Res("fs_o")
                    dma("sp", f"stfs{slot}", fs_o[slot], ah[:, slot, :, :], [r_ah], [r_o])
                    out_res.append(r_o)

        deferred = []

        def run_deferred():
            while deferred:
                deferred.pop(0)()

        def stage_BC(P):
            order = [("P1", 0), ("P2", 0), ("P1", 1), ("R1", 0), ("P2", 1), ("R2a", 0), ("P1", 2), ("R2b", 0), ("R1", 1), ("P2", 2),
                     ("R2a", 1), ("P1", 3), ("R2b", 1), ("R1", 2), ("P2", 3), ("R2a", 2), ("C", 0), ("R2b", 2), ("R1", 3), ("C", 1),
                     ("R2a", 3), ("C", 2), ("R2b", 3), ("C", 3)]
            fn = {"P1": task_P1, "P2": task_P2, "R1": task_R1, "R2a": task_R2a, "R2b": task_R2b, "C": task_C}
            for (k, a) in order:
                fn[k](P, a)
                if (k, a) == ("P2", 0):
                    run_deferred()

        passes = [make_pass(0, "sample", 0, 0)]
        for seq in range(2):
            for st in range(4):
                passes.append(make_pass(len(passes), "prompt", seq, st))
        stage_A(passes[0])
        for i, P in enumerate(passes):
            stage_pre(P)
            stage_BC(P)
            stage_D(P)
            stage_E(P)
            stage_FG(P, passes[i + 1] if i + 1 < len(passes) else None)
        run_deferred()

        assert wst["next_use"] == TOTAL, (wst["next_use"], TOTAL)
        tr.wait_all("sp", out_res)
        tr.run(es)
    global _LAST_TRACKER
    _LAST_TRACKER = tr
    return nc, dbg_outs, pan_order


def _lhs_panel(W, cols):
    sub = W[:, cols]
    return np.ascontiguousarray(sub.reshape(8, 128, 256).transpose(1, 0, 2).reshape(128, 2048))


def _rhs_panel(W, kcs, cols):
    out = np.zeros((128, 4, 512), np.float32)
    for i, kc in enumerate(kcs):
        out[:, i, :] = W[kc * 128:(kc + 1) * 128, cols]
    return out.reshape(128, 2048)


def _build_wpan(order, w_in, w_o_ret, w_o_conv, w_out, w_up, w_down):
    a128 = np.arange(128)
    a256 = np.arange(256)
    a512 = np.arange(512)
    pans = []
    for key in order:
        k = key[0]
        if k in ("q", "k"):
            h = key[1]
            perm = np.concatenate([h * 256 + 2 * a128, h * 256 + 2 * a128 + 1])
            pans.append(_lhs_panel(w_in, perm + (0 if k == "q" else 1024)))
        elif k == "v":
            h, half = key[1], key[2]
            pans.append(_rhs_panel(w_in, list(range(4 * half, 4 * half + 4)), 2048 + h * 512 + a512))
        elif k == "g":
            h, j2 = key[1], key[2]
            pans.append(_lhs_panel(w_in, 4096 + h * 512 + j2 * 256 + a256))
        elif k == "cx":
            c = key[1]
            pans.append(_lhs_panel(w_in, np.concatenate([7168 + c * 128 + a128, 8192 + c * 128 + a128])))
        elif k == "bg":
            pans.append(_lhs_panel(w_in, 6144 + key[1] * 256 + a256))
        elif k == "wor":
            m = key[1]
            wor = w_o_ret[:, m * 128:(m + 1) * 128].reshape(16, 128, 128).transpose(1, 0, 2).reshape(128, 2048)
            pans.append(np.ascontiguousarray(wor))
        elif k == "woc":
            pans.append(_lhs_panel(w_o_conv, key[1] * 256 + a256))
        elif k == "gate":
            m = key[1]
            pans.append(_lhs_panel(w_in, np.concatenate([9216 + m * 128 + a128, 10240 + m * 128 + a128])))
        elif k == "wout":
            ch, half = key[1], key[2]
            pans.append(_rhs_panel(w_out, list(range(4 * half, 4 * half + 4)), ch * 512 + a512))
        elif k == "up":
            f = key[1]
            pans.append(_lhs_panel(w_up, np.concatenate([f * 128 + a128, DFF + f * 128 + a128])))
        elif k == "down":
            ch, i = key[1], key[2]
            pans.append(_rhs_panel(w_down, list(range(4 * i, min(4 * i + 4, NFF))), ch * 512 + a512))
        else:
            raise KeyError(key)
    assert len(pans) == NPAN
    return np.stack(pans).astype(np.float32)


def _consts():
    half = 128
    inv_freq = (np.float32(10000.0) ** (-(np.arange(half, dtype=np.float32) / np.float32(half)))).astype(np.float32)

    def tab(pos):
        ang = (pos.astype(np.float32)[None, :] * inv_freq[:, None]).astype(np.float32)
        a64 = ang.astype(np.float64)
        return np.stack([np.cos(a64), np.sin(a64)], axis=1).astype(np.float32)

    rope_p = tab(np.arange(SEQ))
    ps = PAST + np.arange(DEC_SEQ)
    rope_s = tab(np.concatenate([ps, ps]))
    g = np.array(GAMMA, np.float64)
    j = np.arange(128)
    maskT = np.zeros((128, H, 128), np.float64)
    offs = np.zeros((128, H, 4), np.float64)
    epsv = np.zeros((128, 4, H), np.float64)
    kdec = np.zeros((128, 2, 4, H), np.float64)
    for h in range(H):
        for jb in range(4):
            col = g[h] ** (-(128.0 * jb + j + 1.0)) / 16.0
            if jb == 0:
                maskT[:, h, :] = col[:, None] * (j[None, :] >= j[:, None])
            offs[:, h, jb] = col
            epsv[:, jb, h] = LN_EPS * g[h] ** (-2.0 * (128.0 * jb + j + 1.0))
            kdec[:, 0, jb, h] = g[h] ** (511.0 - 128.0 * jb - j) / 16.0
        kdec[:32, 1, 0, h] = g[h] ** (31.0 - j[:32]) / 16.0
    return dict(rope_p=rope_p, rope_s=rope_s, maskT=maskT.astype(ml_dtypes.bfloat16),
                epsv=epsv.astype(np.float32), kdec=kdec.astype(np.float32), offs=offs.astype(np.float32),
                identb=np.eye(128).astype(ml_dtypes.bfloat16), identf=np.eye(128, dtype=np.float32))


_CACHE = {}


def kernel(x_prompt, x_sample, c_prompt, c_sample, state_retention, state_shortconv, state_ffn_conv,
           ln_in_g, ln_in_b, w_mod, b_mod, w_in, w_o_ret, conv_w, conv_b, w_o_conv, w_out, b_out,
           ln1_g, ln1_b, w_up, ffn_conv_w, ffn_conv_b, w_down, b_down, ln2_g, ln2_b, _debug=()):
    f = lambda a: np.asarray(a, dtype=np.float32)
    x_prompt, x_sample, c_prompt, c_sample = f(x_prompt), f(x_sample), f(c_prompt), f(c_sample)
    state_retention, state_shortconv, state_ffn_conv = f(state_retention), f(state_shortconv), f(state_ffn_conv)
    key = tuple(_debug)
    if key not in _CACHE:
        _CACHE[key] = build_program(debug=_debug)
    nc, dbg_outs, pan_order = _CACHE[key]

    shared = _consts()
    shared["wpan"] = _build_wpan(pan_order, f(w_in)[0], f(w_o_ret)[0], f(w_o_conv)[0], f(w_out)[0], f(w_up)[0], f(w_down)[0])
    wm = f(w_mod)[0]
    shared["wmod"] = np.ascontiguousarray(wm.reshape(8, 128, 48, 128).transpose(2, 1, 0, 3).reshape(48, 128, 1024))
    shared["bmod"] = np.ascontiguousarray(f(b_mod)[0].reshape(48, 128).T)
    vecs = np.stack([f(ln_in_g), f(ln_in_b), f(ln1_g)[0], f(ln1_b)[0], f(ln2_g)[0], f(ln2_b)[0]])
    shared["vtm"] = np.ascontiguousarray(np.broadcast_to(vecs[None], (128, 6, D)))
    shared["vfm"] = np.ascontiguousarray(vecs[:4].reshape(4, 8, 128).transpose(2, 0, 1))
    cwa = np.concatenate([f(conv_w)[0], f(conv_b)], axis=0)
    shared["cw"] = np.ascontiguousarray(cwa.reshape(4, 8, 128).transpose(2, 1, 0))
    fwa = np.concatenate([f(ffn_conv_w)[0], f(ffn_conv_b)], axis=0)
    shared["fw"] = np.ascontiguousarray(fwa.reshape(4, NFF, 128).transpose(2, 1, 0))
    shared["brow"] = np.concatenate([f(b_out)[0], f(b_down)[0]])[None, :].copy()

    in_maps = []
    for i in range(8):
        m = dict(shared)
        m["xp"] = np.ascontiguousarray(x_prompt[2 * i:2 * i + 2])
        m["xs"] = np.ascontiguousarray(x_sample[2 * i:2 * i + 2].reshape(64, D))
        c4 = np.concatenate([c_prompt[2 * i:2 * i + 2], c_sample[2 * i:2 * i + 2]], axis=0)
        m["cT"] = np.ascontiguousarray(c4.reshape(4, 8, 128).transpose(2, 1, 0))
        sr = state_retention[0, 2 * i:2 * i + 2]
        m["sret"] = np.ascontiguousarray(sr.reshape(2, H, 128, 2, DV).transpose(0, 1, 3, 2, 4))
        sc = state_shortconv[0, 2 * i:2 * i + 2]
        m["sconv"] = np.ascontiguousarray(sc.reshape(2, 2, 8, 128).transpose(0, 3, 2, 1))
        sf = state_ffn_conv[0, 2 * i:2 * i + 2]
        m["sffn"] = np.ascontiguousarray(sf.reshape(2, 2, NFF, 128).transpose(0, 3, 2, 1))
        in_maps.append(m)

    res = run_bass_kernel_spmd(nc, in_maps, core_ids=list(range(8)))
    R = res.results
    y_prompt = np.concatenate([r["yp"] for r in R], axis=0)
    y_sample = np.concatenate([r["ys"].reshape(2, DEC_SEQ, D) for r in R], axis=0)

    def ret_state(lo):
        outs = []
        for r in R:
            a = r["rs_o"][lo:lo + 2]
            outs.append(a.transpose(0, 1, 3, 2, 4).reshape(2, H, DK, DV))
        return np.concatenate(outs, axis=0)[None]

    def hist_state(name, lo, nch):
        outs = []
        for r in R:
            a = r[name][lo:lo + 2]
            outs.append(a.transpose(0, 3, 2, 1).reshape(2, 2, nch * 128))
        return np.concatenate(outs, axis=0)[None]

    outs = (y_prompt, y_sample,
            ret_state(0), hist_state("cs_o", 0, 8), hist_state("fs_o", 0, NFF),
            ret_state(2), hist_state("cs_o", 2, 8), hist_state("fs_o", 2, NFF))
    outs = tuple(np.ascontiguousarray(o, dtype=np.float32) for o in outs)
    if _debug:
        return outs, [{k: r["dbg_" + k] for k in dbg_outs} for r in R]
    return outs
```
